# Optimizing a Trainium2 kernel written in Bass

```python
import jax, jax.numpy as jnp
from jax import lax
import numpy as np

D_MODEL = 2048
BATCH = 4
SEQ = 4096
DEPTH = 4

N_HEADS = 16
HEAD_DIM = 128
D_LIN = N_HEADS * HEAD_DIM
CONV_QKV = 4
CHUNK = 64
POOL_WINDOWS = (2, 4, 8, 16)
N_POOL_GROUPS = len(POOL_WINDOWS)
D_POOL = D_MODEL
POOL_GROUP_DIM = D_POOL // N_POOL_GROUPS
D_FF = 5632
CONV_FFN = 3
EPS = 1e-6
D_IN_PROJ = 4 * D_LIN + 2 * N_HEADS + D_POOL + 2 * D_MODEL

kernel_name = "hybrid_gdn_pool_convglu_trunk"


def rmsnorm(x, w):
    xf = x.astype(jnp.float32)
    y = xf * lax.rsqrt(jnp.mean(xf * xf, axis=-1, keepdims=True) + EPS)
    return (y * w.astype(jnp.float32)).astype(x.dtype)


def l2norm(x):
    xf = x.astype(jnp.float32)
    return xf * lax.rsqrt(jnp.sum(xf * xf, axis=-1, keepdims=True) + EPS)


def causal_dwconv(x, w):
    width = w.shape[0]
    seq = x.shape[1]
    xp = jnp.pad(x, ((0, 0), (width - 1, 0), (0, 0)))
    y = xp[:, 0:seq] * w[0]
    for j in range(1, width):
        y = y + xp[:, j:j + seq] * w[j]
    return y


def gated_delta_rule(q, k, v, g, beta):
    f32 = jnp.float32
    q, k, v, g, beta = (t.astype(f32) for t in (q, k, v, g, beta))
    bsz, seq, nh, dk = q.shape
    dv = v.shape[-1]
    pad = (-seq) % CHUNK
    if pad:
        q, k, v = (jnp.pad(t, ((0, 0), (0, pad), (0, 0), (0, 0))) for t in (q, k, v))
        g, beta = (jnp.pad(t, ((0, 0), (0, pad), (0, 0))) for t in (g, beta))
    n_chunks = (seq + pad) // CHUNK

    def to_chunks(t):
        return t.reshape(bsz, n_chunks, CHUNK, nh, t.shape[-1]).transpose(0, 3, 1, 2, 4)

    q, k, v = to_chunks(q), to_chunks(k), to_chunks(v)
    g = g.reshape(bsz, n_chunks, CHUNK, nh).transpose(0, 3, 1, 2)
    beta = beta.reshape(bsz, n_chunks, CHUNK, nh).transpose(0, 3, 1, 2)
    g = jnp.cumsum(g, axis=-1)

    tril = jnp.tril(jnp.ones((CHUNK, CHUNK), dtype=bool))
    strict = jnp.tril(jnp.ones((CHUNK, CHUNK), dtype=bool), k=-1)
    diff = g[..., :, None] - g[..., None, :]
    decay = jnp.exp(jnp.where(tril, diff, -jnp.inf))

    k_beta = k * beta[..., None]
    v_beta = v * beta[..., None]
    lmat = jnp.where(strict, jnp.einsum('bhnid,bhnjd->bhnij', k_beta, k) * decay, 0.0)
    amat = lmat + jnp.eye(CHUNK, dtype=f32)
    rhs = jnp.concatenate([v_beta, k_beta * jnp.exp(g)[..., None]], axis=-1)
    sol = lax.linalg.triangular_solve(amat, rhs, left_side=True, lower=True, unit_diagonal=True)
    u_val = sol[..., :dv]
    w_cum = sol[..., dv:]

    attn_intra = jnp.einsum('bhnid,bhnjd->bhnij', q, k) * decay
    q_decay = q * jnp.exp(g)[..., None]
    k_to_end = k * jnp.exp(g[..., -1:] - g)[..., None]
    chunk_decay = jnp.exp(g[..., -1])

    xs = tuple(jnp.moveaxis(t, 2, 0) for t in (u_val, w_cum, q_decay, attn_intra, k_to_end, chunk_decay))

    def step(state, inp):
        u_c, w_c, qd_c, at_c, ke_c, dec_c = inp
        v_new = u_c - jnp.einsum('bhcd,bhdv->bhcv', w_c, state)
        o_c = jnp.einsum('bhcd,bhdv->bhcv', qd_c, state) + jnp.einsum('bhij,bhjv->bhiv', at_c, v_new)
        state = state * dec_c[..., None, None] + jnp.einsum('bhcd,bhcv->bhdv', ke_c, v_new)
        return state, o_c

    state0 = jnp.zeros((bsz, nh, dk, dv), dtype=f32)
    _, o = lax.scan(step, state0, xs)
    o = o.transpose(1, 0, 3, 2, 4).reshape(bsz, n_chunks * CHUNK, nh, dv)
    return o[:, :seq]


def causal_multiscale_pool(u):
    seq = u.shape[1]
    uf = u.astype(jnp.float32)
    cs = jnp.pad(jnp.cumsum(uf, axis=1), ((0, 0), (1, 0), (0, 0), (0, 0)))
    t = jnp.arange(seq)
    outs = []
    for gi, win in enumerate(POOL_WINDOWS):
        hi = cs[:, 1:, gi]
        lo = cs[:, jnp.maximum(t + 1 - win, 0), gi]
        cnt = jnp.minimum(t + 1, win).astype(jnp.float32)
        outs.append((hi - lo) / cnt[None, :, None])
    pooled = jnp.stack(outs, axis=2)
    return (pooled - uf).astype(u.dtype)


def setup_inputs(seed: int = 0) -> dict:
    key = jax.random.key(seed)
    ks = jax.random.split(key, 20)
    f32 = jnp.float32

    def nrm(k, shape, scale):
        return jax.random.normal(k, shape, f32) * scale

    def gain(k, shape):
        return 1.0 + 0.02 * jax.random.normal(k, shape, f32)

    x = jax.random.normal(ks[0], (BATCH, SEQ, D_MODEL), f32)
    a_init = jax.random.uniform(ks[4], (DEPTH, N_HEADS), f32, 1.0, 16.0)
    dt = jnp.exp(jax.random.uniform(ks[5], (DEPTH, N_HEADS), f32, np.log(1e-3), np.log(1e-1)))
    dt_bias = dt + jnp.log(-jnp.expm1(-dt))
    return {
        "x": x,
        "norm_mix_w": gain(ks[1], (DEPTH, D_MODEL)),
        "w_in": nrm(ks[2], (DEPTH, D_MODEL, D_IN_PROJ), D_MODEL ** -0.5),
        "conv_qkv_w": nrm(ks[3], (DEPTH, CONV_QKV, 3 * D_LIN), CONV_QKV ** -0.5),
        "a_log": jnp.log(a_init),
        "dt_bias": dt_bias,
        "gdn_norm_w": gain(ks[6], (DEPTH, HEAD_DIM)),
        "pool_w": nrm(ks[7], (DEPTH, N_POOL_GROUPS, POOL_GROUP_DIM, POOL_GROUP_DIM), POOL_GROUP_DIM ** -0.5),
        "pool_scale": gain(ks[8], (DEPTH, D_POOL)),
        "w_out": nrm(ks[9], (DEPTH, D_MODEL, D_MODEL), D_MODEL ** -0.5),
        "norm_ffn_w": gain(ks[10], (DEPTH, D_MODEL)),
        "w_up": nrm(ks[11], (DEPTH, D_MODEL, 2 * D_FF), D_MODEL ** -0.5),
        "conv_ffn_w": nrm(ks[12], (DEPTH, CONV_FFN, D_FF), CONV_FFN ** -0.5),
        "conv_ffn_b": nrm(ks[13], (DEPTH, D_FF), 0.02),
        "w_down": nrm(ks[14], (DEPTH, D_FF, D_MODEL), D_FF ** -0.5),
        "norm_final_w": gain(ks[15], (D_MODEL,)),
    }


def reference(x, norm_mix_w, w_in, conv_qkv_w, a_log, dt_bias, gdn_norm_w, pool_w, pool_scale,
              w_out, norm_ffn_w, w_up, conv_ffn_w, conv_ffn_b, w_down, norm_final_w):
    bsz, seq, _ = x.shape
    splits = np.cumsum([D_LIN, D_LIN, D_LIN, D_LIN, N_HEADS, N_HEADS, D_POOL, D_MODEL]).tolist()
    for l in range(DEPTH):
        h = rmsnorm(x, norm_mix_w[l])
        proj = jnp.einsum('bsd,de->bse', h, w_in[l])
        q, k, v, z, b_raw, a_raw, p_in, g_a, g_b = jnp.split(proj, splits, axis=-1)

        qkv = jax.nn.silu(causal_dwconv(jnp.concatenate([q, k, v], axis=-1), conv_qkv_w[l]))
        q, k, v = jnp.split(qkv, 3, axis=-1)
        q = l2norm(q.reshape(bsz, seq, N_HEADS, HEAD_DIM)) * (HEAD_DIM ** -0.5)
        k = l2norm(k.reshape(bsz, seq, N_HEADS, HEAD_DIM))
        v = v.reshape(bsz, seq, N_HEADS, HEAD_DIM)
        beta = jax.nn.sigmoid(b_raw.astype(jnp.float32))
        g_log = -jnp.exp(a_log[l].astype(jnp.float32)) * jax.nn.softplus(
            a_raw.astype(jnp.float32) + dt_bias[l].astype(jnp.float32))
        o = gated_delta_rule(q, k, v, g_log, beta).astype(x.dtype)
        o = rmsnorm(o, gdn_norm_w[l]) * jax.nn.silu(z.reshape(bsz, seq, N_HEADS, HEAD_DIM))
        y_a = o.reshape(bsz, seq, D_LIN)

        pooled = causal_multiscale_pool(p_in.reshape(bsz, seq, N_POOL_GROUPS, POOL_GROUP_DIM))
        y_b = jnp.einsum('bsgc,gcd->bsgd', pooled, pool_w[l]).reshape(bsz, seq, D_POOL) * pool_scale[l]

        mixed = jax.nn.sigmoid(g_a) * y_a + jax.nn.sigmoid(g_b) * y_b
        x = x + jnp.einsum('bsd,de->bse', mixed, w_out[l])

        h = rmsnorm(x, norm_ffn_w[l])
        gate, up = jnp.split(jnp.einsum('bsd,df->bsf', h, w_up[l]), 2, axis=-1)
        gate = causal_dwconv(gate, conv_ffn_w[l]) + conv_ffn_b[l]
        x = x + jnp.einsum('bsf,fd->bsd', jax.nn.gelu(gate, approximate=False) * up, w_down[l])
    return rmsnorm(x, norm_final_w)
```

```python
import numpy as np
from contextlib import ExitStack
import concourse.bass as bass
import concourse.mybir as mybir
from concourse.bass_utils import run_bass_kernel_spmd

F32 = mybir.dt.float32
BF16 = mybir.dt.bfloat16
AF = mybir.ActivationFunctionType
ALU = mybir.AluOpType

D = 2048
NH = 16
HD = 128
DFF = 5632
NFF = DFF // 128
DIN = 4 * D + 2 * NH + D + 2 * D
EPS = 1e-6
TT = 1024
NCH = TT // 128
KC = D // 128
NQ = 4
FH = NFF // NQ


class Tok:
    __slots__ = ("w", "r", "dsem", "dcnt", "name")

    def __init__(self, name=""):
        self.w = None
        self.r = {}
        self.dsem = None
        self.dcnt = 0
        self.name = name


class V:
    __slots__ = ("ap", "toks")

    def __init__(self, ap, toks):
        self.ap = ap
        self.toks = toks


class Tile:
    def __init__(self, t, tok):
        self.t = t
        self.tok = tok

    def __getitem__(self, idx):
        return V(self.t[idx], [self.tok])

    def re(self, pat, **kw):
        return V(self.t[:].rearrange(pat, **kw), [self.tok])


class FW:
    def __init__(self, nc, es):
        self.nc = nc
        self.es = es
        self.eng = {"pe": nc.tensor, "act": nc.scalar, "dve": nc.vector, "pool": nc.gpsimd, "sp": nc.sync}
        self.sem = {e: es.enter_context(nc.semaphore("s_" + e)) for e in self.eng}
        self.cnt = {e: 0 for e in self.eng}
        self.seen = {e: {} for e in self.eng}
        self.nwait = 0
        self.nins = 0
        self.ntile = 0

    def sb(self, name, shape, dt, dma=False):
        t = self.es.enter_context(self.nc.sbuf_tensor("sb_" + name, list(shape), dt))
        tok = self.dtok(name) if dma else Tok(name)
        return Tile(t, tok)

    def dtok(self, name):
        t = Tok(name)
        t.dsem = self.es.enter_context(self.nc.semaphore("d_" + name))
        return t

    def _deps(self, e, reads, writes):
        deps = []
        for t in reads:
            if t.w is not None:
                deps.append(t.w)
        for t in writes:
            if t.w is not None:
                deps.append(t.w)
            deps.extend(t.r.values())
        mysem = self.sem[e]
        seen = self.seen[e]
        for (sem, val) in deps:
            if e == "pe" and sem is mysem:
                continue
            if seen.get(sem, 0) >= val:
                continue
            self.eng[e].wait_ge(sem, val)
            seen[sem] = val
            self.nwait += 1

    def op(self, e, build, reads=(), writes=()):
        self._deps(e, reads, writes)
        ins = build(self.eng[e])
        self.cnt[e] += 1
        self.nins += 1
        ins.then_inc(self.sem[e], 1)
        me = (self.sem[e], self.cnt[e])
        s = self.sem[e]
        for t in reads:
            t.r[s] = me
        for t in writes:
            t.w = me
            t.r = {}
        return ins

    def dma(self, q, out, in_, tok):
        reads, writes = in_.toks, out.toks
        self._deps(q, reads, writes)
        if tok.dcnt > 0 and self.seen[q].get(tok.dsem, 0) < tok.dcnt:
            self.eng[q].wait_ge(tok.dsem, tok.dcnt)
            self.seen[q][tok.dsem] = tok.dcnt
            self.nwait += 1
        ins = self.eng[q].dma_start(out=out.ap, in_=in_.ap)
        tok.dcnt += 16
        ins.then_inc(tok.dsem, 16)
        self.nins += 1
        me = (tok.dsem, tok.dcnt)
        for t in reads:
            t.r[tok.dsem] = me
        for t in writes:
            t.w = me
            t.r = {}
        return ins

    def wait_tok(self, e, tok):
        self._deps(e, [tok], [])

    def mm(self, out, lhsT, rhs, start=True, stop=True):
        return self.op("pe", lambda e: e.matmul(out.ap, lhsT=lhsT.ap, rhs=rhs.ap, start=start, stop=stop),
                       reads=lhsT.toks + rhs.toks, writes=out.toks)

    def tr(self, out, in_, ident):
        return self.op("pe", lambda e: e.transpose(out.ap, in_.ap, ident.ap),
                       reads=in_.toks + ident.toks, writes=out.toks)

    def act(self, out, in_, func, bias=None, scale=None, accum=None, eng="act"):
        kw = {}
        reads = list(in_.toks)
        writes = list(out.toks)
        if bias is not None:
            if isinstance(bias, V):
                kw["bias"] = bias.ap
                reads += bias.toks
            else:
                kw["bias"] = bias
        if scale is not None:
            if isinstance(scale, V):
                kw["scale"] = scale.ap
                reads += scale.toks
            else:
                kw["scale"] = scale
        if accum is not None:
            kw["accum_out"] = accum.ap
            writes += accum.toks
        return self.op(eng, lambda e: e.activation(out=out.ap, in_=in_.ap, func=func, **kw), reads=reads, writes=writes)

    def tt(self, eng, out, in0, in1, op):
        return self.op(eng, lambda e: e.tensor_tensor(out=out.ap, in0=in0.ap, in1=in1.ap, op=op),
                       reads=in0.toks + in1.toks, writes=out.toks)

    def ts(self, eng, out, in0, s1, op0, s2=None, op1=None, accum=None):
        reads = list(in0.toks)
        writes = list(out.toks)
        a1 = s1
        if isinstance(s1, V):
            a1 = s1.ap
            reads += s1.toks
        a2 = s2
        if isinstance(s2, V):
            a2 = s2.ap
            reads += s2.toks
        kw = {}
        if op1 is not None:
            kw["op1"] = op1
        if accum is not None:
            kw["accum_out"] = accum.ap
            writes += accum.toks
        return self.op(eng, lambda e: e.tensor_scalar(out=out.ap, in0=in0.ap, scalar1=a1, scalar2=a2, op0=op0, **kw),
                       reads=reads, writes=writes)

    def stt(self, eng, out, in0, scalar, in1, op0, op1):
        reads = in0.toks + in1.toks
        a = scalar
        if isinstance(scalar, V):
            a = scalar.ap
            reads = reads + scalar.toks
        return self.op(eng, lambda e: e.scalar_tensor_tensor(out=out.ap, in0=in0.ap, scalar=a, in1=in1.ap, op0=op0, op1=op1),
                       reads=reads, writes=out.toks)

    def copy(self, eng, out, in_):
        if eng == "act":
            return self.act(out, in_, AF.Copy)
        return self.op(eng, lambda e: e.tensor_copy(out=out.ap, in_=in_.ap), reads=in_.toks, writes=out.toks)

    def memset(self, eng, out, val):
        return self.op(eng, lambda e: e.memset(out.ap, val), reads=[], writes=out.toks)


class RR:
    def __init__(self, items):
        self.items = items
        self.i = 0

    def get(self):
        it = self.items[self.i % len(self.items)]
        self.i += 1
        return it


O_NMIX = 0
O_NFFN = 16
O_PSC = 32
O_CQKV = 48
O_CFW = O_CQKV + 192
O_CFB = O_CFW + 132
O_GNW = O_CFB + 44
O_ALOG = O_GNW + 1
O_DTB = O_ALOG + 128
NSM = O_DTB + 128
C_ID = 0
C_TRIU = 128
C_SU = 256
C_SL = 384
C_ONE = 512
C_ICNT = 640
C_NFIN = 704
NCONST = 720


def _colmajor(w, ncol_tiles):
    K = w.shape[0]
    return np.ascontiguousarray(w.reshape(K // 128, 128, ncol_tiles, 128).transpose(2, 1, 0, 3)).reshape(
        ncol_tiles, 128, (K // 128) * 128)


def prep_weights(inp, L):
    out = {}
    w_in = inp["w_in"]
    order = []
    for g in range(4):
        for j in range(4 * g, 4 * g + 4):
            order.append(4 * D + 2 * NH + D + D + j * 128)
        for j in range(4 * g, 4 * g + 4):
            order.append(4 * D + 2 * NH + j * 128)
        for h in range(4 * g, 4 * g + 4):
            order.append(0 * D + h * 128)
            order.append(1 * D + h * 128)
            order.append(2 * D + h * 128)
            order.append(3 * D + h * 128)
            order.append(4 * D + 2 * NH + D + h * 128)
    cols = np.concatenate([np.arange(o, o + 128) for o in order])
    win = np.empty((L, len(order), 128, D), np.float32)
    wab = np.empty((L, 128, KC * 32), np.float32)
    wout = np.empty((L, 16, 128, D), np.float32)
    wup = np.empty((L, 2 * NFF, 128, D), np.float32)
    wdn = np.empty((L, NQ, 16, 128, FH * 128), np.float32)
    wpool = np.empty((L, 4, 4, 128, 512), np.float32)
    small = np.zeros((L, 128, NSM), np.float32)
    upcols = np.concatenate([np.concatenate([np.arange(f * 128, f * 128 + 128), np.arange(DFF + f * 128, DFF + f * 128 + 128)])
                             for f in range(NFF)])
    for l in range(L):
        win[l] = _colmajor(w_in[l][:, cols], len(order))
        ab = w_in[l][:, 4 * D:4 * D + 32]
        wab[l] = ab.reshape(KC, 128, 32).transpose(1, 0, 2).reshape(128, KC * 32)
        wout[l] = _colmajor(inp["w_out"][l], 16)
        wup[l] = _colmajor(inp["w_up"][l][:, upcols], 2 * NFF)
        wd = inp["w_down"][l]
        for hh in range(NQ):
            blk = wd[hh * FH * 128:(hh + 1) * FH * 128]
            wdn[l, hh] = _colmajor(blk, 16)
        for g in range(4):
            wpool[l, g] = _colmajor(inp["pool_w"][l, g], 4)
        sm = small[l]
        sm[:, O_NMIX:O_NMIX + 16] = inp["norm_mix_w"][l].reshape(16, 128).T
        sm[:, O_NFFN:O_NFFN + 16] = inp["norm_ffn_w"][l].reshape(16, 128).T
        sm[:, O_PSC:O_PSC + 16] = inp["pool_scale"][l].reshape(16, 128).T
        sm[:, O_CQKV:O_CQKV + 192] = inp["conv_qkv_w"][l].reshape(4, 48, 128).transpose(2, 1, 0).reshape(128, 192)
        sm[:, O_CFW:O_CFW + 132] = inp["conv_ffn_w"][l].reshape(3, NFF, 128).transpose(2, 1, 0).reshape(128, 132)
        sm[:, O_CFB:O_CFB + 44] = inp["conv_ffn_b"][l].reshape(NFF, 128).T
        sm[:, O_GNW] = inp["gdn_norm_w"][l]
        sm[:, O_ALOG:O_ALOG + 128] = np.tile(inp["a_log"][l], NCH)[None, :]
        sm[:, O_DTB:O_DTB + 128] = np.tile(inp["dt_bias"][l], NCH)[None, :]
    out.update(win=win, wab=wab, wout=wout, wup=wup, wdn=wdn, wpool=wpool, small=small)
    c = np.zeros((128, NCONST), np.float32)
    idx = np.arange(128)
    c[:, C_ID:C_ID + 128] = np.eye(128, dtype=np.float32)
    c[:, C_TRIU:C_TRIU + 128] = (idx[:, None] <= idx[None, :])
    c[:, C_SU:C_SU + 128] = (idx[None, :] > idx[:, None])
    c[:, C_SL:C_SL + 128] = (idx[:, None] > idx[None, :])
    c[:, C_ONE:C_ONE + 128] = 1.0
    for g, win_ in enumerate((2, 4, 8, 16)):
        t = np.arange(16)
        c[:, C_ICNT + g * 16:C_ICNT + g * 16 + 16] = (np.float32(1.0) / np.minimum(t + 1, win_).astype(np.float32))[None, :]
    c[:, C_NFIN:C_NFIN + 16] = inp["norm_final_w"].reshape(16, 128).T
    out["consts"] = c
    return out


class StopBuild(Exception):
    pass


def build_program(L, S, dbg=None, stop_at=None):
    assert S % TT == 0
    NT = S // TT
    nc = bass.Bass("TRN2", target_bir_lowering=False)
    dr = lambda name, shape, kind="ExternalInput": nc.dram_tensor(name, list(shape), F32, kind=kind)
    xT_d = dr("xT", [D, S])
    win_d = dr("win", [L, 112, 128, D])
    wab_d = dr("wab", [L, 128, KC * 32])
    wout_d = dr("wout", [L, 16, 128, D])
    wup_d = dr("wup", [L, 2 * NFF, 128, D])
    wdn_d = dr("wdn", [L, NQ, 16, 128, FH * 128])
    wpool_d = dr("wpool", [L, 4, 4, 128, 512])
    small_d = dr("small", [L, 128, NSM])
    consts_d = dr("consts", [128, NCONST])
    yT_d = dr("yT", [D, S], kind="ExternalOutput")
    dbg_list = []

    def checkpoint(name):
        if stop_at == name:
            raise StopBuild()

    with ExitStack() as es:
        fw = FW(nc, es)
        t_win, t_wab, t_wout, t_wup, t_wdn, t_wpool, t_small, t_consts = (Tok() for _ in range(8))
        t_xin = Tok()
        t_y = [[Tok() for _ in range(KC)] for _ in range(NT)]
        st_tok = [fw.dtok("st%d" % i) for i in range(3)]
        st_rr = RR(st_tok)

        cf = fw.sb("cf", [128, NCONST], F32, dma=True)
        cb = fw.sb("cb", [128, 640], BF16, dma=True)
        fw.dma("sp", cf[:, :], V(consts_d.ap()[:, :], [t_consts]), cf.tok)
        fw.dma("pool", cb[:, :], V(consts_d.ap()[:, 0:640], [t_consts]), cb.tok)
        identb = cb[:, C_ID:C_ID + 128]
        triub = cb[:, C_TRIU:C_TRIU + 128]
        sub_ = cb[:, C_SU:C_SU + 128]
        onesb = cb[:, C_ONE:C_ONE + 128]
        triuf = cf[:, C_TRIU:C_TRIU + 128]
        slf = cf[:, C_SL:C_SL + 128]
        onesf = cf[:, C_ONE:C_ONE + 128]
        identf = cf[:, C_ID:C_ID + 128]

        mhalf = fw.sb("mhalf", [128, 512], F32)
        fw.memset("pool", mhalf[:, :], -0.5)
        hT = [fw.sb("hT%d" % k, [128, TT], BF16) for k in range(KC)]
        slots = [fw.sb("slot%d" % k, [128, TT], BF16) for k in range(16)]
        wbuf = RR([fw.sb("wb%d" % i, [128, D], BF16, dma=True) for i in range(3)])
        wdbuf = RR([fw.sb("wd%d" % i, [128, FH * 128], BF16, dma=True) for i in range(2)])
        wpbuf = RR([fw.sb("wp%d" % i, [128, 512], BF16, dma=True) for i in range(2)])
        wabb = fw.sb("wabb", [128, KC * 32], BF16, dma=True)
        small = fw.sb("small", [128, NSM], F32, dma=True)
        xbuf = RR([fw.sb("xb%d" % i, [128, TT], F32, dma=True) for i in range(3)])
        sqb = RR([fw.sb("sq%d" % i, [128, TT], BF16) for i in range(2)])
        rstd = fw.sb("rstd", [128, TT], F32)
        tmpA = RR([fw.sb("tA%d" % i, [128, TT + 16], F32) for i in range(2)])
        tmpB = RR([fw.sb("tB%d" % i, [128, TT + 16], F32) for i in range(3)])
        qkvT = [fw.sb("qkvT%d" % i, [128, TT], BF16) for i in range(3)]
        gateT = fw.sb("gateT", [128, TT], BF16)
        yaT = fw.sb("yaT", [128, TT], F32)
        pooledT = [fw.sb("pooledT%d" % i, [128, TT], BF16) for i in range(4)]
        tokm = [fw.sb("tokm%d" % i, [128, NCH, 128], F32 if i < 2 else BF16) for i in range(3)]
        Sst = fw.sb("Sst", [128, NH, 128], F32)
        Stok = [Tok() for _ in range(NH)]
        c_qkv = fw.sb("c_qkv", [128, 48, 3], F32)
        c_pool = fw.sb("c_pool", [128, 16, 15], F32)
        c_ffn = fw.sb("c_ffn", [128, NFF, 2], F32)
        ab_sb = fw.sb("ab_sb", [128, NCH, 32], F32)
        beta = fw.sb("beta", [128, NCH, 16], F32)
        glog = fw.sb("glog", [128, NCH, 16], F32)
        gcs = fw.sb("gcs", [128, NCH, 16], F32)
        egt = fw.sb("egt", [128, NCH, 16], F32)
        ket = fw.sb("ket", [128, NCH, 16], F32)
        dect = fw.sb("dect", [128, NCH, 16], F32)
        negA = fw.sb("negA", [128, NCH, 16], F32)
        ts0 = fw.sb("ts0", [128, NCH, 16], F32)
        ts1 = fw.sb("ts1", [128, NCH, 16], F32)
        ssk = fw.sb("ssk", [128, NCH], F32)
        ssq = fw.sb("ssq", [128, NCH], F32)
        sc = {n: fw.sb("sc_" + n, [128, NCH], F32) for n in ("rk", "rq", "kg", "ke", "qg")}
        sso = RR([fw.sb("sso%d" % i, [128, 2], F32) for i in range(2)])
        junk = RR([fw.sb("junk%d" % i, [128, 128], F32) for i in range(2)])
        dg = RR([fw.sb("dg%d" % i, [128, 4, 128], F32) for i in range(2)])
        kk4 = RR([fw.sb("kk4_%d" % i, [128, 4, 128], F32) for i in range(2)])
        ketok = RR([fw.sb("ketok%d" % i, [128, 128], F32) for i in range(2)])
        slg = RR([fw.sb("slg%d" % i, [128, 128], F32) for i in range(2)])
        decT = RR([fw.sb("decT%d" % i, [128, 128], F32) for i in range(2)])
        dTi = RR([fw.sb("dTi%d" % i, [128, 128], F32) for i in range(2)])
        dTs = RR([fw.sb("dTs%d" % i, [128, 128], F32) for i in range(2)])
        attnT = RR([fw.sb("attnT%d" % i, [128, 128], F32) for i in range(2)])
        YT = RR([fw.sb("YTb%d" % i, [128, 128], F32) for i in range(3)])
        YTm = RR([fw.sb("YTm%d" % i, [128, 2, 128], F32) for i in range(3)])
        Tfin = RR([fw.sb("Tfin%d" % i, [128, 128], F32) for i in range(2)])
        rp = RR([fw.sb("rp%d" % i, [128, 128], F32) for i in range(2)])
        vnew = RR([fw.sb("vnew%d" % i, [128, 128], F32) for i in range(2)])
        onb = RR([fw.sb("onb%d" % i, [128, 128], BF16) for i in range(2)])

        pbank = [es.enter_context(nc.psum_tensor("pb%d" % i, [128, 512], F32)) for i in range(7)]
        pbf = es.enter_context(nc.psum_tensor("pbf", [128, 1024], BF16))
        accA = [Tile(pbank[0], Tok()), Tile(pbank[1], Tok())]
        accB = [Tile(pbank[2], Tok()), Tile(pbank[3], Tok())]
        acc_rr = RR([accA, accB])

        class PS:
            def __init__(self, t, lo, n, tok):
                self.t, self.lo, self.n, self.tok = t, lo, n, tok

            def v(self, a=0, b=None):
                b = self.n if b is None else b
                return V(self.t[:, self.lo + a:self.lo + b], [self.tok])

        g_rr = RR([PS(pbank[4], 0, 512, Tok()), PS(pbank[5], 0, 512, Tok()), PS(pbank[6], 0, 512, Tok())])
        pair_rr = g_rr
        sing_rr = g_rr
        pbf_ps = PS(pbf, 0, 1024, Tok())

        def sm(off, n=1):
            return small[:, off:off + n]

        def load_small(l):
            fw.dma("sp", small[:, :], V(small_d.ap()[l], [t_small]), small.tok)
            fw.dma("pool", wabb[:, :], V(wab_d.ap()[l], [t_wab]), wabb.tok)
            fw.act(ts0.re("p a b -> p (a b)"), sm(O_ALOG, 128), AF.Exp)
            fw.ts("dve", negA.re("p a b -> p (a b)"), ts0.re("p a b -> p (a b)"), -1.0, ALU.mult)

        def x_src(l, phase, ti, kc):
            cols = slice(ti * TT, (ti + 1) * TT)
            if l == 0 and phase == 0:
                return V(xT_d.ap()[kc * 128:(kc + 1) * 128, cols], [t_xin])
            return V(yT_d.ap()[kc * 128:(kc + 1) * 128, cols], [t_y[ti][kc]])

        def y_dst(ti, kc):
            cols = slice(ti * TT, (ti + 1) * TT)
            return V(yT_d.ap()[kc * 128:(kc + 1) * 128, cols], [t_y[ti][kc]])

        def rmsnorm_to_hT(l, phase, ti, woff):
            acc = acc_rr.get()
            for kc in range(KC):
                xb = xbuf.get()
                fw.dma("sp", xb[:, :], x_src(l, phase, ti, kc), xb.tok)
                sq = sqb.get()
                fw.act(sq[:, :], xb[:, :], AF.Square)
                for hf in range(2):
                    fw.mm(acc[hf][:, :], onesb, sq[:, hf * 512:(hf + 1) * 512], start=(kc == 0), stop=(kc == KC - 1))
            for hf in range(2):
                fw.ts("dve", rstd[:, hf * 512:(hf + 1) * 512], acc[hf][:, :], 1.0 / D, ALU.mult, EPS, ALU.add)
            for hf in range(2):
                fw.tt("pool", rstd[:, hf * 512:(hf + 1) * 512], rstd[:, hf * 512:(hf + 1) * 512], mhalf[:, :], ALU.pow)
            for kc in range(KC):
                xb = xbuf.get()
                fw.dma("sp", xb[:, :], x_src(l, phase, ti, kc), xb.tok)
                fw.stt("dve", hT[kc][:, :], xb[:, :], sm(woff + kc), rstd[:, :], ALU.mult, ALU.mult)

        def proj(wsrc, wtok, buf_rr=None, nk=KC, rhs_tiles=None):
            buf_rr = buf_rr or wbuf
            rhs_tiles = rhs_tiles or hT
            wb = buf_rr.get()
            fw.dma("pool", wb[:, :], V(wsrc, [wtok]), wb.tok)
            acc = acc_rr.get()
            for hf in range(2):
                for kc in range(nk):
                    fw.mm(acc[hf][:, :], wb[:, kc * 128:(kc + 1) * 128], rhs_tiles[kc][:, hf * 512:(hf + 1) * 512],
                          start=(kc == 0), stop=(kc == nk - 1))
            return acc

        def evac2(eng_fn, acc):
            for hf in range(2):
                eng_fn(hf, acc[hf][:, :], slice(hf * 512, (hf + 1) * 512))

        def token_scalars():
            flat = lambda t: t.re("p a b -> p (a b)")
            for c in range(NCH):
                ps = pair_rr.get()
                for kc in range(KC):
                    fw.mm(ps.v(0, 32), hT[kc][:, c * 128:(c + 1) * 128], wabb[:, kc * 32:(kc + 1) * 32],
                          start=(kc == 0), stop=(kc == KC - 1))
                fw.copy("act", ab_sb[:, c, :], ps.v(0, 32))
            checkpoint("tsc1")
            fw.act(beta[:, :, :], ab_sb[:, :, 0:16], AF.Sigmoid)
            fw.tt("dve", ts0[:, :, :], ab_sb[:, :, 16:32],
                  V(small.t[:, O_DTB:O_DTB + 128].rearrange("p (a b) -> p a b", b=16), [small.tok]), ALU.add)
            fw.act(ts1[:, :, :], ts0[:, :, :], AF.Exp)
            fw.act(ts0[:, :, :], ts1[:, :, :], AF.Ln, bias=1.0)
            fw.tt("dve", glog[:, :, :], ts0[:, :, :], negA[:, :, :], ALU.mult)
            checkpoint("tsc2")
            p1 = pair_rr.get()
            fw.mm(p1.v(0, 128), triuf, flat(glog))
            p2 = pair_rr.get()
            fw.mm(p2.v(0, 128), onesf, flat(glog))
            checkpoint("tsc3")
            fw.copy("act", flat(gcs), p1.v(0, 128))
            fw.act(flat(egt), p1.v(0, 128), AF.Exp)
            fw.tt("dve", flat(ts0), p2.v(0, 128), flat(gcs), ALU.subtract)
            fw.act(flat(ket), flat(ts0), AF.Exp)
            fw.act(flat(dect), p2.v(0, 128), AF.Exp)

        def gdn_head(l, h, jslot):
            kT, qT, vT = qkvT[1], qkvT[0], qkvT[2]
            ktok, qtok, vtok = tokm
            for which, (srcT, dst, ss) in enumerate(((kT, ktok, ssk), (qT, qtok, ssq), (vT, vtok, None))):
                for c in range(NCH):
                    fw.tr(pbf_ps.v(c * 128, (c + 1) * 128), srcT[:, c * 128:(c + 1) * 128], identb)
                if ss is not None:
                    sqt = tmpB.get()
                    fw.act(sqt[:, 0:TT], pbf_ps.v(0, TT), AF.Square)
                    sq3 = V(sqt.t[:, 0:TT].rearrange("p (c d) -> p c d", d=128), [sqt.tok])
                    fw.op("dve", lambda e: e.tensor_reduce(out=ss.t[:, :], in_=sq3.ap, axis=mybir.AxisListType.X, op=ALU.add),
                          reads=sq3.toks, writes=[ss.tok])
                fw.copy("dve", dst.re("p c d -> p (c d)"), pbf_ps.v(0, TT))
            checkpoint("gdnA%d" % h)
            hs = lambda t: V(t.t[:, :, h], [t.tok])
            fw.ts("dve", sc["rk"][:, :], ssk[:, :], EPS, ALU.add)
            fw.tt("pool", sc["rk"][:, :], sc["rk"][:, :], mhalf[:, 0:NCH], ALU.pow)
            fw.ts("dve", sc["rq"][:, :], ssq[:, :], EPS, ALU.add)
            fw.tt("pool", sc["rq"][:, :], sc["rq"][:, :], mhalf[:, 0:NCH], ALU.pow)
            fw.ts("dve", sc["rq"][:, :], sc["rq"][:, :], float(HD) ** -0.5, ALU.mult)
            fw.tt("dve", sc["kg"][:, :], sc["rk"][:, :], hs(egt), ALU.mult)
            fw.tt("dve", sc["ke"][:, :], sc["rk"][:, :], hs(ket), ALU.mult)
            fw.tt("dve", sc["qg"][:, :], sc["rq"][:, :], hs(egt), ALU.mult)
            Sh = V(Sst.t[:, h, :], [Stok[h]])
            for c in range(NCH):
                cs = slice(c * 128, (c + 1) * 128)
                col = lambda t: V(t.t[:, c:c + 1], [t.tok])
                colh = lambda t: V(t.t[:, c, h:h + 1], [t.tok])
                d = dg.get()
                fw.ts("pool", d[:, 0, :], identf, col(sc["rk"]), ALU.mult)
                fw.ts("pool", d[:, 1, :], identf, col(sc["kg"]), ALU.mult)
                fw.ts("pool", d[:, 2, :], identf, col(sc["rq"]), ALU.mult)
                fw.ts("pool", d[:, 3, :], identf, col(sc["qg"]), ALU.mult)
                k4 = kk4.get()
                p = pair_rr.get()
                fw.mm(p.v(0, 256), ktok[:, c, :], V(d.t[:, 0:2, :].rearrange("p a b -> p (a b)"), [d.tok]))
                fw.copy("act", V(k4.t[:, 0::2, :], [k4.tok]), V(p.t[:, p.lo:p.lo + 256].rearrange("p (a b) -> p a b", b=128), [p.tok]))
                p = pair_rr.get()
                fw.mm(p.v(0, 256), qtok[:, c, :], V(d.t[:, 2:4, :].rearrange("p a b -> p (a b)"), [d.tok]))
                fw.copy("act", V(k4.t[:, 1::2, :], [k4.tok]), V(p.t[:, p.lo:p.lo + 256].rearrange("p (a b) -> p a b", b=128), [p.tok]))
                ke_t = ketok.get()
                fw.ts("pool", ke_t[:, :], ktok[:, c, :], col(sc["ke"]), ALU.mult)
                sl = slg.get()
                fw.ts("pool", sl[:, :], slf, colh(glog), ALU.mult)
                pd = sing_rr.get()
                fw.mm(pd.v(0, 128), sl[:, :], triuf)
                dT = decT.get()
                fw.act(dT[:, :], pd.v(0, 128), AF.Exp)
                di = dTi.get()
                ds_ = dTs.get()
                fw.tt("pool", di[:, :], dT[:, :], V(cf.t[:, C_TRIU:C_TRIU + 128], [cf.tok]), ALU.mult)
                fw.tt("pool", ds_[:, :], dT[:, :], V(cf.t[:, C_SU:C_SU + 128], [cf.tok]), ALU.mult)
                pg = pair_rr.get()
                fw.mm(pg.v(0, 256), k4[:, 0, :], V(k4.t[:, 0:2, :].rearrange("p a b -> p (a b)"), [k4.tok]))
                at = attnT.get()
                fw.tt("dve", at[:, :], pg.v(128, 256), di[:, :], ALU.mult)
                ym = YTm.get()
                fw.stt("dve", ym[:, 0, :], pg.v(0, 128), colh(beta), ds_[:, :], ALU.mult, ALU.mult)
                fw.tt("pool", ym[:, 1, :], identf, ym[:, 0, :], ALU.subtract)
                pn = g_rr.get()
                fw.mm(pn.v(0, 128), ym[:, 0, :], identf)
                yt = YT.get()
                fw.copy("act", yt[:, :], pn.v(0, 128))
                for k in range(7):
                    last = (k == 6)
                    first = (k == 0)
                    pm = pair_rr.get()
                    if first:
                        fw.mm(pm.v(0, 128), yt[:, :], ym[:, 0, :])
                    elif last:
                        fw.mm(pm.v(128, 256), yt[:, :], ym[:, 1, :])
                    else:
                        fw.mm(pm.v(0, 256), yt[:, :], V(ym.t[:, 0:2, :].rearrange("p a b -> p (a b)"), [ym.tok]))
                    if not last:
                        p2_ = sing_rr.get()
                        fw.mm(p2_.v(0, 128), ym[:, 0, :], yt[:, :])
                        yt2 = YT.get()
                        fw.copy("act", yt2[:, :], p2_.v(0, 128))
                        ym2 = YTm.get()
                        fw.copy("act", ym2[:, 0, :], pm.v(0, 128))
                        if first:
                            fw.copy("pool", ym2[:, 1, :], ym[:, 1, :])
                        else:
                            fw.tt("dve", ym2[:, 1, :], pm.v(128, 256), ym[:, 1, :], ALU.add)
                        ym, yt = ym2, yt2
                    else:
                        tf = Tfin.get()
                        fw.tt("dve", tf[:, :], pm.v(128, 256), ym[:, 1, :], ALU.add)
                checkpoint("inv%d_%d" % (h, c))
                pr = sing_rr.get()
                fw.mm(pr.v(0, 128), k4[:, 2, :], Sh)
                r_ = rp.get()
                fw.tt("dve", r_[:, :], vtok[:, c, :], pr.v(0, 128), ALU.subtract)
                pv_ = sing_rr.get()
                fw.mm(pv_.v(0, 128), tf[:, :], r_[:, :])
                vn = vnew.get()
                fw.ts("dve", vn[:, :], pv_.v(0, 128), colh(beta), ALU.mult)
                po = sing_rr.get()
                fw.mm(po.v(0, 128), k4[:, 3, :], Sh, start=True, stop=False)
                fw.mm(po.v(0, 128), at[:, :], vn[:, :], start=False, stop=True)
                pS = sing_rr.get()
                fw.mm(pS.v(0, 128), ke_t[:, :], vn[:, :])
                fw.stt("dve", Sh, Sh, colh(dect), pS.v(0, 128), ALU.mult, ALU.add)
                so = sso.get()
                fw.act(junk.get()[:, :], po.v(0, 128), AF.Square, accum=so[:, 0:1])
                fw.ts("dve", so[:, 1:2], so[:, 0:1], 1.0 / HD, ALU.mult, EPS, ALU.add)
                fw.tt("pool", so[:, 1:2], so[:, 1:2], mhalf[:, 0:1], ALU.pow)
                ob = onb.get()
                fw.act(ob[:, :], po.v(0, 128), AF.Copy, scale=so[:, 1:2])
                fw.tr(pbf_ps.v(128, 256), ob[:, :], identb)
                fw.stt("dve", yaT[:, cs], pbf_ps.v(128, 256), sm(O_GNW), gateT[:, cs], ALU.mult, ALU.mult)
                checkpoint("chunk%d_%d" % (h, c))
            fw.tt("pool", slots[jslot][:, :], yaT[:, :], slots[jslot][:, :], ALU.add)

        def mixer(l, ti):
            rmsnorm_to_hT(l, 0, ti, O_NMIX)
            checkpoint("rms")
            token_scalars()
            checkpoint("tsc")
            ct = 0
            for g in range(4):
                win_ = 2 ** (g + 1)
                for jj in range(4):
                    j = 4 * g + jj
                    acc = proj(win_d.ap()[l, ct], t_win); ct += 1
                    evac2(lambda hf, ps, sl: fw.act(slots[j][:, sl], ps, AF.Sigmoid), acc)
                for jj in range(4):
                    j = 4 * g + jj
                    acc = proj(win_d.ap()[l, ct], t_win); ct += 1
                    U = tmpA.get()
                    fw.copy("pool", U[:, 1:16], c_pool[:, j, :])
                    fw.memset("pool", U[:, 0:1], 0.0)
                    evac2(lambda hf, ps, sl: fw.copy("act", U[:, 16 + sl.start:16 + sl.stop], ps), acc)
                    fw.copy("pool", c_pool[:, j, :], U[:, TT + 1:TT + 16])
                    cur = U
                    sh = 1
                    e0 = 1
                    while sh < win_:
                        nxt = tmpB.get()
                        fw.tt("pool" if sh > 1 else "dve", nxt[:, e0:TT + 16], cur[:, e0:TT + 16], cur[:, e0 - sh:TT + 16 - sh], ALU.add)
                        cur = nxt
                        sh *= 2
                        e0 = 2 * sh - 1
                    fw.stt("dve", pooledT[jj][:, :], cur[:, 16:TT + 16], 1.0 / win_, U[:, 16:TT + 16], ALU.mult, ALU.subtract)
                    if ti == 0:
                        tq = junk.get()
                        fw.tt("dve", tq[:, 0:16], cur[:, 16:32], cf[:, C_ICNT + g * 16:C_ICNT + g * 16 + 16], ALU.mult)
                        fw.tt("dve", pooledT[jj][:, 0:16], tq[:, 0:16], U[:, 16:32], ALU.subtract)
                for ot in range(4):
                    j = 4 * g + ot
                    acc = proj(wpool_d.ap()[l, g, ot], t_wpool, buf_rr=wpbuf, nk=4, rhs_tiles=pooledT)
                    evac2(lambda hf, ps, sl: fw.stt("dve", slots[j][:, sl], ps, sm(O_PSC + j), slots[j][:, sl], ALU.mult, ALU.mult), acc)
                checkpoint("pool%d" % g)
                for h in range(4 * g, 4 * g + 4):
                    for which in range(3):
                        cti = which * 16 + h
                        acc = proj(win_d.ap()[l, ct], t_win); ct += 1
                        R = tmpA.get()
                        A = tmpB.get()
                        fw.copy("pool", R[:, 0:3], c_qkv[:, cti, :])
                        wq = lambda jtap: sm(O_CQKV + cti * 4 + jtap)
                        evac2(lambda hf, ps, sl: fw.copy("act", R[:, 3 + sl.start:3 + sl.stop], ps), acc)
                        evac2(lambda hf, ps, sl: fw.act(A[:, sl], ps, AF.Copy, scale=wq(3)), acc)
                        fw.copy("pool", c_qkv[:, cti, :], R[:, TT:TT + 3])
                        fw.stt("dve", A[:, 0:TT], R[:, 2:TT + 2], wq(2), A[:, 0:TT], ALU.mult, ALU.add)
                        fw.stt("dve", A[:, 0:TT], R[:, 1:TT + 1], wq(1), A[:, 0:TT], ALU.mult, ALU.add)
                        fw.stt("dve", A[:, 0:TT], R[:, 0:TT], wq(0), A[:, 0:TT], ALU.mult, ALU.add)
                        fw.act(qkvT[which][:, :], A[:, 0:TT], AF.Silu)
                    accz = proj(win_d.ap()[l, ct], t_win); ct += 1
                    Z = tmpB.get()
                    evac2(lambda hf, ps, sl: fw.act(Z[:, sl], ps, AF.Silu), accz)
                    accg = proj(win_d.ap()[l, ct], t_win); ct += 1
                    G = tmpB.get()
                    evac2(lambda hf, ps, sl: fw.act(G[:, sl], ps, AF.Sigmoid), accg)
                    fw.tt("pool", gateT[:, :], Z[:, 0:TT], G[:, 0:TT], ALU.mult)
                    checkpoint("qkv%d" % h)
                    gdn_head(l, h, h)
                    checkpoint("head%d" % h)
            assert ct == 112
            for j in range(16):
                acc = proj(wout_d.ap()[l, j], t_wout, rhs_tiles=slots)
                xb = xbuf.get()
                fw.dma("sp", xb[:, :], x_src(l, 0, ti, j), xb.tok)
                evac2(lambda hf, ps, sl: fw.tt("dve", xb[:, sl], ps, xb[:, sl], ALU.add), acc)
                fw.dma("sp", y_dst(ti, j), xb[:, :], st_rr.get())

        def ffn(l, ti):
            rmsnorm_to_hT(l, 1, ti, O_NFFN)
            for hh in range(NQ):
                for ff in range(FH):
                    f = hh * FH + ff
                    accg = proj(wup_d.ap()[l, 2 * f], t_wup)
                    R = tmpA.get()
                    A = tmpB.get()
                    fw.copy("pool", R[:, 0:2], c_ffn[:, f, :])
                    wf = lambda jtap: sm(O_CFW + f * 3 + jtap)
                    evac2(lambda hf, ps, sl: fw.copy("act", R[:, 2 + sl.start:2 + sl.stop], ps), accg)
                    evac2(lambda hf, ps, sl: fw.act(A[:, sl], ps, AF.Identity, scale=wf(2), bias=sm(O_CFB + f)), accg)
                    fw.copy("pool", c_ffn[:, f, :], R[:, TT:TT + 2])
                    fw.stt("dve", A[:, 0:TT], R[:, 1:TT + 1], wf(1), A[:, 0:TT], ALU.mult, ALU.add)
                    fw.stt("dve", A[:, 0:TT], R[:, 0:TT], wf(0), A[:, 0:TT], ALU.mult, ALU.add)
                    Gl = tmpB.get()
                    fw.act(Gl[:, 0:TT], A[:, 0:TT], AF.Gelu)
                    accu = proj(wup_d.ap()[l, 2 * f + 1], t_wup)
                    evac2(lambda hf, ps, sl: fw.tt("dve", slots[ff][:, sl], ps, Gl[:, sl], ALU.mult), accu)
                for j in range(16):
                    acc = proj(wdn_d.ap()[l, hh, j], t_wdn, buf_rr=wdbuf, nk=FH, rhs_tiles=slots)
                    xb = xbuf.get()
                    fw.dma("sp", xb[:, :], x_src(l, 1, ti, j), xb.tok)
                    evac2(lambda hf, ps, sl: fw.tt("dve", xb[:, sl], ps, xb[:, sl], ALU.add), acc)
                    fw.dma("sp", y_dst(ti, j), xb[:, :], st_rr.get())

        def final_norm(ti):
            acc = acc_rr.get()
            for kc in range(KC):
                xb = xbuf.get()
                fw.dma("sp", xb[:, :], x_src(L, 0, ti, kc), xb.tok)
                sq = sqb.get()
                fw.act(sq[:, :], xb[:, :], AF.Square)
                for hf in range(2):
                    fw.mm(acc[hf][:, :], onesb, sq[:, hf * 512:(hf + 1) * 512], start=(kc == 0), stop=(kc == KC - 1))
            for hf in range(2):
                fw.ts("dve", rstd[:, hf * 512:(hf + 1) * 512], acc[hf][:, :], 1.0 / D, ALU.mult, EPS, ALU.add)
            for hf in range(2):
                fw.tt("pool", rstd[:, hf * 512:(hf + 1) * 512], rstd[:, hf * 512:(hf + 1) * 512], mhalf[:, :], ALU.pow)
            for kc in range(KC):
                xb = xbuf.get()
                fw.dma("sp", xb[:, :], x_src(L, 0, ti, kc), xb.tok)
                fw.stt("dve", xb[:, :], xb[:, :], cf[:, C_NFIN + kc:C_NFIN + kc + 1], rstd[:, :],
                       ALU.mult, ALU.mult)
                fw.dma("sp", y_dst(ti, kc), xb[:, :], st_rr.get())

        def dump(name, v, width):
            dd = dr("dbg_" + name, [128, width], kind="ExternalOutput")
            tk = fw.dtok("dbg_" + name)
            fw.dma("pool", V(dd.ap()[:, :], [Tok()]), v, tk)
            dbg_list.append(tk)

        build_program.dump = dump
        build_program.env = locals()
        try:
          for l in range(L):
            load_small(l)
            fw.memset("pool", Sst[:, :, :], 0.0)
            for h in range(NH):
                Stok[h].w = Sst.tok.w
            fw.memset("pool", c_qkv[:, :, :], 0.0)
            fw.memset("pool", c_pool[:, :, :], 0.0)
            fw.memset("pool", c_ffn[:, :, :], 0.0)
            for ti in range(NT):
                mixer(l, ti)
                checkpoint("mixer")
                ffn(l, ti)
                checkpoint("ffn")
          for ti in range(NT):
            final_norm(ti)
        except StopBuild:
            if build_program.on_stop:
                build_program.on_stop(locals())
        for t in st_tok + dbg_list:
            if t.dcnt:
                fw._deps("sp", [], [])
                fw.eng["sp"].wait_ge(t.dsem, t.dcnt)
        for e in ("pe", "act", "dve", "pool"):
            if fw.cnt[e]:
                fw.eng["sp"].wait_ge(fw.sem[e], fw.cnt[e])
        print("program: ins=%d waits=%d" % (fw.nins, fw.nwait), {e: fw.cnt[e] for e in fw.cnt})
    return nc


build_program.on_stop = None
_CACHE = {}


def run(inputs, L, S, ncores, stop_at=None, ret_all=False):
    B = inputs["x"].shape[0]
    pw = prep_weights(inputs, L)
    key = (L, S, stop_at)
    if key not in _CACHE:
        _CACHE[key] = build_program(L, S, stop_at=stop_at)
    nc = _CACHE[key]
    in_maps = []
    for c in range(ncores):
        b = c % B
        m = {"xT": np.ascontiguousarray(inputs["x"][b, :S].T)}
        m.update(pw)
        in_maps.append(m)
    res = run_bass_kernel_spmd(nc, in_maps, core_ids=list(range(ncores)))
    if ret_all:
        return res.results
    out = np.stack([np.ascontiguousarray(res.results[b]["yT"].T) for b in range(B)], axis=0)
    return out.astype(np.float32)


def kernel(**inputs):
    inputs = {k: np.asarray(v, dtype=np.float32) for k, v in inputs.items()}
    return run(inputs, 4, 4096, 8)
```

```python
import numpy as np
from contextlib import ExitStack
import concourse.bass as bass
import concourse.mybir as mybir
from concourse.bass_utils import run_bass_kernel_spmd

F32 = mybir.dt.float32
BF16 = mybir.dt.bfloat16
AF = mybir.ActivationFunctionType
ALU = mybir.AluOpType

D = 2048
NH = 16
HD = 128
DFF = 5632
NFF = DFF // 128
DIN = 4 * D + 2 * NH + D + 2 * D
EPS = 1e-6
TT = 1024
NCH = TT // 128
KC = D // 128
NQ = 4
FH = NFF // NQ


class Tok:
    __slots__ = ("w", "r", "dsem", "dcnt", "name")

    def __init__(self, name=""):
        self.w = None
        self.r = {}
        self.dsem = None
        self.dcnt = 0
        self.name = name


class V:
    __slots__ = ("ap", "toks")

    def __init__(self, ap, toks):
        self.ap = ap
        self.toks = toks


class Tile:
    def __init__(self, t, tok):
        self.t = t
        self.tok = tok

    def __getitem__(self, idx):
        return V(self.t[idx], [self.tok])

    def re(self, pat, **kw):
        return V(self.t[:].rearrange(pat, **kw), [self.tok])


class FW:
    def __init__(self, nc, es):
        self.nc = nc
        self.es = es
        self.eng = {"pe": nc.tensor, "act": nc.scalar, "dve": nc.vector, "pool": nc.gpsimd, "sp": nc.sync}
        self.sem = {e: es.enter_context(nc.semaphore("s_" + e)) for e in self.eng}
        self.cnt = {e: 0 for e in self.eng}
        self.seen = {e: {} for e in self.eng}
        self.nwait = 0
        self.nins = 0
        self.ntile = 0

    def sb(self, name, shape, dt, dma=False):
        t = self.es.enter_context(self.nc.sbuf_tensor("sb_" + name, list(shape), dt))
        tok = self.dtok(name) if dma else Tok(name)
        return Tile(t, tok)

    def dtok(self, name):
        t = Tok(name)
        t.dsem = self.es.enter_context(self.nc.semaphore("d_" + name))
        return t

    def _deps(self, e, reads, writes):
        deps = []
        for t in reads:
            if t.w is not None:
                deps.append(t.w)
        for t in writes:
            if t.w is not None:
                deps.append(t.w)
            deps.extend(t.r.values())
        mysem = self.sem[e]
        seen = self.seen[e]
        for (sem, val) in deps:
            if e == "pe" and sem is mysem:
                continue
            if seen.get(sem, 0) >= val:
                continue
            self.eng[e].wait_ge(sem, val)
            seen[sem] = val
            self.nwait += 1

    def op(self, e, build, reads=(), writes=()):
        self._deps(e, reads, writes)
        ins = build(self.eng[e])
        self.cnt[e] += 1
        self.nins += 1
        ins.then_inc(self.sem[e], 1)
        me = (self.sem[e], self.cnt[e])
        s = self.sem[e]
        for t in reads:
            t.r[s] = me
        for t in writes:
            t.w = me
            t.r = {}
        return ins

    def dma(self, q, out, in_, tok):
        reads, writes = in_.toks, out.toks
        self._deps(q, reads, writes)
        if tok.dcnt > 0 and self.seen[q].get(tok.dsem, 0) < tok.dcnt:
            self.eng[q].wait_ge(tok.dsem, tok.dcnt)
            self.seen[q][tok.dsem] = tok.dcnt
            self.nwait += 1
        ins = self.eng[q].dma_start(out=out.ap, in_=in_.ap)
        tok.dcnt += 16
        ins.then_inc(tok.dsem, 16)
        self.nins += 1
        me = (tok.dsem, tok.dcnt)
        for t in reads:
            t.r[tok.dsem] = me
        for t in writes:
            t.w = me
            t.r = {}
        return ins

    def wait_tok(self, e, tok):
        self._deps(e, [tok], [])

    def mm(self, out, lhsT, rhs, start=True, stop=True):
        return self.op("pe", lambda e: e.matmul(out.ap, lhsT=lhsT.ap, rhs=rhs.ap, start=start, stop=stop),
                       reads=lhsT.toks + rhs.toks, writes=out.toks)

    def tr(self, out, in_, ident):
        return self.op("pe", lambda e: e.transpose(out.ap, in_.ap, ident.ap),
                       reads=in_.toks + ident.toks, writes=out.toks)

    def act(self, out, in_, func, bias=None, scale=None, accum=None, eng="act"):
        kw = {}
        reads = list(in_.toks)
        writes = list(out.toks)
        if bias is not None:
            if isinstance(bias, V):
                kw["bias"] = bias.ap
                reads += bias.toks
            else:
                kw["bias"] = bias
        if scale is not None:
            if isinstance(scale, V):
                kw["scale"] = scale.ap
                reads += scale.toks
            else:
                kw["scale"] = scale
        if accum is not None:
            kw["accum_out"] = accum.ap
            writes += accum.toks
        return self.op(eng, lambda e: e.activation(out=out.ap, in_=in_.ap, func=func, **kw), reads=reads, writes=writes)

    def tt(self, eng, out, in0, in1, op):
        return self.op(eng, lambda e: e.tensor_tensor(out=out.ap, in0=in0.ap, in1=in1.ap, op=op),
                       reads=in0.toks + in1.toks, writes=out.toks)

    def ts(self, eng, out, in0, s1, op0, s2=None, op1=None, accum=None):
        reads = list(in0.toks)
        writes = list(out.toks)
        a1 = s1
        if isinstance(s1, V):
            a1 = s1.ap
            reads += s1.toks
        a2 = s2
        if isinstance(s2, V):
            a2 = s2.ap
            reads += s2.toks
        kw = {}
        if op1 is not None:
            kw["op1"] = op1
        if accum is not None:
            kw["accum_out"] = accum.ap
            writes += accum.toks
        return self.op(eng, lambda e: e.tensor_scalar(out=out.ap, in0=in0.ap, scalar1=a1, scalar2=a2, op0=op0, **kw),
                       reads=reads, writes=writes)

    def stt(self, eng, out, in0, scalar, in1, op0, op1):
        reads = in0.toks + in1.toks
        a = scalar
        if isinstance(scalar, V):
            a = scalar.ap
            reads = reads + scalar.toks
        return self.op(eng, lambda e: e.scalar_tensor_tensor(out=out.ap, in0=in0.ap, scalar=a, in1=in1.ap, op0=op0, op1=op1),
                       reads=reads, writes=out.toks)

    def copy(self, eng, out, in_):
        if eng == "act":
            return self.act(out, in_, AF.Copy)
        return self.op(eng, lambda e: e.tensor_copy(out=out.ap, in_=in_.ap), reads=in_.toks, writes=out.toks)

    def memset(self, eng, out, val):
        return self.op(eng, lambda e: e.memset(out.ap, val), reads=[], writes=out.toks)


class RR:
    def __init__(self, items):
        self.items = items
        self.i = 0

    def get(self):
        it = self.items[self.i % len(self.items)]
        self.i += 1
        return it


O_NMIX = 0
O_NFFN = 16
O_PSC = 32
O_CQKV = 48
O_CFW = O_CQKV + 192
O_CFB = O_CFW + 132
O_GNW = O_CFB + 44
O_ALOG = O_GNW + 1
O_DTB = O_ALOG + 128
NSM = O_DTB + 128
C_ID = 0
C_TRIU = 128
C_SU = 256
C_SL = 384
C_ONE = 512
C_ICNT = 640
C_NFIN = 704
NCONST = 720


def _colmajor(w, ncol_tiles):
    K = w.shape[0]
    return np.ascontiguousarray(w.reshape(K // 128, 128, ncol_tiles, 128).transpose(2, 1, 0, 3)).reshape(
        ncol_tiles, 128, (K // 128) * 128)


def prep_weights(inp, L):
    out = {}
    w_in = inp["w_in"]
    order = []
    for g in range(4):
        for j in range(4 * g, 4 * g + 4):
            order.append(4 * D + 2 * NH + D + D + j * 128)
        for j in range(4 * g, 4 * g + 4):
            order.append(4 * D + 2 * NH + j * 128)
        for h in range(4 * g, 4 * g + 4):
            order.append(0 * D + h * 128)
            order.append(1 * D + h * 128)
            order.append(2 * D + h * 128)
            order.append(3 * D + h * 128)
            order.append(4 * D + 2 * NH + D + h * 128)
    cols = np.concatenate([np.arange(o, o + 128) for o in order])
    win = np.empty((L, len(order), 128, D), np.float32)
    wab = np.empty((L, 128, KC * 32), np.float32)
    wout = np.empty((L, 16, 128, D), np.float32)
    wup = np.empty((L, 2 * NFF, 128, D), np.float32)
    wdn = np.empty((L, NQ, 16, 128, FH * 128), np.float32)
    wpool = np.empty((L, 4, 4, 128, 512), np.float32)
    small = np.zeros((L, 128, NSM), np.float32)
    upcols = np.concatenate([np.concatenate([np.arange(f * 128, f * 128 + 128), np.arange(DFF + f * 128, DFF + f * 128 + 128)])
                             for f in range(NFF)])
    for l in range(L):
        win[l] = _colmajor(w_in[l][:, cols], len(order))
        ab = w_in[l][:, 4 * D:4 * D + 32]
        wab[l] = ab.reshape(KC, 128, 32).transpose(1, 0, 2).reshape(128, KC * 32)
        wout[l] = _colmajor(inp["w_out"][l], 16)
        wup[l] = _colmajor(inp["w_up"][l][:, upcols], 2 * NFF)
        wd = inp["w_down"][l]
        for hh in range(NQ):
            blk = wd[hh * FH * 128:(hh + 1) * FH * 128]
            wdn[l, hh] = _colmajor(blk, 16)
        for g in range(4):
            wpool[l, g] = _colmajor(inp["pool_w"][l, g], 4)
        sm = small[l]
        sm[:, O_NMIX:O_NMIX + 16] = inp["norm_mix_w"][l].reshape(16, 128).T
        sm[:, O_NFFN:O_NFFN + 16] = inp["norm_ffn_w"][l].reshape(16, 128).T
        sm[:, O_PSC:O_PSC + 16] = inp["pool_scale"][l].reshape(16, 128).T
        sm[:, O_CQKV:O_CQKV + 192] = inp["conv_qkv_w"][l].reshape(4, 48, 128).transpose(2, 1, 0).reshape(128, 192)
        sm[:, O_CFW:O_CFW + 132] = inp["conv_ffn_w"][l].reshape(3, NFF, 128).transpose(2, 1, 0).reshape(128, 132)
        sm[:, O_CFB:O_CFB + 44] = inp["conv_ffn_b"][l].reshape(NFF, 128).T
        sm[:, O_GNW] = inp["gdn_norm_w"][l]
        sm[:, O_ALOG:O_ALOG + 128] = np.tile(inp["a_log"][l], NCH)[None, :]
        sm[:, O_DTB:O_DTB + 128] = np.tile(inp["dt_bias"][l], NCH)[None, :]
    out.update(win=win, wab=wab, wout=wout, wup=wup, wdn=wdn, wpool=wpool, small=small)
    c = np.zeros((128, NCONST), np.float32)
    idx = np.arange(128)
    c[:, C_ID:C_ID + 128] = np.eye(128, dtype=np.float32)
    c[:, C_TRIU:C_TRIU + 128] = (idx[:, None] <= idx[None, :])
    c[:, C_SU:C_SU + 128] = (idx[None, :] > idx[:, None])
    c[:, C_SL:C_SL + 128] = (idx[:, None] > idx[None, :])
    c[:, C_ONE:C_ONE + 128] = 1.0
    for g, win_ in enumerate((2, 4, 8, 16)):
        t = np.arange(16)
        c[:, C_ICNT + g * 16:C_ICNT + g * 16 + 16] = (np.float32(1.0) / np.minimum(t + 1, win_).astype(np.float32))[None, :]
    c[:, C_NFIN:C_NFIN + 16] = inp["norm_final_w"].reshape(16, 128).T
    out["consts"] = c
    return out


class StopBuild(Exception):
    pass


def build_program(L, S, dbg=None, stop_at=None):
    assert S % TT == 0
    NT = S // TT
    nc = bass.Bass("TRN2", target_bir_lowering=False)
    dr = lambda name, shape, kind="ExternalInput": nc.dram_tensor(name, list(shape), F32, kind=kind)
    xT_d = dr("xT", [D, S])
    win_d = dr("win", [L, 112, 128, D])
    wab_d = dr("wab", [L, 128, KC * 32])
    wout_d = dr("wout", [L, 16, 128, D])
    wup_d = dr("wup", [L, 2 * NFF, 128, D])
    wdn_d = dr("wdn", [L, NQ, 16, 128, FH * 128])
    wpool_d = dr("wpool", [L, 4, 4, 128, 512])
    small_d = dr("small", [L, 128, NSM])
    consts_d = dr("consts", [128, NCONST])
    yT_d = dr("yT", [D, S], kind="ExternalOutput")
    dbg_list = []

    def checkpoint(name):
        if stop_at == name:
            raise StopBuild()

    with ExitStack() as es:
        fw = FW(nc, es)
        t_win, t_wab, t_wout, t_wup, t_wdn, t_wpool, t_small, t_consts = (Tok() for _ in range(8))
        t_xin = Tok()
        t_y = [[Tok() for _ in range(KC)] for _ in range(NT)]
        st_tok = [fw.dtok("st%d" % i) for i in range(3)]
        st_rr = RR(st_tok)

        cf = fw.sb("cf", [128, NCONST], F32, dma=True)
        cb = fw.sb("cb", [128, 640], BF16, dma=True)
        fw.dma("sp", cf[:, :], V(consts_d.ap()[:, :], [t_consts]), cf.tok)
        fw.dma("pool", cb[:, :], V(consts_d.ap()[:, 0:640], [t_consts]), cb.tok)
        identb = cb[:, C_ID:C_ID + 128]
        triub = cb[:, C_TRIU:C_TRIU + 128]
        sub_ = cb[:, C_SU:C_SU + 128]
        onesb = cb[:, C_ONE:C_ONE + 128]
        triuf = cf[:, C_TRIU:C_TRIU + 128]
        slf = cf[:, C_SL:C_SL + 128]
        onesf = cf[:, C_ONE:C_ONE + 128]
        identf = cf[:, C_ID:C_ID + 128]

        hT = [fw.sb("hT%d" % k, [128, TT], BF16) for k in range(KC)]
        slots = [fw.sb("slot%d" % k, [128, TT], BF16) for k in range(16)]
        wbuf = RR([fw.sb("wb%d" % i, [128, D], BF16, dma=True) for i in range(3)])
        wdbuf = RR([fw.sb("wd%d" % i, [128, FH * 128], BF16, dma=True) for i in range(2)])
        wpbuf = RR([fw.sb("wp%d" % i, [128, 512], BF16, dma=True) for i in range(2)])
        wabb = fw.sb("wabb", [128, KC * 32], BF16, dma=True)
        small = fw.sb("small", [128, NSM], F32, dma=True)
        xbuf = RR([fw.sb("xb%d" % i, [128, TT], F32, dma=True) for i in range(2)])
        sqb = RR([fw.sb("sq%d" % i, [128, TT], BF16) for i in range(2)])
        rstd = fw.sb("rstd", [128, TT], F32)
        tmpA = RR([fw.sb("tA%d" % i, [128, TT + 16], F32) for i in range(2)])
        tmpB = RR([fw.sb("tB%d" % i, [128, TT + 16], F32) for i in range(2)])
        qkvT = [fw.sb("qkvT%d" % i, [128, TT], BF16) for i in range(3)]
        gateT = fw.sb("gateT", [128, TT], BF16)
        yaT = fw.sb("yaT", [128, TT], F32)
        pooledT = qkvT + [gateT]
        tokm = [fw.sb("tokm%d" % i, [128, NCH, 128], F32 if i < 2 else BF16) for i in range(3)]
        Sst = fw.sb("Sst", [128, NH, 128], F32)
        Stok = [Tok() for _ in range(NH)]
        c_qkv = fw.sb("c_qkv", [128, 48, 3], F32)
        c_pool = fw.sb("c_pool", [128, 16, 15], F32)
        c_ffn = fw.sb("c_ffn", [128, NFF, 2], F32)
        ab_sb = fw.sb("ab_sb", [128, NCH, 32], F32)
        beta = fw.sb("beta", [128, NCH, 16], F32)
        glog = fw.sb("glog", [128, NCH, 16], F32)
        gcs = fw.sb("gcs", [128, NCH, 16], F32)
        egt = fw.sb("egt", [128, NCH, 16], F32)
        ket = fw.sb("ket", [128, NCH, 16], F32)
        dect = fw.sb("dect", [128, NCH, 16], F32)
        negA = fw.sb("negA", [128, NCH, 16], F32)
        ts0 = fw.sb("ts0", [128, NCH, 16], F32)
        ts1 = fw.sb("ts1", [128, NCH, 16], F32)
        ssk = fw.sb("ssk", [128, NCH], F32)
        ssq = fw.sb("ssq", [128, NCH], F32)
        sc = {n: fw.sb("sc_" + n, [128, NCH], F32) for n in ("rk", "rq", "kg", "ke", "qg")}
        sso = RR([fw.sb("sso%d" % i, [128, 2], F32) for i in range(2)])
        junk = RR([fw.sb("junk%d" % i, [128, 128], F32) for i in range(2)])
        KI = 2
        lsets = []
        for i in range(2 * KI):
            lsets.append(dict(
                dgy=fw.sb("dgy%d" % i, [128, 4, 128], F32),
                k4=fw.sb("k4_%d" % i, [128, 4, 128], F32),
                ketok=fw.sb("ketok%d" % i, [128, 128], F32),
                sld=fw.sb("sld%d" % i, [128, 128], F32),
                dT=fw.sb("dT%d" % i, [128, 128], F32),
                dTi=fw.sb("dTi%d" % i, [128, 128], F32),
                dTs=fw.sb("dTs%d" % i, [128, 128], F32),
                attnT=fw.sb("attnT%d" % i, [128, 128], F32),
                ytA=fw.sb("ytA%d" % i, [128, 128], F32),
                ytB=fw.sb("ytB%d" % i, [128, 128], F32),
                Tfin=fw.sb("Tfin%d" % i, [128, 128], F32),
            ))
        rp = RR([fw.sb("rp%d" % i, [128, 128], F32) for i in range(2)])
        vnew = RR([fw.sb("vnew%d" % i, [128, 128], F32) for i in range(2)])
        onb = RR([fw.sb("onb%d" % i, [128, 128], BF16) for i in range(2)])
        eps_t = fw.sb("eps_t", [128, 1], F32)
        fw.memset("dve", eps_t[:, :], EPS)
        eps_c = eps_t[:, 0:1]

        pbank = [es.enter_context(nc.psum_tensor("pb%d" % i, [128, 512], F32)) for i in range(7)]
        pbf = es.enter_context(nc.psum_tensor("pbf", [128, 1024], BF16))
        _acc_banks = RR([Tile(pbank[0], Tok()), Tile(pbank[1], Tok()), Tile(pbank[2], Tok())])

        class _AccRR:
            def get(self):
                return [_acc_banks.get(), _acc_banks.get()]

        acc_rr = _AccRR()

        class PS:
            def __init__(self, t, lo, n, tok):
                self.t, self.lo, self.n, self.tok = t, lo, n, tok

            def v(self, a=0, b=None):
                b = self.n if b is None else b
                return V(self.t[:, self.lo + a:self.lo + b], [self.tok])

        g_rr = RR([PS(pbank[3], 0, 512, Tok()), PS(pbank[4], 0, 512, Tok()), PS(pbank[5], 0, 512, Tok()),
                   PS(pbank[6], 0, 512, Tok())])
        pair_rr = g_rr
        g_banks = g_rr.items
        sing_rr = g_rr
        pbf_ps = PS(pbf, 0, 1024, Tok())

        def sm(off, n=1):
            return small[:, off:off + n]

        def load_small(l):
            fw.dma("sp", small[:, :], V(small_d.ap()[l], [t_small]), small.tok)
            fw.dma("pool", wabb[:, :], V(wab_d.ap()[l], [t_wab]), wabb.tok)
            fw.act(ts0.re("p a b -> p (a b)"), sm(O_ALOG, 128), AF.Exp)
            fw.ts("dve", negA.re("p a b -> p (a b)"), ts0.re("p a b -> p (a b)"), -1.0, ALU.mult)

        def x_src(l, phase, ti, kc):
            cols = slice(ti * TT, (ti + 1) * TT)
            if l == 0 and phase == 0:
                return V(xT_d.ap()[kc * 128:(kc + 1) * 128, cols], [t_xin])
            return V(yT_d.ap()[kc * 128:(kc + 1) * 128, cols], [t_y[ti][kc]])

        def y_dst(ti, kc):
            cols = slice(ti * TT, (ti + 1) * TT)
            return V(yT_d.ap()[kc * 128:(kc + 1) * 128, cols], [t_y[ti][kc]])

        def rmsnorm_to_hT(l, phase, ti, woff):
            acc = acc_rr.get()
            for kc in range(KC):
                xb = xbuf.get()
                fw.dma("sp", xb[:, :], x_src(l, phase, ti, kc), xb.tok)
                sq = sqb.get()
                fw.act(sq[:, :], xb[:, :], AF.Square)
                for hf in range(2):
                    fw.mm(acc[hf][:, :], onesb, sq[:, hf * 512:(hf + 1) * 512], start=(kc == 0), stop=(kc == KC - 1))
            for hf in range(2):
                fw.act(rstd[:, hf * 512:(hf + 1) * 512], acc[hf][:, :], AF.Ln, scale=1.0 / D, bias=eps_c)
                fw.act(rstd[:, hf * 512:(hf + 1) * 512], rstd[:, hf * 512:(hf + 1) * 512], AF.Exp, scale=-0.5)
            for kc in range(KC):
                xb = xbuf.get()
                fw.dma("sp", xb[:, :], x_src(l, phase, ti, kc), xb.tok)
                fw.stt("dve", hT[kc][:, :], xb[:, :], sm(woff + kc), rstd[:, :], ALU.mult, ALU.mult)

        def proj(wsrc, wtok, buf_rr=None, nk=KC, rhs_tiles=None):
            buf_rr = buf_rr or wbuf
            rhs_tiles = rhs_tiles or hT
            wb = buf_rr.get()
            fw.dma("pool", wb[:, :], V(wsrc, [wtok]), wb.tok)
            acc = acc_rr.get()
            for hf in range(2):
                for kc in range(nk):
                    fw.mm(acc[hf][:, :], wb[:, kc * 128:(kc + 1) * 128], rhs_tiles[kc][:, hf * 512:(hf + 1) * 512],
                          start=(kc == 0), stop=(kc == nk - 1))
            return acc

        def evac2(eng_fn, acc):
            for hf in range(2):
                eng_fn(hf, acc[hf][:, :], slice(hf * 512, (hf + 1) * 512))

        def token_scalars():
            flat = lambda t: t.re("p a b -> p (a b)")
            for c in range(NCH):
                ps = pair_rr.get()
                for kc in range(KC):
                    fw.mm(ps.v(0, 32), hT[kc][:, c * 128:(c + 1) * 128], wabb[:, kc * 32:(kc + 1) * 32],
                          start=(kc == 0), stop=(kc == KC - 1))
                fw.copy("act", ab_sb[:, c, :], ps.v(0, 32))
            checkpoint("tsc1")
            fw.act(beta[:, :, :], ab_sb[:, :, 0:16], AF.Sigmoid)
            fw.tt("dve", ts0[:, :, :], ab_sb[:, :, 16:32],
                  V(small.t[:, O_DTB:O_DTB + 128].rearrange("p (a b) -> p a b", b=16), [small.tok]), ALU.add)
            fw.act(ts1[:, :, :], ts0[:, :, :], AF.Exp)
            fw.act(ts0[:, :, :], ts1[:, :, :], AF.Ln, bias=1.0)
            fw.tt("dve", glog[:, :, :], ts0[:, :, :], negA[:, :, :], ALU.mult)
            checkpoint("tsc2")
            p1 = pair_rr.get()
            fw.mm(p1.v(0, 128), triuf, flat(glog))
            p2 = pair_rr.get()
            fw.mm(p2.v(0, 128), onesf, flat(glog))
            checkpoint("tsc3")
            fw.copy("act", flat(gcs), p1.v(0, 128))
            fw.act(flat(egt), p1.v(0, 128), AF.Exp)
            fw.tt("dve", flat(ts0), p2.v(0, 128), flat(gcs), ALU.subtract)
            fw.act(flat(ket), flat(ts0), AF.Exp)
            fw.act(flat(dect), p2.v(0, 128), AF.Exp)

        def interleave(gens):
            gens = list(gens)
            while gens:
                for g_ in list(gens):
                    try:
                        next(g_)
                    except StopIteration:
                        gens.remove(g_)

        def chain(*gens):
            for g_ in gens:
                yield from g_

        def gdn_local(h, c, B, pb):
            ktok, qtok, vtok = tokm
            col = lambda t: V(t.t[:, c:c + 1], [t.tok])
            colh = lambda t: V(t.t[:, c, h:h + 1], [t.tok])
            d = B["dgy"]
            fw.ts("dve", d[:, 0, :], identf, col(sc["rk"]), ALU.mult); yield
            fw.ts("dve", d[:, 1, :], identf, col(sc["kg"]), ALU.mult); yield
            fw.act(d[:, 2, :], identf, AF.Copy, scale=col(sc["rq"])); yield
            fw.act(d[:, 3, :], identf, AF.Copy, scale=col(sc["qg"])); yield
            k4 = B["k4"]
            p = PS(pb.t, 0, 256, pb.tok)
            fw.mm(p.v(0, 256), ktok[:, c, :], V(d.t[:, 0:2, :].rearrange("p a b -> p (a b)"), [d.tok])); yield
            fw.copy("act", V(k4.t[:, 0::2, :], [k4.tok]), V(p.t[:, p.lo:p.lo + 256].rearrange("p (a b) -> p a b", b=128), [p.tok])); yield
            p = PS(pb.t, 0, 256, pb.tok)
            fw.mm(p.v(0, 256), qtok[:, c, :], V(d.t[:, 2:4, :].rearrange("p a b -> p (a b)"), [d.tok])); yield
            fw.copy("act", V(k4.t[:, 1::2, :], [k4.tok]), V(p.t[:, p.lo:p.lo + 256].rearrange("p (a b) -> p a b", b=128), [p.tok])); yield
            if c == 0: checkpoint("La")
            ke_t = B["ketok"]
            fw.act(ke_t[:, :], ktok[:, c, :], AF.Copy, scale=col(sc["ke"])); yield
            sl = B["sld"]
            fw.ts("dve", sl[:, :], slf, colh(glog), ALU.mult); yield
            pd = PS(pb.t, 0, 128, pb.tok)
            fw.mm(pd.v(0, 128), sl[:, :], triuf); yield
            dT = B["dT"]
            fw.act(dT[:, :], pd.v(0, 128), AF.Exp); yield
            di = B["dTi"]
            ds_ = B["dTs"]
            fw.tt("dve", di[:, :], dT[:, :], V(cf.t[:, C_TRIU:C_TRIU + 128], [cf.tok]), ALU.mult); yield
            fw.tt("pool", ds_[:, :], dT[:, :], V(cf.t[:, C_SU:C_SU + 128], [cf.tok]), ALU.mult); yield
            pg = PS(pb.t, 0, 256, pb.tok)
            fw.mm(pg.v(0, 256), k4[:, 0, :], V(k4.t[:, 0:2, :].rearrange("p a b -> p (a b)"), [k4.tok])); yield
            at = B["attnT"]
            fw.tt("dve", at[:, :], pg.v(128, 256), di[:, :], ALU.mult); yield
            if c == 0: checkpoint("Lb")
            ymA = V(d.t[:, 0:2, :], [d.tok])
            ymB = V(d.t[:, 2:4, :], [d.tok])
            half = lambda ym, i: V(ym.ap[:, i, :], ym.toks)
            flat2 = lambda ym: V(ym.ap.rearrange("p a b -> p (a b)"), ym.toks)
            fw.stt("dve", half(ymA, 0), pg.v(0, 128), colh(beta), ds_[:, :], ALU.mult, ALU.mult); yield
            fw.tt("dve", half(ymB, 1), identf, half(ymA, 0), ALU.subtract); yield
            pn = PS(pb.t, 0, 128, pb.tok)
            fw.mm(pn.v(0, 128), half(ymA, 0), identf); yield
            ytA, ytB = B["ytA"], B["ytB"]
            fw.copy("act", ytA[:, :], pn.v(0, 128)); yield
            if c == 0: checkpoint("Lc")
            ym, ym2, yt, yt2 = ymA, ymB, ytA, ytB
            for k in range(7):
                last = (k == 6)
                first = (k == 0)
                pm = PS(pb.t, 0, 256, pb.tok)
                if first:
                    fw.mm(pm.v(0, 128), yt[:, :], half(ym, 0)); yield
                elif last:
                    fw.mm(pm.v(128, 256), yt[:, :], half(ym, 1)); yield
                else:
                    fw.mm(pm.v(0, 256), yt[:, :], flat2(ym)); yield
                if not last:
                    fw.copy("act", half(ym2, 0), pm.v(0, 128)); yield
                    if not first:
                        fw.tt("dve", half(ym2, 1), pm.v(128, 256), half(ym, 1), ALU.add); yield
                    p2_ = PS(pb.t, 0, 128, pb.tok)
                    fw.mm(p2_.v(0, 128), half(ym, 0), yt[:, :]); yield
                    fw.copy("act", yt2[:, :], p2_.v(0, 128)); yield
                    ym, ym2, yt, yt2 = ym2, ym, yt2, yt
                else:
                    tf = B["Tfin"]
                    fw.tt("dve", tf[:, :], pm.v(128, 256), half(ym, 1), ALU.add); yield

        def gdn_seq(h, c, B, pbX, pbY):
            ktok, qtok, vtok = tokm
            cs = slice(c * 128, (c + 1) * 128)
            colh = lambda t: V(t.t[:, c, h:h + 1], [t.tok])
            Sh = V(Sst.t[:, h, :], [Stok[h]])
            k4, ke_t, at, tf = B["k4"], B["ketok"], B["attnT"], B["Tfin"]
            if c == 0: checkpoint("Qa")
            pr = PS(pbX.t, 0, 128, pbX.tok)
            fw.mm(pr.v(0, 128), k4[:, 2, :], Sh); yield
            r_ = rp.get()
            fw.tt("dve", r_[:, :], vtok[:, c, :], pr.v(0, 128), ALU.subtract); yield
            pv_ = PS(pbY.t, 0, 128, pbY.tok)
            fw.mm(pv_.v(0, 128), tf[:, :], r_[:, :]); yield
            vn = vnew.get()
            fw.act(vn[:, :], pv_.v(0, 128), AF.Copy, scale=colh(beta)); yield
            po = PS(pbX.t, 0, 128, pbX.tok)
            fw.mm(po.v(0, 128), k4[:, 3, :], Sh, start=True, stop=False)
            fw.mm(po.v(0, 128), at[:, :], vn[:, :], start=False, stop=True); yield
            pS = PS(pbY.t, 0, 128, pbY.tok)
            fw.mm(pS.v(0, 128), ke_t[:, :], vn[:, :]); yield
            fw.stt("dve", Sh, Sh, colh(dect), pS.v(0, 128), ALU.mult, ALU.add); yield
            if c == 0: checkpoint("Qb")
            so = sso.get()
            fw.act(junk.get()[:, :], po.v(0, 128), AF.Square, accum=so[:, 0:1]); yield
            fw.act(so[:, 1:2], so[:, 0:1], AF.Ln, scale=1.0 / HD, bias=eps_c); yield
            fw.act(so[:, 1:2], so[:, 1:2], AF.Exp, scale=-0.5); yield
            ob = onb.get()
            fw.act(ob[:, :], po.v(0, 128), AF.Copy, scale=so[:, 1:2]); yield
            fw.tr(pbf_ps.v(128, 256), ob[:, :], identb); yield
            fw.stt("dve", yaT[:, cs], pbf_ps.v(128, 256), sm(O_GNW), gateT[:, cs], ALU.mult, ALU.mult); yield

        def gdn_head(l, h, jslot):
            kT, qT, vT = qkvT[1], qkvT[0], qkvT[2]
            ktok, qtok, vtok = tokm
            for which, (srcT, dst, ss) in enumerate(((kT, ktok, ssk), (qT, qtok, ssq), (vT, vtok, None))):
                for c in range(NCH):
                    fw.tr(pbf_ps.v(c * 128, (c + 1) * 128), srcT[:, c * 128:(c + 1) * 128], identb)
                if ss is not None:
                    sqt = tmpB.get()
                    fw.act(sqt[:, 0:TT], pbf_ps.v(0, TT), AF.Square)
                    sq3 = V(sqt.t[:, 0:TT].rearrange("p (c d) -> p c d", d=128), [sqt.tok])
                    fw.op("dve", lambda e: e.tensor_reduce(out=ss.t[:, :], in_=sq3.ap, axis=mybir.AxisListType.X, op=ALU.add),
                          reads=sq3.toks, writes=[ss.tok])
                fw.copy("dve" if which != 1 else "act", dst.re("p c d -> p (c d)"), pbf_ps.v(0, TT))
            checkpoint("gdnA%d" % h)
            hs = lambda t: V(t.t[:, :, h], [t.tok])
            fw.act(sc["rk"][:, :], ssk[:, :], AF.Ln, bias=eps_c)
            fw.act(sc["rk"][:, :], sc["rk"][:, :], AF.Exp, scale=-0.5)
            fw.act(sc["rq"][:, :], ssq[:, :], AF.Ln, bias=eps_c)
            fw.act(sc["rq"][:, :], sc["rq"][:, :], AF.Exp, scale=-0.5)
            fw.ts("dve", sc["rq"][:, :], sc["rq"][:, :], float(HD) ** -0.5, ALU.mult)
            fw.tt("dve", sc["kg"][:, :], sc["rk"][:, :], hs(egt), ALU.mult)
            fw.tt("dve", sc["ke"][:, :], sc["rk"][:, :], hs(ket), ALU.mult)
            fw.tt("dve", sc["qg"][:, :], sc["rq"][:, :], hs(egt), ALU.mult)
            L_ = lambda c: gdn_local(h, c, lsets[c % (2 * KI)], g_banks[c % KI])
            Q_ = lambda c: gdn_seq(h, c, lsets[c % (2 * KI)], g_banks[KI], g_banks[KI + 1])
            groups = [list(range(c0, c0 + KI)) for c0 in range(0, NCH, KI)]
            interleave([L_(c) for c in groups[0]])
            for gi in range(1, len(groups)):
                interleave([L_(c) for c in groups[gi]] + [chain(*[Q_(c) for c in groups[gi - 1]])])
            interleave([chain(*[Q_(c) for c in groups[-1]])])
            fw.tt("pool", slots[jslot][:, :], yaT[:, :], slots[jslot][:, :], ALU.add)
            checkpoint("headdone%d" % h)

        def mixer(l, ti):
            rmsnorm_to_hT(l, 0, ti, O_NMIX)
            checkpoint("rms")
            token_scalars()
            checkpoint("tsc")
            ct = 0
            for g in range(4):
                win_ = 2 ** (g + 1)
                for jj in range(4):
                    j = 4 * g + jj
                    acc = proj(win_d.ap()[l, ct], t_win); ct += 1
                    evac2(lambda hf, ps, sl: fw.act(slots[j][:, sl], ps, AF.Sigmoid), acc)
                for jj in range(4):
                    j = 4 * g + jj
                    acc = proj(win_d.ap()[l, ct], t_win); ct += 1
                    U = tmpA.get()
                    fw.copy("act", U[:, 1:16], c_pool[:, j, :])
                    fw.memset("dve", U[:, 0:1], 0.0)
                    evac2(lambda hf, ps, sl: fw.copy("act", U[:, 16 + sl.start:16 + sl.stop], ps), acc)
                    fw.copy("act", c_pool[:, j, :], U[:, TT + 1:TT + 16])
                    cur = U
                    sh = 1
                    e0 = 1
                    while sh < win_:
                        nxt = tmpB.get()
                        fw.tt("pool" if sh > 1 else "dve", nxt[:, e0:TT + 16], cur[:, e0:TT + 16], cur[:, e0 - sh:TT + 16 - sh], ALU.add)
                        cur = nxt
                        sh *= 2
                        e0 = 2 * sh - 1
                    fw.stt("dve", pooledT[jj][:, :], cur[:, 16:TT + 16], 1.0 / win_, U[:, 16:TT + 16], ALU.mult, ALU.subtract)
                    if ti == 0:
                        tq = junk.get()
                        fw.tt("dve", tq[:, 0:16], cur[:, 16:32], cf[:, C_ICNT + g * 16:C_ICNT + g * 16 + 16], ALU.mult)
                        fw.tt("dve", pooledT[jj][:, 0:16], tq[:, 0:16], U[:, 16:32], ALU.subtract)
                for ot in range(4):
                    j = 4 * g + ot
                    acc = proj(wpool_d.ap()[l, g, ot], t_wpool, buf_rr=wpbuf, nk=4, rhs_tiles=pooledT)
                    evac2(lambda hf, ps, sl: fw.stt("dve", slots[j][:, sl], ps, sm(O_PSC + j), slots[j][:, sl], ALU.mult, ALU.mult), acc)
                checkpoint("pool%d" % g)
                for h in range(4 * g, 4 * g + 4):
                    for which in range(3):
                        cti = which * 16 + h
                        acc = proj(win_d.ap()[l, ct], t_win); ct += 1
                        R = tmpA.get()
                        A = tmpB.get()
                        fw.copy("act", R[:, 0:3], c_qkv[:, cti, :])
                        wq = lambda jtap: sm(O_CQKV + cti * 4 + jtap)
                        evac2(lambda hf, ps, sl: fw.copy("act", R[:, 3 + sl.start:3 + sl.stop], ps), acc)
                        evac2(lambda hf, ps, sl: fw.act(A[:, sl], ps, AF.Copy, scale=wq(3)), acc)
                        fw.copy("act", c_qkv[:, cti, :], R[:, TT:TT + 3])
                        fw.stt("dve", A[:, 0:TT], R[:, 2:TT + 2], wq(2), A[:, 0:TT], ALU.mult, ALU.add)
                        fw.stt("dve", A[:, 0:TT], R[:, 1:TT + 1], wq(1), A[:, 0:TT], ALU.mult, ALU.add)
                        fw.stt("dve", A[:, 0:TT], R[:, 0:TT], wq(0), A[:, 0:TT], ALU.mult, ALU.add)
                        fw.act(qkvT[which][:, :], A[:, 0:TT], AF.Silu)
                    accz = proj(win_d.ap()[l, ct], t_win); ct += 1
                    Z = tmpB.get()
                    evac2(lambda hf, ps, sl: fw.act(Z[:, sl], ps, AF.Silu), accz)
                    accg = proj(win_d.ap()[l, ct], t_win); ct += 1
                    evac2(lambda hf, ps, sl: fw.act(gateT[:, sl], ps, AF.Sigmoid), accg)
                    fw.tt("pool", gateT[:, :], Z[:, 0:TT], gateT[:, :], ALU.mult)
                    checkpoint("qkv%d" % h)
                    gdn_head(l, h, h)
                    checkpoint("head%d" % h)
            assert ct == 112
            for j in range(16):
                acc = proj(wout_d.ap()[l, j], t_wout, rhs_tiles=slots)
                xb = xbuf.get()
                fw.dma("sp", xb[:, :], x_src(l, 0, ti, j), xb.tok)
                evac2(lambda hf, ps, sl: fw.tt("dve", xb[:, sl], ps, xb[:, sl], ALU.add), acc)
                fw.dma("sp", y_dst(ti, j), xb[:, :], st_rr.get())

        def ffn(l, ti):
            rmsnorm_to_hT(l, 1, ti, O_NFFN)
            for hh in range(NQ):
                for ff in range(FH):
                    f = hh * FH + ff
                    accg = proj(wup_d.ap()[l, 2 * f], t_wup)
                    R = tmpA.get()
                    A = tmpB.get()
                    fw.copy("act", R[:, 0:2], c_ffn[:, f, :])
                    wf = lambda jtap: sm(O_CFW + f * 3 + jtap)
                    evac2(lambda hf, ps, sl: fw.copy("act", R[:, 2 + sl.start:2 + sl.stop], ps), accg)
                    evac2(lambda hf, ps, sl: fw.act(A[:, sl], ps, AF.Identity, scale=wf(2), bias=sm(O_CFB + f)), accg)
                    fw.copy("act", c_ffn[:, f, :], R[:, TT:TT + 2])
                    fw.stt("dve", A[:, 0:TT], R[:, 1:TT + 1], wf(1), A[:, 0:TT], ALU.mult, ALU.add)
                    fw.stt("dve", A[:, 0:TT], R[:, 0:TT], wf(0), A[:, 0:TT], ALU.mult, ALU.add)
                    Gl = tmpB.get()
                    fw.act(Gl[:, 0:TT], A[:, 0:TT], AF.Gelu)
                    accu = proj(wup_d.ap()[l, 2 * f + 1], t_wup)
                    evac2(lambda hf, ps, sl: fw.tt("dve", slots[ff][:, sl], ps, Gl[:, sl], ALU.mult), accu)
                for j in range(16):
                    acc = proj(wdn_d.ap()[l, hh, j], t_wdn, buf_rr=wdbuf, nk=FH, rhs_tiles=slots)
                    xb = xbuf.get()
                    fw.dma("sp", xb[:, :], x_src(l, 1, ti, j), xb.tok)
                    evac2(lambda hf, ps, sl: fw.tt("dve", xb[:, sl], ps, xb[:, sl], ALU.add), acc)
                    fw.dma("sp", y_dst(ti, j), xb[:, :], st_rr.get())

        def final_norm(ti):
            acc = acc_rr.get()
            for kc in range(KC):
                xb = xbuf.get()
                fw.dma("sp", xb[:, :], x_src(L, 0, ti, kc), xb.tok)
                sq = sqb.get()
                fw.act(sq[:, :], xb[:, :], AF.Square)
                for hf in range(2):
                    fw.mm(acc[hf][:, :], onesb, sq[:, hf * 512:(hf + 1) * 512], start=(kc == 0), stop=(kc == KC - 1))
            for hf in range(2):
                fw.act(rstd[:, hf * 512:(hf + 1) * 512], acc[hf][:, :], AF.Ln, scale=1.0 / D, bias=eps_c)
                fw.act(rstd[:, hf * 512:(hf + 1) * 512], rstd[:, hf * 512:(hf + 1) * 512], AF.Exp, scale=-0.5)
            for kc in range(KC):
                xb = xbuf.get()
                fw.dma("sp", xb[:, :], x_src(L, 0, ti, kc), xb.tok)
                fw.stt("dve", xb[:, :], xb[:, :], cf[:, C_NFIN + kc:C_NFIN + kc + 1], rstd[:, :],
                       ALU.mult, ALU.mult)
                fw.dma("sp", y_dst(ti, kc), xb[:, :], st_rr.get())

        def dump(name, v, width):
            dd = dr("dbg_" + name, [128, width], kind="ExternalOutput")
            tk = fw.dtok("dbg_" + name)
            fw.dma("pool", V(dd.ap()[:, :], [Tok()]), v, tk)
            dbg_list.append(tk)

        build_program.dump = dump
        build_program.env = locals()
        try:
          for l in range(L):
            load_small(l)
            fw.memset("dve", Sst[:, :, :], 0.0)
            for h in range(NH):
                Stok[h].w = Sst.tok.w
            fw.memset("dve", c_qkv[:, :, :], 0.0)
            fw.memset("dve", c_pool[:, :, :], 0.0)
            fw.memset("dve", c_ffn[:, :, :], 0.0)
            for ti in range(NT):
                mixer(l, ti)
                checkpoint("mixer")
                ffn(l, ti)
                checkpoint("ffn")
          for ti in range(NT):
            final_norm(ti)
        except StopBuild:
            if build_program.on_stop:
                build_program.on_stop(locals())
        for t in st_tok + dbg_list:
            if t.dcnt:
                fw._deps("sp", [], [])
                fw.eng["sp"].wait_ge(t.dsem, t.dcnt)
        for e in ("pe", "act", "dve", "pool"):
            if fw.cnt[e]:
                fw.eng["sp"].wait_ge(fw.sem[e], fw.cnt[e])
        print("program: ins=%d waits=%d" % (fw.nins, fw.nwait), {e: fw.cnt[e] for e in fw.cnt})
    return nc


build_program.on_stop = None
_CACHE = {}


def run(inputs, L, S, ncores, stop_at=None, ret_all=False):
    B = inputs["x"].shape[0]
    pw = prep_weights(inputs, L)
    key = (L, S, stop_at)
    if key not in _CACHE:
        _CACHE[key] = build_program(L, S, stop_at=stop_at)
    nc = _CACHE[key]
    in_maps = []
    for c in range(ncores):
        b = c % B
        m = {"xT": np.ascontiguousarray(inputs["x"][b, :S].T)}
        m.update(pw)
        in_maps.append(m)
    res = run_bass_kernel_spmd(nc, in_maps, core_ids=list(range(ncores)))
    if ret_all:
        return res.results
    out = np.stack([np.ascontiguousarray(res.results[b]["yT"].T) for b in range(B)], axis=0)
    return out.astype(np.float32)


def kernel(**inputs):
    inputs = {k: np.asarray(v, dtype=np.float32) for k, v in inputs.items()}
    return run(inputs, 4, 4096, 8)
```

```python
import numpy as np
from contextlib import ExitStack
import concourse.bass as bass
import concourse.mybir as mybir
from concourse.bass_utils import run_bass_kernel_spmd

F32 = mybir.dt.float32
BF16 = mybir.dt.bfloat16
AF = mybir.ActivationFunctionType
ALU = mybir.AluOpType

D = 2048
NH = 16
HD = 128
DFF = 5632
NFF = DFF // 128
DIN = 4 * D + 2 * NH + D + 2 * D
EPS = 1e-6
TT = 1024
NCH = TT // 128
KC = D // 128
NQ = 4
FH = NFF // NQ


class Tok:
    __slots__ = ("w", "r", "dsem", "dcnt", "name")

    def __init__(self, name=""):
        self.w = None
        self.r = {}
        self.dsem = None
        self.dcnt = 0
        self.name = name


class V:
    __slots__ = ("ap", "toks")

    def __init__(self, ap, toks):
        self.ap = ap
        self.toks = toks


class Tile:
    def __init__(self, t, tok):
        self.t = t
        self.tok = tok

    def __getitem__(self, idx):
        return V(self.t[idx], [self.tok])

    def re(self, pat, **kw):
        return V(self.t[:].rearrange(pat, **kw), [self.tok])


class FW:
    def __init__(self, nc, es):
        self.nc = nc
        self.es = es
        self.eng = {"pe": nc.tensor, "act": nc.scalar, "dve": nc.vector, "pool": nc.gpsimd, "sp": nc.sync}
        self.sem = {e: es.enter_context(nc.semaphore("s_" + e)) for e in self.eng}
        self.cnt = {e: 0 for e in self.eng}
        self.seen = {e: {} for e in self.eng}
        self.nwait = 0
        self.nins = 0
        self.ntile = 0
        self.dtoks = []

    def sb(self, name, shape, dt, dma=False):
        t = self.es.enter_context(self.nc.sbuf_tensor("sb_" + name, list(shape), dt))
        tok = self.dtok(name) if dma else Tok(name)
        return Tile(t, tok)

    def dtok(self, name):
        t = Tok(name)
        t.dsem = self.es.enter_context(self.nc.semaphore("d_" + name))
        self.dtoks.append(t)
        return t

    def _deps(self, e, reads, writes):
        deps = []
        for t in reads:
            if t.w is not None:
                deps.append(t.w)
        for t in writes:
            if t.w is not None:
                deps.append(t.w)
            deps.extend(t.r.values())
        mysem = self.sem[e]
        seen = self.seen[e]
        for (sem, val) in deps:
            if e == "pe" and sem is mysem:
                continue
            if seen.get(sem, 0) >= val:
                continue
            self.eng[e].wait_ge(sem, val)
            seen[sem] = val
            self.nwait += 1

    def op(self, e, build, reads=(), writes=()):
        self._deps(e, reads, writes)
        ins = build(self.eng[e])
        self.cnt[e] += 1
        self.nins += 1
        ins.then_inc(self.sem[e], 1)
        me = (self.sem[e], self.cnt[e])
        s = self.sem[e]
        for t in reads:
            t.r[s] = me
        for t in writes:
            t.w = me
            t.r = {}
        return ins

    def dma(self, q, out, in_, tok):
        reads, writes = in_.toks, out.toks
        self._deps(q, reads, writes)
        if tok.dcnt > 0 and self.seen[q].get(tok.dsem, 0) < tok.dcnt:
            self.eng[q].wait_ge(tok.dsem, tok.dcnt)
            self.seen[q][tok.dsem] = tok.dcnt
            self.nwait += 1
        ins = self.eng[q].dma_start(out=out.ap, in_=in_.ap)
        tok.dcnt += 16
        ins.then_inc(tok.dsem, 16)
        self.nins += 1
        me = (tok.dsem, tok.dcnt)
        for t in reads:
            t.r[tok.dsem] = me
        for t in writes:
            t.w = me
            t.r = {}
        return ins

    def wait_tok(self, e, tok):
        self._deps(e, [tok], [])

    def mm(self, out, lhsT, rhs, start=True, stop=True):
        return self.op("pe", lambda e: e.matmul(out.ap, lhsT=lhsT.ap, rhs=rhs.ap, start=start, stop=stop),
                       reads=lhsT.toks + rhs.toks, writes=out.toks)

    def tr(self, out, in_, ident):
        return self.op("pe", lambda e: e.transpose(out.ap, in_.ap, ident.ap),
                       reads=in_.toks + ident.toks, writes=out.toks)

    def act(self, out, in_, func, bias=None, scale=None, accum=None, eng="act"):
        kw = {}
        reads = list(in_.toks)
        writes = list(out.toks)
        if bias is not None:
            if isinstance(bias, V):
                kw["bias"] = bias.ap
                reads += bias.toks
            else:
                kw["bias"] = bias
        if scale is not None:
            if isinstance(scale, V):
                kw["scale"] = scale.ap
                reads += scale.toks
            else:
                kw["scale"] = scale
        if accum is not None:
            kw["accum_out"] = accum.ap
            writes += accum.toks
        return self.op(eng, lambda e: e.activation(out=out.ap, in_=in_.ap, func=func, **kw), reads=reads, writes=writes)

    def tt(self, eng, out, in0, in1, op):
        return self.op(eng, lambda e: e.tensor_tensor(out=out.ap, in0=in0.ap, in1=in1.ap, op=op),
                       reads=in0.toks + in1.toks, writes=out.toks)

    def ts(self, eng, out, in0, s1, op0, s2=None, op1=None, accum=None):
        reads = list(in0.toks)
        writes = list(out.toks)
        a1 = s1
        if isinstance(s1, V):
            a1 = s1.ap
            reads += s1.toks
        a2 = s2
        if isinstance(s2, V):
            a2 = s2.ap
            reads += s2.toks
        kw = {}
        if op1 is not None:
            kw["op1"] = op1
        if accum is not None:
            kw["accum_out"] = accum.ap
            writes += accum.toks
        return self.op(eng, lambda e: e.tensor_scalar(out=out.ap, in0=in0.ap, scalar1=a1, scalar2=a2, op0=op0, **kw),
                       reads=reads, writes=writes)

    def stt(self, eng, out, in0, scalar, in1, op0, op1):
        reads = in0.toks + in1.toks
        a = scalar
        if isinstance(scalar, V):
            a = scalar.ap
            reads = reads + scalar.toks
        return self.op(eng, lambda e: e.scalar_tensor_tensor(out=out.ap, in0=in0.ap, scalar=a, in1=in1.ap, op0=op0, op1=op1),
                       reads=reads, writes=out.toks)

    def copy(self, eng, out, in_):
        if eng == "act":
            return self.act(out, in_, AF.Copy)
        return self.op(eng, lambda e: e.tensor_copy(out=out.ap, in_=in_.ap), reads=in_.toks, writes=out.toks)

    def memset(self, eng, out, val):
        return self.op(eng, lambda e: e.memset(out.ap, val), reads=[], writes=out.toks)


class RR:
    def __init__(self, items):
        self.items = items
        self.i = 0

    def get(self):
        it = self.items[self.i % len(self.items)]
        self.i += 1
        return it


O_NMIX = 0
O_NFFN = 16
O_PSC = 32
O_CQKV = 48
O_CFW = O_CQKV + 192
O_CFB = O_CFW + 132
O_GNW = O_CFB + 44
O_ALOG = O_GNW + 1
O_DTB = O_ALOG + 128
NSM = O_DTB + 128
C_ID = 0
C_TRIU = 128
C_SU = 256
C_SL = 384
C_ONE = 512
C_ICNT = 640
C_NFIN = 704
NCONST = 720


def _colmajor(w, ncol_tiles):
    K = w.shape[0]
    return np.ascontiguousarray(w.reshape(K // 128, 128, ncol_tiles, 128).transpose(2, 1, 0, 3)).reshape(
        ncol_tiles, 128, (K // 128) * 128)


def prep_weights(inp, L):
    out = {}
    w_in = inp["w_in"]
    order = []
    for g in range(4):
        for j in range(4 * g, 4 * g + 4):
            order.append(4 * D + 2 * NH + D + D + j * 128)
        for j in range(4 * g, 4 * g + 4):
            order.append(4 * D + 2 * NH + j * 128)
        for h in range(4 * g, 4 * g + 4):
            order.append(0 * D + h * 128)
            order.append(1 * D + h * 128)
            order.append(2 * D + h * 128)
            order.append(3 * D + h * 128)
            order.append(4 * D + 2 * NH + D + h * 128)
    cols = np.concatenate([np.arange(o, o + 128) for o in order])
    win = np.empty((L, len(order), 128, D), np.float32)
    wab = np.empty((L, 128, KC * 32), np.float32)
    wout = np.empty((L, 16, 128, D), np.float32)
    wup = np.empty((L, 2 * NFF, 128, D), np.float32)
    wdn = np.empty((L, NQ, 16, 128, FH * 128), np.float32)
    wpool = np.empty((L, 4, 4, 128, 512), np.float32)
    small = np.zeros((L, 128, NSM), np.float32)
    upcols = np.concatenate([np.concatenate([np.arange(f * 128, f * 128 + 128), np.arange(DFF + f * 128, DFF + f * 128 + 128)])
                             for f in range(NFF)])
    for l in range(L):
        win[l] = _colmajor(w_in[l][:, cols], len(order))
        ab = w_in[l][:, 4 * D:4 * D + 32]
        wab[l] = ab.reshape(KC, 128, 32).transpose(1, 0, 2).reshape(128, KC * 32)
        wout[l] = _colmajor(inp["w_out"][l], 16)
        wup[l] = _colmajor(inp["w_up"][l][:, upcols], 2 * NFF)
        wd = inp["w_down"][l]
        for hh in range(NQ):
            blk = wd[hh * FH * 128:(hh + 1) * FH * 128]
            wdn[l, hh] = _colmajor(blk, 16)
        for g in range(4):
            wpool[l, g] = _colmajor(inp["pool_w"][l, g], 4)
        sm = small[l]
        sm[:, O_NMIX:O_NMIX + 16] = inp["norm_mix_w"][l].reshape(16, 128).T
        sm[:, O_NFFN:O_NFFN + 16] = inp["norm_ffn_w"][l].reshape(16, 128).T
        sm[:, O_PSC:O_PSC + 16] = inp["pool_scale"][l].reshape(16, 128).T
        sm[:, O_CQKV:O_CQKV + 192] = inp["conv_qkv_w"][l].reshape(4, 48, 128).transpose(2, 1, 0).reshape(128, 192)
        sm[:, O_CFW:O_CFW + 132] = inp["conv_ffn_w"][l].reshape(3, NFF, 128).transpose(2, 1, 0).reshape(128, 132)
        sm[:, O_CFB:O_CFB + 44] = inp["conv_ffn_b"][l].reshape(NFF, 128).T
        sm[:, O_GNW] = inp["gdn_norm_w"][l]
        sm[:, O_ALOG:O_ALOG + 128] = np.tile(inp["a_log"][l], NCH)[None, :]
        sm[:, O_DTB:O_DTB + 128] = np.tile(inp["dt_bias"][l], NCH)[None, :]
    out.update(win=win, wab=wab, wout=wout, wup=wup, wdn=wdn, wpool=wpool, small=small)
    c = np.zeros((128, NCONST), np.float32)
    idx = np.arange(128)
    c[:, C_ID:C_ID + 128] = np.eye(128, dtype=np.float32)
    c[:, C_TRIU:C_TRIU + 128] = (idx[:, None] <= idx[None, :])
    c[:, C_SU:C_SU + 128] = (idx[None, :] > idx[:, None])
    c[:, C_SL:C_SL + 128] = (idx[:, None] > idx[None, :])
    c[:, C_ONE:C_ONE + 128] = 1.0
    for g, win_ in enumerate((2, 4, 8, 16)):
        t = np.arange(16)
        c[:, C_ICNT + g * 16:C_ICNT + g * 16 + 16] = (np.float32(1.0) / np.minimum(t + 1, win_).astype(np.float32))[None, :]
    c[:, C_NFIN:C_NFIN + 16] = inp["norm_final_w"].reshape(16, 128).T
    out["consts"] = c
    return out


class StopBuild(Exception):
    pass


def build_program(L, S, dbg=None, stop_at=None):
    assert S % TT == 0
    NT = S // TT
    nc = bass.Bass("TRN2", target_bir_lowering=False)
    dr = lambda name, shape, kind="ExternalInput": nc.dram_tensor(name, list(shape), F32, kind=kind)
    xT_d = dr("xT", [D, S])
    win_d = dr("win", [L, 112, 128, D])
    wab_d = dr("wab", [L, 128, KC * 32])
    wout_d = dr("wout", [L, 16, 128, D])
    wup_d = dr("wup", [L, 2 * NFF, 128, D])
    wdn_d = dr("wdn", [L, NQ, 16, 128, FH * 128])
    wpool_d = dr("wpool", [L, 4, 4, 128, 512])
    small_d = dr("small", [L, 128, NSM])
    consts_d = dr("consts", [128, NCONST])
    yT_d = dr("yT", [D, S], kind="ExternalOutput")
    dbg_list = []

    def checkpoint(name):
        if stop_at == name:
            raise StopBuild()

    with ExitStack() as es:
        fw = FW(nc, es)
        t_win, t_wab, t_wout, t_wup, t_wdn, t_wpool, t_small, t_consts = (Tok() for _ in range(8))
        t_xin = Tok()
        t_y = [[Tok() for _ in range(KC)] for _ in range(NT)]
        st_tok = [fw.dtok("st%d" % i) for i in range(3)]
        st_rr = RR(st_tok)

        cf = fw.sb("cf", [128, NCONST], F32, dma=True)
        cb = fw.sb("cb", [128, 640], BF16, dma=True)
        fw.dma("sp", cf[:, :], V(consts_d.ap()[:, :], [t_consts]), cf.tok)
        fw.dma("pool", cb[:, :], V(consts_d.ap()[:, 0:640], [t_consts]), cb.tok)
        identb = cb[:, C_ID:C_ID + 128]
        triub = cb[:, C_TRIU:C_TRIU + 128]
        sub_ = cb[:, C_SU:C_SU + 128]
        onesb = cb[:, C_ONE:C_ONE + 128]
        triuf = cf[:, C_TRIU:C_TRIU + 128]
        slf = cf[:, C_SL:C_SL + 128]
        onesf = cf[:, C_ONE:C_ONE + 128]
        identf = cf[:, C_ID:C_ID + 128]

        hT = [fw.sb("hT%d" % k, [128, TT], BF16) for k in range(KC)]
        slots = [fw.sb("slot%d" % k, [128, TT], BF16) for k in range(16)]
        wbuf = RR([fw.sb("wb%d" % i, [128, D], BF16, dma=True) for i in range(3)])
        wdbuf = RR([fw.sb("wd%d" % i, [128, FH * 128], BF16, dma=True) for i in range(2)])
        wpbuf = RR([fw.sb("wp%d" % i, [128, 512], BF16, dma=True) for i in range(2)])
        wabb = fw.sb("wabb", [128, KC * 32], BF16, dma=True)
        small = fw.sb("small", [128, NSM], F32, dma=True)
        xbuf = RR([fw.sb("xb%d" % i, [128, TT], F32, dma=True) for i in range(2)])
        sqb = RR([fw.sb("sq%d" % i, [128, TT], BF16) for i in range(2)])
        rstd = fw.sb("rstd", [128, TT], F32)
        tmpA = RR([fw.sb("tA%d" % i, [128, TT + 16], F32) for i in range(2)])
        tmpB = RR([fw.sb("tB%d" % i, [128, TT + 16], F32) for i in range(2)])
        qkvT = [fw.sb("qkvT%d" % i, [128, TT], BF16) for i in range(3)]
        gateTs = [fw.sb("gateT%d" % i, [128, TT], BF16) for i in range(2)]
        yaT = fw.sb("yaT", [128, TT], F32)
        pooledT = qkvT + [gateTs[0]]
        tokm = [fw.sb("tokm%d" % i, [128, NCH, 128], F32 if i < 2 else BF16) for i in range(3)]
        Sst = fw.sb("Sst", [128, NH, 128], F32)
        Stok = [Tok() for _ in range(NH)]
        c_qkv = fw.sb("c_qkv", [128, 48, 3], F32)
        c_pool = fw.sb("c_pool", [128, 16, 15], F32)
        c_ffn = fw.sb("c_ffn", [128, NFF, 2], F32)
        ab_sb = fw.sb("ab_sb", [128, NCH, 32], F32)
        beta = fw.sb("beta", [128, NCH, 16], F32)
        glog = fw.sb("glog", [128, NCH, 16], F32)
        gcs = fw.sb("gcs", [128, NCH, 16], F32)
        egt = fw.sb("egt", [128, NCH, 16], F32)
        ket = fw.sb("ket", [128, NCH, 16], F32)
        dect = fw.sb("dect", [128, NCH, 16], F32)
        negA = fw.sb("negA", [128, NCH, 16], F32)
        ts0 = fw.sb("ts0", [128, NCH, 16], F32)
        ts1 = fw.sb("ts1", [128, NCH, 16], F32)
        ssk = fw.sb("ssk", [128, NCH], F32)
        ssq = fw.sb("ssq", [128, NCH], F32)
        sc = {n: fw.sb("sc_" + n, [128, NCH], F32) for n in ("rk", "rq", "kg", "ke", "qg")}
        sso = RR([fw.sb("sso%d" % i, [128, 2], F32) for i in range(2)])
        junk = RR([fw.sb("junk%d" % i, [128, 128], F32) for i in range(2)])
        junk2 = RR([fw.sb("junkb%d" % i, [128, 16], F32) for i in range(2)])
        KI = 2
        lsets = []
        for i in range(2 * KI):
            lsets.append(dict(
                dgy=fw.sb("dgy%d" % i, [128, 4, 128], F32),
                k4=fw.sb("k4_%d" % i, [128, 4, 128], F32),
                ketok=fw.sb("ketok%d" % i, [128, 128], F32),
                sld=fw.sb("sld%d" % i, [128, 128], F32),
                dT=fw.sb("dT%d" % i, [128, 128], F32),
                dTi=fw.sb("dTi%d" % i, [128, 128], F32),
                dTs=fw.sb("dTs%d" % i, [128, 128], F32),
                attnT=fw.sb("attnT%d" % i, [128, 128], F32),
                ytA=fw.sb("ytA%d" % i, [128, 128], F32),
                ytB=fw.sb("ytB%d" % i, [128, 128], F32),
                Tfin=fw.sb("Tfin%d" % i, [128, 128], F32),
            ))
        rp = RR([fw.sb("rp%d" % i, [128, 128], F32) for i in range(2)])
        vnew = RR([fw.sb("vnew%d" % i, [128, 128], F32) for i in range(2)])
        onb = RR([fw.sb("onb%d" % i, [128, 128], BF16) for i in range(2)])
        eps_t = fw.sb("eps_t", [128, 1], F32)
        fw.memset("dve", eps_t[:, :], EPS)
        eps_c = eps_t[:, 0:1]

        pbank = [es.enter_context(nc.psum_tensor("pb%d" % i, [128, 512], F32)) for i in range(7)]
        pbf = es.enter_context(nc.psum_tensor("pbf", [128, 1024], BF16))
        _acc_banks = RR([Tile(pbank[0], Tok()), Tile(pbank[1], Tok()), Tile(pbank[2], Tok())])

        class _AccRR:
            def get(self):
                return [_acc_banks.get(), _acc_banks.get()]

        acc_rr = _AccRR()

        class PS:
            def __init__(self, t, lo, n, tok):
                self.t, self.lo, self.n, self.tok = t, lo, n, tok

            def v(self, a=0, b=None):
                b = self.n if b is None else b
                return V(self.t[:, self.lo + a:self.lo + b], [self.tok])

        g_rr = RR([PS(pbank[3], 0, 512, Tok()), PS(pbank[4], 0, 512, Tok()), PS(pbank[5], 0, 512, Tok()),
                   PS(pbank[6], 0, 512, Tok())])
        pair_rr = g_rr
        g_banks = g_rr.items
        sing_rr = g_rr
        pbf_ps = PS(pbf, 0, 1024, Tok())

        def sm(off, n=1):
            return small[:, off:off + n]

        def load_small(l):
            fw.dma("sp", small[:, :], V(small_d.ap()[l], [t_small]), small.tok)
            fw.dma("pool", wabb[:, :], V(wab_d.ap()[l], [t_wab]), wabb.tok)
            fw.act(ts0.re("p a b -> p (a b)"), sm(O_ALOG, 128), AF.Exp)
            fw.ts("dve", negA.re("p a b -> p (a b)"), ts0.re("p a b -> p (a b)"), -1.0, ALU.mult)

        def x_src(l, phase, ti, kc):
            cols = slice(ti * TT, (ti + 1) * TT)
            if l == 0 and phase == 0:
                return V(xT_d.ap()[kc * 128:(kc + 1) * 128, cols], [t_xin])
            return V(yT_d.ap()[kc * 128:(kc + 1) * 128, cols], [t_y[ti][kc]])

        def y_dst(ti, kc):
            cols = slice(ti * TT, (ti + 1) * TT)
            return V(yT_d.ap()[kc * 128:(kc + 1) * 128, cols], [t_y[ti][kc]])

        def rmsnorm_to_hT(l, phase, ti, woff):
            acc = acc_rr.get()
            for kc in range(KC):
                xb = xbuf.get()
                fw.dma("sp", xb[:, :], x_src(l, phase, ti, kc), xb.tok)
                sq = sqb.get()
                fw.act(sq[:, :], xb[:, :], AF.Square)
                for hf in range(2):
                    fw.mm(acc[hf][:, :], onesb, sq[:, hf * 512:(hf + 1) * 512], start=(kc == 0), stop=(kc == KC - 1))
            for hf in range(2):
                fw.act(rstd[:, hf * 512:(hf + 1) * 512], acc[hf][:, :], AF.Ln, scale=1.0 / D, bias=eps_c)
                fw.act(rstd[:, hf * 512:(hf + 1) * 512], rstd[:, hf * 512:(hf + 1) * 512], AF.Exp, scale=-0.5)
            for kc in range(KC):
                xb = xbuf.get()
                fw.dma("sp", xb[:, :], x_src(l, phase, ti, kc), xb.tok)
                fw.stt("dve", hT[kc][:, :], xb[:, :], sm(woff + kc), rstd[:, :], ALU.mult, ALU.mult)

        class WStream:
            def __init__(self, bufs, src_iter):
                self.bufs, self.it, self.issued, self.taken, self.n = bufs, src_iter, 0, 0, len(bufs)
                self.keys = []

            def _issue(self):
                try:
                    key, ap, tok = next(self.it)
                except StopIteration:
                    return False
                wb = self.bufs[self.issued % self.n]
                fw.dma("pool", wb[:, :], V(ap, [tok]), wb.tok)
                self.keys.append(key)
                self.issued += 1
                return True

            def next(self, key):
                m = self.taken
                while self.issued < m + self.n:
                    if not self._issue():
                        break
                assert self.keys[m] == key, (self.keys[m], key)
                self.taken += 1
                return self.bufs[m % self.n]

        def _src_main():
            for l_ in range(L):
                for ti_ in range(NT):
                    for ct_ in range(112):
                        yield ("in", l_, ti_, ct_), win_d.ap()[l_, ct_], t_win
                    for j_ in range(16):
                        yield ("out", l_, ti_, j_), wout_d.ap()[l_, j_], t_wout
                    for f_ in range(2 * NFF):
                        yield ("up", l_, ti_, f_), wup_d.ap()[l_, f_], t_wup

        def _src_dn():
            for l_ in range(L):
                for ti_ in range(NT):
                    for hh_ in range(NQ):
                        for j_ in range(16):
                            yield ("dn", l_, ti_, hh_, j_), wdn_d.ap()[l_, hh_, j_], t_wdn

        def _src_pool():
            for l_ in range(L):
                for ti_ in range(NT):
                    for g_ in range(4):
                        for ot_ in range(4):
                            yield ("pw", l_, ti_, g_, ot_), wpool_d.ap()[l_, g_, ot_], t_wpool

        ws_main = WStream(wbuf.items, _src_main())
        ws_dn = WStream(wdbuf.items, _src_dn())
        ws_pool = WStream(wpbuf.items, _src_pool())

        def proj_g(stream, key, nk=KC, rhs_tiles=None):
            rhs_tiles = rhs_tiles or hT
            wb = stream.next(key)
            acc = acc_rr.get()
            n = 0
            for hf in range(2):
                for kc in range(nk):
                    fw.mm(acc[hf][:, :], wb[:, kc * 128:(kc + 1) * 128], rhs_tiles[kc][:, hf * 512:(hf + 1) * 512],
                          start=(kc == 0), stop=(kc == nk - 1))
                    n += 1
                    if n % 4 == 0 and not (kc == nk - 1):
                        yield
                yield
            return acc

        def drain(g_):
            try:
                while True:
                    next(g_)
            except StopIteration as e_:
                return e_.value

        def proj(stream, key, nk=KC, rhs_tiles=None):
            return drain(proj_g(stream, key, nk=nk, rhs_tiles=rhs_tiles))

        def evac2(eng_fn, acc):
            for hf in range(2):
                eng_fn(hf, acc[hf][:, :], slice(hf * 512, (hf + 1) * 512))

        def token_scalars():
            flat = lambda t: t.re("p a b -> p (a b)")
            for c in range(NCH):
                ps = pair_rr.get()
                for kc in range(KC):
                    fw.mm(ps.v(0, 32), hT[kc][:, c * 128:(c + 1) * 128], wabb[:, kc * 32:(kc + 1) * 32],
                          start=(kc == 0), stop=(kc == KC - 1))
                fw.copy("act", ab_sb[:, c, :], ps.v(0, 32))
            checkpoint("tsc1")
            fw.act(beta[:, :, :], ab_sb[:, :, 0:16], AF.Sigmoid)
            fw.tt("dve", ts0[:, :, :], ab_sb[:, :, 16:32],
                  V(small.t[:, O_DTB:O_DTB + 128].rearrange("p (a b) -> p a b", b=16), [small.tok]), ALU.add)
            fw.act(ts1[:, :, :], ts0[:, :, :], AF.Exp)
            fw.act(ts0[:, :, :], ts1[:, :, :], AF.Ln, bias=1.0)
            fw.tt("dve", glog[:, :, :], ts0[:, :, :], negA[:, :, :], ALU.mult)
            checkpoint("tsc2")
            p1 = pair_rr.get()
            fw.mm(p1.v(0, 128), triuf, flat(glog))
            p2 = pair_rr.get()
            fw.mm(p2.v(0, 128), onesf, flat(glog))
            checkpoint("tsc3")
            fw.copy("act", flat(gcs), p1.v(0, 128))
            fw.act(flat(egt), p1.v(0, 128), AF.Exp)
            fw.tt("dve", flat(ts0), p2.v(0, 128), flat(gcs), ALU.subtract)
            fw.act(flat(ket), flat(ts0), AF.Exp)
            fw.act(flat(dect), p2.v(0, 128), AF.Exp)

        def interleave(gens, bg=None):
            gens = list(gens)
            while gens:
                for g_ in list(gens):
                    try:
                        next(g_)
                    except StopIteration:
                        gens.remove(g_)
                if bg is not None and not bg[1]:
                    try:
                        next(bg[0])
                    except StopIteration:
                        bg[1] = True

        def chain(*gens):
            for g_ in gens:
                yield from g_

        def gdn_local(h, c, B, pb):
            ktok, qtok, vtok = tokm
            col = lambda t: V(t.t[:, c:c + 1], [t.tok])
            colh = lambda t: V(t.t[:, c, h:h + 1], [t.tok])
            d = B["dgy"]
            fw.ts("dve", d[:, 0, :], identf, col(sc["rk"]), ALU.mult); yield
            fw.ts("dve", d[:, 1, :], identf, col(sc["kg"]), ALU.mult); yield
            fw.act(d[:, 2, :], identf, AF.Copy, scale=col(sc["rq"])); yield
            fw.act(d[:, 3, :], identf, AF.Copy, scale=col(sc["qg"])); yield
            k4 = B["k4"]
            p = PS(pb.t, 0, 256, pb.tok)
            fw.mm(p.v(0, 256), ktok[:, c, :], V(d.t[:, 0:2, :].rearrange("p a b -> p (a b)"), [d.tok])); yield
            fw.copy("act", V(k4.t[:, 0::2, :], [k4.tok]), V(p.t[:, p.lo:p.lo + 256].rearrange("p (a b) -> p a b", b=128), [p.tok])); yield
            p = PS(pb.t, 0, 256, pb.tok)
            fw.mm(p.v(0, 256), qtok[:, c, :], V(d.t[:, 2:4, :].rearrange("p a b -> p (a b)"), [d.tok])); yield
            fw.copy("act", V(k4.t[:, 1::2, :], [k4.tok]), V(p.t[:, p.lo:p.lo + 256].rearrange("p (a b) -> p a b", b=128), [p.tok])); yield
            if c == 0: checkpoint("La")
            ke_t = B["ketok"]
            fw.act(ke_t[:, :], ktok[:, c, :], AF.Copy, scale=col(sc["ke"])); yield
            sl = B["sld"]
            fw.ts("dve", sl[:, :], slf, colh(glog), ALU.mult); yield
            pd = PS(pb.t, 0, 128, pb.tok)
            fw.mm(pd.v(0, 128), sl[:, :], triuf); yield
            dT = B["dT"]
            fw.act(dT[:, :], pd.v(0, 128), AF.Exp); yield
            di = B["dTi"]
            ds_ = B["dTs"]
            fw.tt("dve", di[:, :], dT[:, :], V(cf.t[:, C_TRIU:C_TRIU + 128], [cf.tok]), ALU.mult); yield
            fw.tt("pool", ds_[:, :], dT[:, :], V(cf.t[:, C_SU:C_SU + 128], [cf.tok]), ALU.mult); yield
            pg = PS(pb.t, 0, 256, pb.tok)
            fw.mm(pg.v(0, 256), k4[:, 0, :], V(k4.t[:, 0:2, :].rearrange("p a b -> p (a b)"), [k4.tok])); yield
            at = B["attnT"]
            fw.tt("dve", at[:, :], pg.v(128, 256), di[:, :], ALU.mult); yield
            if c == 0: checkpoint("Lb")
            ymA = V(d.t[:, 0:2, :], [d.tok])
            ymB = V(d.t[:, 2:4, :], [d.tok])
            half = lambda ym, i: V(ym.ap[:, i, :], ym.toks)
            flat2 = lambda ym: V(ym.ap.rearrange("p a b -> p (a b)"), ym.toks)
            fw.stt("dve", half(ymA, 0), pg.v(0, 128), colh(beta), ds_[:, :], ALU.mult, ALU.mult); yield
            fw.tt("dve", half(ymB, 1), identf, half(ymA, 0), ALU.subtract); yield
            pn = PS(pb.t, 0, 128, pb.tok)
            fw.mm(pn.v(0, 128), half(ymA, 0), identf); yield
            ytA, ytB = B["ytA"], B["ytB"]
            fw.copy("act", ytA[:, :], pn.v(0, 128)); yield
            if c == 0: checkpoint("Lc")
            ym, ym2, yt, yt2 = ymA, ymB, ytA, ytB
            for k in range(7):
                last = (k == 6)
                first = (k == 0)
                pm = PS(pb.t, 0, 256, pb.tok)
                if first:
                    fw.mm(pm.v(0, 128), yt[:, :], half(ym, 0)); yield
                elif last:
                    fw.mm(pm.v(128, 256), yt[:, :], half(ym, 1)); yield
                else:
                    fw.mm(pm.v(0, 256), yt[:, :], flat2(ym)); yield
                if not last:
                    fw.copy("act", half(ym2, 0), pm.v(0, 128)); yield
                    if not first:
                        fw.tt("dve", half(ym2, 1), pm.v(128, 256), half(ym, 1), ALU.add); yield
                    p2_ = PS(pb.t, 0, 128, pb.tok)
                    fw.mm(p2_.v(0, 128), half(ym, 0), yt[:, :]); yield
                    fw.copy("act", yt2[:, :], p2_.v(0, 128)); yield
                    ym, ym2, yt, yt2 = ym2, ym, yt2, yt
                else:
                    tf = B["Tfin"]
                    fw.tt("dve", tf[:, :], pm.v(128, 256), half(ym, 1), ALU.add); yield

        def gdn_seq(h, c, B, pbX, pbY):
            gateT = gateTs[h % 2]
            ktok, qtok, vtok = tokm
            cs = slice(c * 128, (c + 1) * 128)
            colh = lambda t: V(t.t[:, c, h:h + 1], [t.tok])
            Sh = V(Sst.t[:, h, :], [Stok[h]])
            k4, ke_t, at, tf = B["k4"], B["ketok"], B["attnT"], B["Tfin"]
            if c == 0: checkpoint("Qa")
            pr = PS(pbX.t, 0, 128, pbX.tok)
            fw.mm(pr.v(0, 128), k4[:, 2, :], Sh); yield
            r_ = rp.get()
            fw.tt("dve", r_[:, :], vtok[:, c, :], pr.v(0, 128), ALU.subtract); yield
            pv_ = PS(pbY.t, 0, 128, pbY.tok)
            fw.mm(pv_.v(0, 128), tf[:, :], r_[:, :]); yield
            vn = vnew.get()
            fw.act(vn[:, :], pv_.v(0, 128), AF.Copy, scale=colh(beta)); yield
            po = PS(pbX.t, 0, 128, pbX.tok)
            fw.mm(po.v(0, 128), k4[:, 3, :], Sh, start=True, stop=False)
            fw.mm(po.v(0, 128), at[:, :], vn[:, :], start=False, stop=True); yield
            pS = PS(pbY.t, 0, 128, pbY.tok)
            fw.mm(pS.v(0, 128), ke_t[:, :], vn[:, :]); yield
            fw.stt("dve", Sh, Sh, colh(dect), pS.v(0, 128), ALU.mult, ALU.add); yield
            if c == 0: checkpoint("Qb")
            so = sso.get()
            fw.act(junk.get()[:, :], po.v(0, 128), AF.Square, accum=so[:, 0:1]); yield
            fw.act(so[:, 1:2], so[:, 0:1], AF.Ln, scale=1.0 / HD, bias=eps_c); yield
            fw.act(so[:, 1:2], so[:, 1:2], AF.Exp, scale=-0.5); yield
            ob = onb.get()
            fw.act(ob[:, :], po.v(0, 128), AF.Copy, scale=so[:, 1:2]); yield
            fw.tr(pbf_ps.v(128, 256), ob[:, :], identb); yield
            fw.stt("dve", yaT[:, cs], pbf_ps.v(128, 256), sm(O_GNW), gateT[:, cs], ALU.mult, ALU.mult); yield

        def gdn_head(l, h, jslot, bg=None):
            kT, qT, vT = qkvT[1], qkvT[0], qkvT[2]
            ktok, qtok, vtok = tokm
            for which, (srcT, dst, ss) in enumerate(((kT, ktok, ssk), (qT, qtok, ssq), (vT, vtok, None))):
                for c in range(NCH):
                    fw.tr(pbf_ps.v(c * 128, (c + 1) * 128), srcT[:, c * 128:(c + 1) * 128], identb)
                if ss is not None:
                    sqt = tmpB.get()
                    fw.act(sqt[:, 0:TT], pbf_ps.v(0, TT), AF.Square)
                    sq3 = V(sqt.t[:, 0:TT].rearrange("p (c d) -> p c d", d=128), [sqt.tok])
                    fw.op("dve", lambda e: e.tensor_reduce(out=ss.t[:, :], in_=sq3.ap, axis=mybir.AxisListType.X, op=ALU.add),
                          reads=sq3.toks, writes=[ss.tok])
                fw.copy("dve" if which != 1 else "act", dst.re("p c d -> p (c d)"), pbf_ps.v(0, TT))
            checkpoint("gdnA%d" % h)
            hs = lambda t: V(t.t[:, :, h], [t.tok])
            fw.act(sc["rk"][:, :], ssk[:, :], AF.Ln, bias=eps_c)
            fw.act(sc["rk"][:, :], sc["rk"][:, :], AF.Exp, scale=-0.5)
            fw.act(sc["rq"][:, :], ssq[:, :], AF.Ln, bias=eps_c)
            fw.act(sc["rq"][:, :], sc["rq"][:, :], AF.Exp, scale=-0.5)
            fw.ts("dve", sc["rq"][:, :], sc["rq"][:, :], float(HD) ** -0.5, ALU.mult)
            fw.tt("dve", sc["kg"][:, :], sc["rk"][:, :], hs(egt), ALU.mult)
            fw.tt("dve", sc["ke"][:, :], sc["rk"][:, :], hs(ket), ALU.mult)
            fw.tt("dve", sc["qg"][:, :], sc["rq"][:, :], hs(egt), ALU.mult)
            L_ = lambda c: gdn_local(h, c, lsets[c % (2 * KI)], g_banks[c % KI])
            Q_ = lambda c: gdn_seq(h, c, lsets[c % (2 * KI)], g_banks[KI], g_banks[KI + 1])
            groups = [list(range(c0, c0 + KI)) for c0 in range(0, NCH, KI)]
            bgs = [bg, False] if bg is not None else None
            interleave([L_(c) for c in groups[0]], bgs)
            for gi in range(1, len(groups)):
                interleave([L_(c) for c in groups[gi]] + [chain(*[Q_(c) for c in groups[gi - 1]])], bgs)
            interleave([chain(*[Q_(c) for c in groups[-1]])], bgs)
            if bg is not None:
                drain(bg)
            fw.tt("pool", slots[jslot][:, :], yaT[:, :], slots[jslot][:, :], ALU.add)
            checkpoint("headdone%d" % h)

        def pool_phase_g(l, ti, g):
            win_ = 2 ** (g + 1)
            base = g * 28
            for jj in range(4):
                j = 4 * g + jj
                acc = yield from proj_g(ws_main, ("in", l, ti, base + jj))
                evac2(lambda hf, ps, sl: fw.act(slots[j][:, sl], ps, AF.Sigmoid), acc); yield
            for jj in range(4):
                j = 4 * g + jj
                acc = yield from proj_g(ws_main, ("in", l, ti, base + 4 + jj))
                U = tmpA.get()
                fw.copy("act", U[:, 1:16], c_pool[:, j, :])
                fw.memset("dve", U[:, 0:1], 0.0)
                evac2(lambda hf, ps, sl: fw.copy("act", U[:, 16 + sl.start:16 + sl.stop], ps), acc); yield
                fw.copy("act", c_pool[:, j, :], U[:, TT + 1:TT + 16])
                cur = U
                sh = 1
                e0 = 1
                while sh < win_:
                    nxt = tmpB.get()
                    fw.tt("pool" if sh > 1 else "dve", nxt[:, e0:TT + 16], cur[:, e0:TT + 16], cur[:, e0 - sh:TT + 16 - sh], ALU.add); yield
                    cur = nxt
                    sh *= 2
                    e0 = 2 * sh - 1
                fw.stt("dve", pooledT[jj][:, :], cur[:, 16:TT + 16], 1.0 / win_, U[:, 16:TT + 16], ALU.mult, ALU.subtract); yield
                if ti == 0:
                    tq = junk2.get()
                    fw.tt("dve", tq[:, 0:16], cur[:, 16:32], cf[:, C_ICNT + g * 16:C_ICNT + g * 16 + 16], ALU.mult)
                    fw.tt("dve", pooledT[jj][:, 0:16], tq[:, 0:16], U[:, 16:32], ALU.subtract); yield
            for ot in range(4):
                j = 4 * g + ot
                acc = yield from proj_g(ws_pool, ("pw", l, ti, g, ot), nk=4, rhs_tiles=pooledT)
                evac2(lambda hf, ps, sl: fw.stt("dve", slots[j][:, sl], ps, sm(O_PSC + j), slots[j][:, sl], ALU.mult, ALU.mult), acc); yield

        def head_prep_g(l, ti, h):
            g = h // 4
            base = g * 28 + 8 + (h % 4) * 5
            gateT = gateTs[h % 2]
            for which in range(3):
                cti = which * 16 + h
                acc = yield from proj_g(ws_main, ("in", l, ti, base + which))
                R = tmpA.get()
                A = tmpB.get()
                fw.copy("act", R[:, 0:3], c_qkv[:, cti, :])
                wq = lambda jtap: sm(O_CQKV + cti * 4 + jtap)
                evac2(lambda hf, ps, sl: fw.copy("act", R[:, 3 + sl.start:3 + sl.stop], ps), acc); yield
                evac2(lambda hf, ps, sl: fw.act(A[:, sl], ps, AF.Copy, scale=wq(3)), acc); yield
                fw.copy("act", c_qkv[:, cti, :], R[:, TT:TT + 3])
                fw.stt("dve", A[:, 0:TT], R[:, 2:TT + 2], wq(2), A[:, 0:TT], ALU.mult, ALU.add); yield
                fw.stt("dve", A[:, 0:TT], R[:, 1:TT + 1], wq(1), A[:, 0:TT], ALU.mult, ALU.add); yield
                fw.stt("dve", A[:, 0:TT], R[:, 0:TT], wq(0), A[:, 0:TT], ALU.mult, ALU.add); yield
                fw.act(qkvT[which][:, :], A[:, 0:TT], AF.Silu); yield
            accz = yield from proj_g(ws_main, ("in", l, ti, base + 3))
            Z = tmpB.get()
            evac2(lambda hf, ps, sl: fw.act(Z[:, sl], ps, AF.Silu), accz); yield
            accg = yield from proj_g(ws_main, ("in", l, ti, base + 4))
            evac2(lambda hf, ps, sl: fw.act(gateT[:, sl], ps, AF.Sigmoid), accg); yield
            fw.tt("pool", gateT[:, :], Z[:, 0:TT], gateT[:, :], ALU.mult); yield

        def prep_g(l, ti, h):
            if h % 4 == 0:
                yield from pool_phase_g(l, ti, h // 4)
            yield from head_prep_g(l, ti, h)

        def mixer(l, ti):
            rmsnorm_to_hT(l, 0, ti, O_NMIX)
            checkpoint("rms")
            token_scalars()
            checkpoint("tsc")
            drain(prep_g(l, ti, 0))
            checkpoint("qkv0")
            for h in range(NH):
                gdn_head(l, h, h, prep_g(l, ti, h + 1) if h + 1 < NH else None)
                checkpoint("head%d" % h)
            for j in range(16):
                acc = proj(ws_main, ("out", l, ti, j), rhs_tiles=slots)
                xb = xbuf.get()
                fw.dma("sp", xb[:, :], x_src(l, 0, ti, j), xb.tok)
                evac2(lambda hf, ps, sl: fw.tt("dve", xb[:, sl], ps, xb[:, sl], ALU.add), acc)
                fw.dma("sp", y_dst(ti, j), xb[:, :], st_rr.get())

        def ffn(l, ti):
            rmsnorm_to_hT(l, 1, ti, O_NFFN)
            for hh in range(NQ):
                for ff in range(FH):
                    f = hh * FH + ff
                    accg = proj(ws_main, ("up", l, ti, 2 * f))
                    R = tmpA.get()
                    A = tmpB.get()
                    fw.copy("act", R[:, 0:2], c_ffn[:, f, :])
                    wf = lambda jtap: sm(O_CFW + f * 3 + jtap)
                    evac2(lambda hf, ps, sl: fw.copy("act", R[:, 2 + sl.start:2 + sl.stop], ps), accg)
                    evac2(lambda hf, ps, sl: fw.act(A[:, sl], ps, AF.Identity, scale=wf(2), bias=sm(O_CFB + f)), accg)
                    fw.copy("act", c_ffn[:, f, :], R[:, TT:TT + 2])
                    fw.stt("dve", A[:, 0:TT], R[:, 1:TT + 1], wf(1), A[:, 0:TT], ALU.mult, ALU.add)
                    fw.stt("dve", A[:, 0:TT], R[:, 0:TT], wf(0), A[:, 0:TT], ALU.mult, ALU.add)
                    Gl = tmpB.get()
                    fw.act(Gl[:, 0:TT], A[:, 0:TT], AF.Gelu)
                    accu = proj(ws_main, ("up", l, ti, 2 * f + 1))
                    evac2(lambda hf, ps, sl: fw.tt("dve", slots[ff][:, sl], ps, Gl[:, sl], ALU.mult), accu)
                for j in range(16):
                    acc = proj(ws_dn, ("dn", l, ti, hh, j), nk=FH, rhs_tiles=slots)
                    xb = xbuf.get()
                    fw.dma("sp", xb[:, :], x_src(l, 1, ti, j), xb.tok)
                    evac2(lambda hf, ps, sl: fw.tt("dve", xb[:, sl], ps, xb[:, sl], ALU.add), acc)
                    fw.dma("sp", y_dst(ti, j), xb[:, :], st_rr.get())

        def final_norm(ti):
            acc = acc_rr.get()
            for kc in range(KC):
                xb = xbuf.get()
                fw.dma("sp", xb[:, :], x_src(L, 0, ti, kc), xb.tok)
                sq = sqb.get()
                fw.act(sq[:, :], xb[:, :], AF.Square)
                for hf in range(2):
                    fw.mm(acc[hf][:, :], onesb, sq[:, hf * 512:(hf + 1) * 512], start=(kc == 0), stop=(kc == KC - 1))
            for hf in range(2):
                fw.act(rstd[:, hf * 512:(hf + 1) * 512], acc[hf][:, :], AF.Ln, scale=1.0 / D, bias=eps_c)
                fw.act(rstd[:, hf * 512:(hf + 1) * 512], rstd[:, hf * 512:(hf + 1) * 512], AF.Exp, scale=-0.5)
            for kc in range(KC):
                xb = xbuf.get()
                fw.dma("sp", xb[:, :], x_src(L, 0, ti, kc), xb.tok)
                fw.stt("dve", xb[:, :], xb[:, :], cf[:, C_NFIN + kc:C_NFIN + kc + 1], rstd[:, :],
                       ALU.mult, ALU.mult)
                fw.dma("sp", y_dst(ti, kc), xb[:, :], st_rr.get())

        def dump(name, v, width):
            dd = dr("dbg_" + name, [128, width], kind="ExternalOutput")
            tk = fw.dtok("dbg_" + name)
            fw.dma("pool", V(dd.ap()[:, :], [Tok()]), v, tk)
            dbg_list.append(tk)

        build_program.dump = dump
        build_program.env = locals()
        try:
          for l in range(L):
            load_small(l)
            fw.memset("dve", Sst[:, :, :], 0.0)
            for h in range(NH):
                Stok[h].w = Sst.tok.w
            fw.memset("dve", c_qkv[:, :, :], 0.0)
            fw.memset("dve", c_pool[:, :, :], 0.0)
            fw.memset("dve", c_ffn[:, :, :], 0.0)
            for ti in range(NT):
                mixer(l, ti)
                checkpoint("mixer")
                ffn(l, ti)
                checkpoint("ffn")
          for ti in range(NT):
            final_norm(ti)
        except StopBuild:
            if build_program.on_stop:
                build_program.on_stop(locals())
        for t in fw.dtoks:
            if t.dcnt:
                fw.eng["sp"].wait_ge(t.dsem, t.dcnt)
        for e in ("pe", "act", "dve", "pool"):
            if fw.cnt[e]:
                fw.eng["sp"].wait_ge(fw.sem[e], fw.cnt[e])
        print("program: ins=%d waits=%d" % (fw.nins, fw.nwait), {e: fw.cnt[e] for e in fw.cnt})
    return nc


build_program.on_stop = None
_CACHE = {}


def run(inputs, L, S, ncores, stop_at=None, ret_all=False):
    B = inputs["x"].shape[0]
    pw = prep_weights(inputs, L)
    key = (L, S, stop_at)
    if key not in _CACHE:
        _CACHE[key] = build_program(L, S, stop_at=stop_at)
    nc = _CACHE[key]
    in_maps = []
    for c in range(ncores):
        b = c % B
        m = {"xT": np.ascontiguousarray(inputs["x"][b, :S].T)}
        m.update(pw)
        in_maps.append(m)
    res = run_bass_kernel_spmd(nc, in_maps, core_ids=list(range(ncores)))
    if ret_all:
        return res.results
    out = np.stack([np.ascontiguousarray(res.results[b]["yT"].T) for b in range(B)], axis=0)
    return out.astype(np.float32)


def kernel(**inputs):
    inputs = {k: np.asarray(v, dtype=np.float32) for k, v in inputs.items()}
    return run(inputs, 4, 4096, 8)
```

```python
import numpy as np
from contextlib import ExitStack
import concourse.bass as bass
import concourse.mybir as mybir
from concourse.bass_utils import run_bass_kernel_spmd

F32 = mybir.dt.float32
BF16 = mybir.dt.bfloat16
AF = mybir.ActivationFunctionType
ALU = mybir.AluOpType

D = 2048
NH = 16
HD = 128
DFF = 5632
NFF = DFF // 128
DIN = 4 * D + 2 * NH + D + 2 * D
EPS = 1e-6
TT = 1024
NCH = TT // 128
KC = D // 128
NQ = 4
FH = NFF // NQ


class Tok:
    __slots__ = ("w", "r", "dsem", "dcnt", "name")

    def __init__(self, name=""):
        self.w = None
        self.r = {}
        self.dsem = None
        self.dcnt = 0
        self.name = name


class V:
    __slots__ = ("ap", "toks")

    def __init__(self, ap, toks):
        self.ap = ap
        self.toks = toks


class Tile:
    def __init__(self, t, tok):
        self.t = t
        self.tok = tok

    def __getitem__(self, idx):
        return V(self.t[idx], [self.tok])

    def re(self, pat, **kw):
        return V(self.t[:].rearrange(pat, **kw), [self.tok])


class FW:
    def __init__(self, nc, es):
        self.nc = nc
        self.es = es
        self.eng = {"pe": nc.tensor, "act": nc.scalar, "dve": nc.vector, "pool": nc.gpsimd, "sp": nc.sync}
        self.sem = {e: es.enter_context(nc.semaphore("s_" + e)) for e in self.eng}
        self.cnt = {e: 0 for e in self.eng}
        self.seen = {e: {} for e in self.eng}
        self.nwait = 0
        self.nins = 0
        self.ntile = 0
        self.dtoks = []

    def sb(self, name, shape, dt, dma=False):
        t = self.es.enter_context(self.nc.sbuf_tensor("sb_" + name, list(shape), dt))
        tok = self.dtok(name) if dma else Tok(name)
        return Tile(t, tok)

    def dtok(self, name):
        t = Tok(name)
        t.dsem = self.es.enter_context(self.nc.semaphore("d_" + name))
        self.dtoks.append(t)
        return t

    def _deps(self, e, reads, writes):
        deps = []
        for t in reads:
            if t.w is not None:
                deps.append(t.w)
        for t in writes:
            if t.w is not None:
                deps.append(t.w)
            deps.extend(t.r.values())
        mysem = self.sem[e]
        seen = self.seen[e]
        for (sem, val) in deps:
            if e == "pe" and sem is mysem:
                continue
            if seen.get(sem, 0) >= val:
                continue
            self.eng[e].wait_ge(sem, val)
            seen[sem] = val
            self.nwait += 1

    def op(self, e, build, reads=(), writes=()):
        self._deps(e, reads, writes)
        ins = build(self.eng[e])
        self.cnt[e] += 1
        self.nins += 1
        ins.then_inc(self.sem[e], 1)
        me = (self.sem[e], self.cnt[e])
        s = self.sem[e]
        for t in reads:
            t.r[s] = me
        for t in writes:
            t.w = me
            t.r = {}
        return ins

    def dma(self, q, out, in_, tok):
        reads, writes = in_.toks, out.toks
        self._deps(q, reads, writes)
        if tok.dcnt > 0 and self.seen[q].get(tok.dsem, 0) < tok.dcnt:
            self.eng[q].wait_ge(tok.dsem, tok.dcnt)
            self.seen[q][tok.dsem] = tok.dcnt
            self.nwait += 1
        ins = self.eng[q].dma_start(out=out.ap, in_=in_.ap)
        tok.dcnt += 16
        ins.then_inc(tok.dsem, 16)
        self.nins += 1
        me = (tok.dsem, tok.dcnt)
        for t in reads:
            t.r[tok.dsem] = me
        for t in writes:
            t.w = me
            t.r = {}
        return ins

    def wait_tok(self, e, tok):
        self._deps(e, [tok], [])

    def mm(self, out, lhsT, rhs, start=True, stop=True):
        return self.op("pe", lambda e: e.matmul(out.ap, lhsT=lhsT.ap, rhs=rhs.ap, start=start, stop=stop),
                       reads=lhsT.toks + rhs.toks, writes=out.toks)

    def tr(self, out, in_, ident):
        return self.op("pe", lambda e: e.transpose(out.ap, in_.ap, ident.ap),
                       reads=in_.toks + ident.toks, writes=out.toks)

    def act(self, out, in_, func, bias=None, scale=None, accum=None, eng="act"):
        kw = {}
        reads = list(in_.toks)
        writes = list(out.toks)
        if bias is not None:
            if isinstance(bias, V):
                kw["bias"] = bias.ap
                reads += bias.toks
            else:
                kw["bias"] = bias
        if scale is not None:
            if isinstance(scale, V):
                kw["scale"] = scale.ap
                reads += scale.toks
            else:
                kw["scale"] = scale
        if accum is not None:
            kw["accum_out"] = accum.ap
            writes += accum.toks
        return self.op(eng, lambda e: e.activation(out=out.ap, in_=in_.ap, func=func, **kw), reads=reads, writes=writes)

    def tt(self, eng, out, in0, in1, op):
        return self.op(eng, lambda e: e.tensor_tensor(out=out.ap, in0=in0.ap, in1=in1.ap, op=op),
                       reads=in0.toks + in1.toks, writes=out.toks)

    def ts(self, eng, out, in0, s1, op0, s2=None, op1=None, accum=None):
        reads = list(in0.toks)
        writes = list(out.toks)
        a1 = s1
        if isinstance(s1, V):
            a1 = s1.ap
            reads += s1.toks
        a2 = s2
        if isinstance(s2, V):
            a2 = s2.ap
            reads += s2.toks
        kw = {}
        if op1 is not None:
            kw["op1"] = op1
        if accum is not None:
            kw["accum_out"] = accum.ap
            writes += accum.toks
        return self.op(eng, lambda e: e.tensor_scalar(out=out.ap, in0=in0.ap, scalar1=a1, scalar2=a2, op0=op0, **kw),
                       reads=reads, writes=writes)

    def stt(self, eng, out, in0, scalar, in1, op0, op1):
        reads = in0.toks + in1.toks
        a = scalar
        if isinstance(scalar, V):
            a = scalar.ap
            reads = reads + scalar.toks
        return self.op(eng, lambda e: e.scalar_tensor_tensor(out=out.ap, in0=in0.ap, scalar=a, in1=in1.ap, op0=op0, op1=op1),
                       reads=reads, writes=out.toks)

    def copy(self, eng, out, in_):
        if eng == "act":
            return self.act(out, in_, AF.Copy)
        return self.op(eng, lambda e: e.tensor_copy(out=out.ap, in_=in_.ap), reads=in_.toks, writes=out.toks)

    def memset(self, eng, out, val):
        return self.op(eng, lambda e: e.memset(out.ap, val), reads=[], writes=out.toks)


class RR:
    def __init__(self, items):
        self.items = items
        self.i = 0

    def get(self):
        it = self.items[self.i % len(self.items)]
        self.i += 1
        return it


O_NMIX = 0
O_NFFN = 16
O_PSC = 32
O_CQKV = 48
O_CFW = O_CQKV + 192
O_CFB = O_CFW + 132
O_GNW = O_CFB + 44
O_ALOG = O_GNW + 1
O_DTB = O_ALOG + 128
NSM = O_DTB + 128
C_ID = 0
C_TRIU = 128
C_SU = 256
C_SL = 384
C_ONE = 512
C_ICNT = 640
C_NFIN = 704
NCONST = 720


def _colmajor(w, ncol_tiles):
    K = w.shape[0]
    return np.ascontiguousarray(w.reshape(K // 128, 128, ncol_tiles, 128).transpose(2, 1, 0, 3)).reshape(
        ncol_tiles, 128, (K // 128) * 128)


def prep_weights(inp, L):
    out = {}
    w_in = inp["w_in"]
    order = []
    for g in range(4):
        for j in range(4 * g, 4 * g + 4):
            order.append(4 * D + 2 * NH + D + D + j * 128)
        for j in range(4 * g, 4 * g + 4):
            order.append(4 * D + 2 * NH + j * 128)
        for h in range(4 * g, 4 * g + 4):
            order.append(0 * D + h * 128)
            order.append(1 * D + h * 128)
            order.append(2 * D + h * 128)
            order.append(3 * D + h * 128)
            order.append(4 * D + 2 * NH + D + h * 128)
    cols = np.concatenate([np.arange(o, o + 128) for o in order])
    win = np.empty((L, len(order), 128, D), np.float32)
    wab = np.empty((L, 128, KC * 32), np.float32)
    wout = np.empty((L, 16, 128, D), np.float32)
    wup = np.empty((L, 2 * NFF, 128, D), np.float32)
    wdn = np.empty((L, NQ, 16, 128, FH * 128), np.float32)
    wpool = np.empty((L, 4, 4, 128, 512), np.float32)
    small = np.zeros((L, 128, NSM), np.float32)
    upcols = np.concatenate([np.concatenate([np.arange(f * 128, f * 128 + 128), np.arange(DFF + f * 128, DFF + f * 128 + 128)])
                             for f in range(NFF)])
    for l in range(L):
        win[l] = _colmajor(w_in[l][:, cols], len(order))
        ab = w_in[l][:, 4 * D:4 * D + 32]
        wab[l] = ab.reshape(KC, 128, 32).transpose(1, 0, 2).reshape(128, KC * 32)
        wout[l] = _colmajor(inp["w_out"][l], 16)
        wup[l] = _colmajor(inp["w_up"][l][:, upcols], 2 * NFF)
        wd = inp["w_down"][l]
        for hh in range(NQ):
            blk = wd[hh * FH * 128:(hh + 1) * FH * 128]
            wdn[l, hh] = _colmajor(blk, 16)
        for g in range(4):
            wpool[l, g] = _colmajor(inp["pool_w"][l, g], 4)
        sm = small[l]
        sm[:, O_NMIX:O_NMIX + 16] = inp["norm_mix_w"][l].reshape(16, 128).T
        sm[:, O_NFFN:O_NFFN + 16] = inp["norm_ffn_w"][l].reshape(16, 128).T
        sm[:, O_PSC:O_PSC + 16] = inp["pool_scale"][l].reshape(16, 128).T
        sm[:, O_CQKV:O_CQKV + 192] = inp["conv_qkv_w"][l].reshape(4, 48, 128).transpose(2, 1, 0).reshape(128, 192)
        sm[:, O_CFW:O_CFW + 132] = inp["conv_ffn_w"][l].reshape(3, NFF, 128).transpose(2, 1, 0).reshape(128, 132)
        sm[:, O_CFB:O_CFB + 44] = inp["conv_ffn_b"][l].reshape(NFF, 128).T
        sm[:, O_GNW] = inp["gdn_norm_w"][l]
        sm[:, O_ALOG:O_ALOG + 128] = np.tile(inp["a_log"][l], NCH)[None, :]
        sm[:, O_DTB:O_DTB + 128] = np.tile(inp["dt_bias"][l], NCH)[None, :]
    out.update(win=win, wab=wab, wout=wout, wup=wup, wdn=wdn, wpool=wpool, small=small)
    c = np.zeros((128, NCONST), np.float32)
    idx = np.arange(128)
    c[:, C_ID:C_ID + 128] = np.eye(128, dtype=np.float32)
    c[:, C_TRIU:C_TRIU + 128] = (idx[:, None] <= idx[None, :])
    c[:, C_SU:C_SU + 128] = (idx[None, :] > idx[:, None])
    c[:, C_SL:C_SL + 128] = (idx[:, None] > idx[None, :])
    c[:, C_ONE:C_ONE + 128] = 1.0
    for g, win_ in enumerate((2, 4, 8, 16)):
        t = np.arange(16)
        c[:, C_ICNT + g * 16:C_ICNT + g * 16 + 16] = (np.float32(1.0) / np.minimum(t + 1, win_).astype(np.float32))[None, :]
    c[:, C_NFIN:C_NFIN + 16] = inp["norm_final_w"].reshape(16, 128).T
    out["consts"] = c
    return out


class StopBuild(Exception):
    pass


def build_program(L, S, dbg=None, stop_at=None):
    assert S % TT == 0
    NT = S // TT
    nc = bass.Bass("TRN2", target_bir_lowering=False)
    dr = lambda name, shape, kind="ExternalInput": nc.dram_tensor(name, list(shape), F32, kind=kind)
    xT_d = dr("xT", [D, S])
    win_d = dr("win", [L, 112, 128, D])
    wab_d = dr("wab", [L, 128, KC * 32])
    wout_d = dr("wout", [L, 16, 128, D])
    wup_d = dr("wup", [L, 2 * NFF, 128, D])
    wdn_d = dr("wdn", [L, NQ, 16, 128, FH * 128])
    wpool_d = dr("wpool", [L, 4, 4, 128, 512])
    small_d = dr("small", [L, 128, NSM])
    consts_d = dr("consts", [128, NCONST])
    yT_d = dr("yT", [D, S], kind="ExternalOutput")
    dbg_list = []

    def checkpoint(name):
        if stop_at == name:
            raise StopBuild()

    with ExitStack() as es:
        fw = FW(nc, es)
        t_win, t_wab, t_wout, t_wup, t_wdn, t_wpool, t_small, t_consts = (Tok() for _ in range(8))
        t_xin = Tok()
        t_y = [[Tok() for _ in range(KC)] for _ in range(NT)]
        st_tok = [fw.dtok("st%d" % i) for i in range(3)]
        st_rr = RR(st_tok)

        cf = fw.sb("cf", [128, NCONST], F32, dma=True)
        cb = fw.sb("cb", [128, 640], BF16, dma=True)
        fw.dma("sp", cf[:, :], V(consts_d.ap()[:, :], [t_consts]), cf.tok)
        fw.dma("pool", cb[:, :], V(consts_d.ap()[:, 0:640], [t_consts]), cb.tok)
        identb = cb[:, C_ID:C_ID + 128]
        triub = cb[:, C_TRIU:C_TRIU + 128]
        sub_ = cb[:, C_SU:C_SU + 128]
        onesb = cb[:, C_ONE:C_ONE + 128]
        triuf = cf[:, C_TRIU:C_TRIU + 128]
        slf = cf[:, C_SL:C_SL + 128]
        onesf = cf[:, C_ONE:C_ONE + 128]
        identf = cf[:, C_ID:C_ID + 128]

        hT = [fw.sb("hT%d" % k, [128, TT], BF16) for k in range(KC)]
        slots = [fw.sb("slot%d" % k, [128, TT], BF16) for k in range(16)]
        wbuf = RR([fw.sb("wb%d" % i, [128, D], BF16, dma=True) for i in range(3)])
        wdbuf = RR([fw.sb("wd%d" % i, [128, FH * 128], BF16, dma=True) for i in range(2)])
        wpbuf = RR([fw.sb("wp%d" % i, [128, 512], BF16, dma=True) for i in range(2)])
        wabb = fw.sb("wabb", [128, KC * 32], BF16, dma=True)
        small = fw.sb("small", [128, NSM], F32, dma=True)
        xbuf = RR([fw.sb("xb%d" % i, [128, TT], F32, dma=True) for i in range(2)])
        sqb = RR([fw.sb("sq%d" % i, [128, TT], BF16) for i in range(2)])
        rstd = fw.sb("rstd", [128, TT], F32)
        tmpA = RR([fw.sb("tA%d" % i, [128, TT + 16], F32) for i in range(2)])
        tmpB = RR([fw.sb("tB%d" % i, [128, TT + 16], F32) for i in range(2)])
        qkvT = [fw.sb("qkvT%d" % i, [128, TT], BF16) for i in range(3)]
        gateTs = [fw.sb("gateT%d" % i, [128, TT], BF16) for i in range(2)]
        yaT = fw.sb("yaT", [128, TT], F32)
        pooledT = qkvT + [gateTs[0]]
        tokm = [fw.sb("tokm%d" % i, [128, NCH, 128], F32 if i < 2 else BF16) for i in range(3)]
        Sst = fw.sb("Sst", [128, NH, 128], F32)
        Stok = [Tok() for _ in range(NH)]
        c_qkv = fw.sb("c_qkv", [128, 48, 3], F32)
        c_pool = fw.sb("c_pool", [128, 16, 15], F32)
        c_ffn = fw.sb("c_ffn", [128, NFF, 2], F32)
        ab_sb = fw.sb("ab_sb", [128, NCH, 32], F32)
        beta = fw.sb("beta", [128, NCH, 16], F32)
        glog = fw.sb("glog", [128, NCH, 16], F32)
        gcs = fw.sb("gcs", [128, NCH, 16], F32)
        egt = fw.sb("egt", [128, NCH, 16], F32)
        ket = fw.sb("ket", [128, NCH, 16], F32)
        dect = fw.sb("dect", [128, NCH, 16], F32)
        negA = fw.sb("negA", [128, NCH, 16], F32)
        ts0 = fw.sb("ts0", [128, NCH, 16], F32)
        ts1 = fw.sb("ts1", [128, NCH, 16], F32)
        ssk = fw.sb("ssk", [128, NCH], F32)
        ssq = fw.sb("ssq", [128, NCH], F32)
        sc = {n: fw.sb("sc_" + n, [128, NCH], F32) for n in ("rk", "rq", "kg", "ke", "qg")}
        sso = RR([fw.sb("sso%d" % i, [128, 2], F32) for i in range(2)])
        junk = RR([fw.sb("junk%d" % i, [128, 128], F32) for i in range(2)])
        junk2 = RR([fw.sb("junkb%d" % i, [128, 16], F32) for i in range(2)])
        KI = 4
        lsets = []
        for i in range(KI):
            lsets.append(dict(
                dgy=fw.sb("dgy%d" % i, [128, 4, 128], F32),
                kq=fw.sb("kq%d" % i, [128, 2, 128], F32),
                sld=fw.sb("sld%d" % i, [128, 128], F32),
                dTs=fw.sb("dTs%d" % i, [128, 128], F32),
            ))
        psets = []
        for i in range(2 * KI):
            psets.append(dict(
                kgq=fw.sb("kgq%d" % i, [128, 2, 128], F32),
                ketok=fw.sb("ketok%d" % i, [128, 128], F32),
                attnT=fw.sb("attnT%d" % i, [128, 128], F32),
                Tfin=fw.sb("Tfin%d" % i, [128, 128], F32),
            ))
        osb = RR([fw.sb("osb%d" % i, [128, 128], F32) for i in range(2)])
        rp = RR([fw.sb("rp%d" % i, [128, 128], F32) for i in range(2)])
        vnew = RR([fw.sb("vnew%d" % i, [128, 128], F32) for i in range(2)])
        onb = RR([fw.sb("onb%d" % i, [128, 128], BF16) for i in range(2)])
        eps_t = fw.sb("eps_t", [128, 1], F32)
        fw.memset("dve", eps_t[:, :], EPS)
        eps_c = eps_t[:, 0:1]

        pbank = [es.enter_context(nc.psum_tensor("pb%d" % i, [128, 512], F32)) for i in range(7)]
        pbf = es.enter_context(nc.psum_tensor("pbf", [128, 1024], BF16))
        _acc_banks = RR([Tile(pbank[0], Tok()), Tile(pbank[1], Tok())])

        class _AccRR:
            def get(self):
                return [_acc_banks.get(), _acc_banks.get()]

        acc_rr = _AccRR()

        class PS:
            def __init__(self, t, lo, n, tok):
                self.t, self.lo, self.n, self.tok = t, lo, n, tok

            def v(self, a=0, b=None):
                b = self.n if b is None else b
                return V(self.t[:, self.lo + a:self.lo + b], [self.tok])

        g_rr = RR([PS(pbank[2], 0, 512, Tok()), PS(pbank[3], 0, 512, Tok()), PS(pbank[4], 0, 512, Tok()),
                   PS(pbank[5], 0, 512, Tok())])
        q_bank = PS(pbank[6], 0, 512, Tok())
        pair_rr = g_rr
        g_banks = g_rr.items
        sing_rr = g_rr
        pbf_ps = PS(pbf, 0, 1024, Tok())

        def sm(off, n=1):
            return small[:, off:off + n]

        def load_small(l):
            fw.dma("sp", small[:, :], V(small_d.ap()[l], [t_small]), small.tok)
            fw.dma("pool", wabb[:, :], V(wab_d.ap()[l], [t_wab]), wabb.tok)
            fw.act(ts0.re("p a b -> p (a b)"), sm(O_ALOG, 128), AF.Exp)
            fw.ts("dve", negA.re("p a b -> p (a b)"), ts0.re("p a b -> p (a b)"), -1.0, ALU.mult)

        def x_src(l, phase, ti, kc):
            cols = slice(ti * TT, (ti + 1) * TT)
            if l == 0 and phase == 0:
                return V(xT_d.ap()[kc * 128:(kc + 1) * 128, cols], [t_xin])
            return V(yT_d.ap()[kc * 128:(kc + 1) * 128, cols], [t_y[ti][kc]])

        def y_dst(ti, kc):
            cols = slice(ti * TT, (ti + 1) * TT)
            return V(yT_d.ap()[kc * 128:(kc + 1) * 128, cols], [t_y[ti][kc]])

        def rmsnorm_to_hT(l, phase, ti, woff):
            acc = acc_rr.get()
            for kc in range(KC):
                xb = xbuf.get()
                fw.dma("sp", xb[:, :], x_src(l, phase, ti, kc), xb.tok)
                sq = sqb.get()
                fw.act(sq[:, :], xb[:, :], AF.Square)
                for hf in range(2):
                    fw.mm(acc[hf][:, :], onesb, sq[:, hf * 512:(hf + 1) * 512], start=(kc == 0), stop=(kc == KC - 1))
            for hf in range(2):
                fw.act(rstd[:, hf * 512:(hf + 1) * 512], acc[hf][:, :], AF.Ln, scale=1.0 / D, bias=eps_c)
                fw.act(rstd[:, hf * 512:(hf + 1) * 512], rstd[:, hf * 512:(hf + 1) * 512], AF.Exp, scale=-0.5)
            for kc in range(KC):
                xb = xbuf.get()
                fw.dma("sp", xb[:, :], x_src(l, phase, ti, kc), xb.tok)
                fw.stt("dve", hT[kc][:, :], xb[:, :], sm(woff + kc), rstd[:, :], ALU.mult, ALU.mult)

        class WStream:
            def __init__(self, bufs, src_iter):
                self.bufs, self.it, self.issued, self.taken, self.n = bufs, src_iter, 0, 0, len(bufs)
                self.keys = []

            def _issue(self):
                try:
                    key, ap, tok = next(self.it)
                except StopIteration:
                    return False
                wb = self.bufs[self.issued % self.n]
                fw.dma("pool", wb[:, :], V(ap, [tok]), wb.tok)
                self.keys.append(key)
                self.issued += 1
                return True

            def next(self, key):
                m = self.taken
                while self.issued < m + self.n:
                    if not self._issue():
                        break
                assert self.keys[m] == key, (self.keys[m], key)
                self.taken += 1
                return self.bufs[m % self.n]

        def _src_main():
            for l_ in range(L):
                for ti_ in range(NT):
                    for ct_ in range(112):
                        yield ("in", l_, ti_, ct_), win_d.ap()[l_, ct_], t_win
                    for j_ in range(16):
                        yield ("out", l_, ti_, j_), wout_d.ap()[l_, j_], t_wout
                    for f_ in range(2 * NFF):
                        yield ("up", l_, ti_, f_), wup_d.ap()[l_, f_], t_wup

        def _src_dn():
            for l_ in range(L):
                for ti_ in range(NT):
                    for hh_ in range(NQ):
                        for j_ in range(16):
                            yield ("dn", l_, ti_, hh_, j_), wdn_d.ap()[l_, hh_, j_], t_wdn

        def _src_pool():
            for l_ in range(L):
                for ti_ in range(NT):
                    for g_ in range(4):
                        for ot_ in range(4):
                            yield ("pw", l_, ti_, g_, ot_), wpool_d.ap()[l_, g_, ot_], t_wpool

        ws_main = WStream(wbuf.items, _src_main())
        ws_dn = WStream(wdbuf.items, _src_dn())
        ws_pool = WStream(wpbuf.items, _src_pool())

        def proj_g(stream, key, nk=KC, rhs_tiles=None):
            rhs_tiles = rhs_tiles or hT
            wb = stream.next(key)
            acc = acc_rr.get()
            n = 0
            for hf in range(2):
                for kc in range(nk):
                    fw.mm(acc[hf][:, :], wb[:, kc * 128:(kc + 1) * 128], rhs_tiles[kc][:, hf * 512:(hf + 1) * 512],
                          start=(kc == 0), stop=(kc == nk - 1))
                    n += 1
                    if n % 4 == 0 and not (kc == nk - 1):
                        yield
                yield
            return acc

        def drain(g_):
            try:
                while True:
                    next(g_)
            except StopIteration as e_:
                return e_.value

        def proj(stream, key, nk=KC, rhs_tiles=None):
            return drain(proj_g(stream, key, nk=nk, rhs_tiles=rhs_tiles))

        def evac2(eng_fn, acc):
            for hf in range(2):
                eng_fn(hf, acc[hf][:, :], slice(hf * 512, (hf + 1) * 512))

        def token_scalars():
            flat = lambda t: t.re("p a b -> p (a b)")
            for c in range(NCH):
                ps = pair_rr.get()
                for kc in range(KC):
                    fw.mm(ps.v(0, 32), hT[kc][:, c * 128:(c + 1) * 128], wabb[:, kc * 32:(kc + 1) * 32],
                          start=(kc == 0), stop=(kc == KC - 1))
                fw.copy("act", ab_sb[:, c, :], ps.v(0, 32))
            checkpoint("tsc1")
            fw.act(beta[:, :, :], ab_sb[:, :, 0:16], AF.Sigmoid)
            fw.tt("dve", ts0[:, :, :], ab_sb[:, :, 16:32],
                  V(small.t[:, O_DTB:O_DTB + 128].rearrange("p (a b) -> p a b", b=16), [small.tok]), ALU.add)
            fw.act(ts1[:, :, :], ts0[:, :, :], AF.Exp)
            fw.act(ts0[:, :, :], ts1[:, :, :], AF.Ln, bias=1.0)
            fw.tt("dve", glog[:, :, :], ts0[:, :, :], negA[:, :, :], ALU.mult)
            checkpoint("tsc2")
            p1 = pair_rr.get()
            fw.mm(p1.v(0, 128), triuf, flat(glog))
            p2 = pair_rr.get()
            fw.mm(p2.v(0, 128), onesf, flat(glog))
            checkpoint("tsc3")
            fw.copy("act", flat(gcs), p1.v(0, 128))
            fw.act(flat(egt), p1.v(0, 128), AF.Exp)
            fw.tt("dve", flat(ts0), p2.v(0, 128), flat(gcs), ALU.subtract)
            fw.act(flat(ket), flat(ts0), AF.Exp)
            fw.act(flat(dect), p2.v(0, 128), AF.Exp)

        def interleave(gens, bg=None):
            gens = list(gens)
            while gens:
                for g_ in list(gens):
                    try:
                        next(g_)
                    except StopIteration:
                        gens.remove(g_)
                if bg is not None and not bg[1]:
                    try:
                        next(bg[0])
                    except StopIteration:
                        bg[1] = True

        def chain(*gens):
            for g_ in gens:
                yield from g_

        def gdn_local(h, c, Ls, Ps, pb):
            ktok, qtok, vtok = tokm
            col = lambda t: V(t.t[:, c:c + 1], [t.tok])
            colh = lambda t: V(t.t[:, c, h:h + 1], [t.tok])
            d = Ls["dgy"]
            kq = Ls["kq"]
            kgq = Ps["kgq"]
            fw.ts("dve", d[:, 0, :], identf, col(sc["rk"]), ALU.mult); yield
            fw.ts("dve", d[:, 1, :], identf, col(sc["kg"]), ALU.mult); yield
            fw.act(d[:, 2, :], identf, AF.Copy, scale=col(sc["rq"])); yield
            fw.act(d[:, 3, :], identf, AF.Copy, scale=col(sc["qg"])); yield
            p = PS(pb.t, 0, 256, pb.tok)
            fw.mm(p.v(0, 256), ktok[:, c, :], V(d.t[:, 0:2, :].rearrange("p a b -> p (a b)"), [d.tok])); yield
            fw.copy("act", kq[:, 0, :], p.v(0, 128)); yield
            fw.copy("act", kgq[:, 0, :], p.v(128, 256)); yield
            fw.mm(p.v(0, 256), qtok[:, c, :], V(d.t[:, 2:4, :].rearrange("p a b -> p (a b)"), [d.tok])); yield
            fw.copy("act", kq[:, 1, :], p.v(0, 128)); yield
            fw.copy("act", kgq[:, 1, :], p.v(128, 256)); yield
            if c == 0: checkpoint("La")
            ke_t = Ps["ketok"]
            fw.act(ke_t[:, :], ktok[:, c, :], AF.Copy, scale=col(sc["ke"])); yield
            sl = Ls["sld"]
            fw.ts("dve", sl[:, :], slf, colh(glog), ALU.mult); yield
            pd = PS(pb.t, 0, 128, pb.tok)
            fw.mm(pd.v(0, 128), sl[:, :], triuf); yield
            fw.act(sl[:, :], pd.v(0, 128), AF.Exp); yield
            ds_ = Ls["dTs"]
            fw.tt("dve", ds_[:, :], sl[:, :], V(cf.t[:, C_SU:C_SU + 128], [cf.tok]), ALU.mult); yield
            fw.tt("dve", sl[:, :], sl[:, :], V(cf.t[:, C_TRIU:C_TRIU + 128], [cf.tok]), ALU.mult); yield
            if c == 0: checkpoint("Lb")
            pg = PS(pb.t, 0, 256, pb.tok)
            fw.mm(pg.v(0, 256), kq[:, 0, :], kq.re("p a b -> p (a b)")); yield
            at = Ps["attnT"]
            fw.tt("dve", at[:, :], pg.v(128, 256), sl[:, :], ALU.mult); yield
            ymA = V(d.t[:, 0:2, :], [d.tok])
            ymB = V(d.t[:, 2:4, :], [d.tok])
            half = lambda ym, i: V(ym.ap[:, i, :], ym.toks)
            flat2 = lambda ym: V(ym.ap.rearrange("p a b -> p (a b)"), ym.toks)
            fw.stt("dve", half(ymA, 0), pg.v(0, 128), colh(beta), ds_[:, :], ALU.mult, ALU.mult); yield
            fw.tt("dve", half(ymB, 1), identf, half(ymA, 0), ALU.subtract); yield
            pn = PS(pb.t, 0, 128, pb.tok)
            fw.mm(pn.v(0, 128), half(ymA, 0), identf); yield
            ytA, ytB = kq[:, 0, :], kq[:, 1, :]
            fw.copy("act", ytA, pn.v(0, 128)); yield
            if c == 0: checkpoint("Lc")
            ym, ym2, yt, yt2 = ymA, ymB, ytA, ytB
            for k in range(7):
                last = (k == 6)
                first = (k == 0)
                pm = PS(pb.t, 0, 256, pb.tok)
                if first:
                    fw.mm(pm.v(0, 128), yt, half(ym, 0)); yield
                elif last:
                    fw.mm(pm.v(128, 256), yt, half(ym, 1)); yield
                else:
                    fw.mm(pm.v(0, 256), yt, flat2(ym)); yield
                if not last:
                    fw.copy("act", half(ym2, 0), pm.v(0, 128)); yield
                    if not first:
                        fw.tt("dve", half(ym2, 1), pm.v(128, 256), half(ym, 1), ALU.add); yield
                    p2_ = PS(pb.t, 0, 128, pb.tok)
                    fw.mm(p2_.v(0, 128), half(ym, 0), yt); yield
                    fw.copy("act", yt2, p2_.v(0, 128)); yield
                    ym, ym2, yt, yt2 = ym2, ym, yt2, yt
                else:
                    tf = Ps["Tfin"]
                    fw.tt("dve", tf[:, :], pm.v(128, 256), half(ym, 1), ALU.add); yield

        def gdn_seq(h, c, Ps, pq):
            gateT = gateTs[h % 2]
            ktok, qtok, vtok = tokm
            cs = slice(c * 128, (c + 1) * 128)
            colh = lambda t: V(t.t[:, c, h:h + 1], [t.tok])
            Sh = V(Sst.t[:, h, :], [Stok[h]])
            kgq, ke_t, at, tf = Ps["kgq"], Ps["ketok"], Ps["attnT"], Ps["Tfin"]
            if c == 0: checkpoint("Qa")
            pr = PS(pq.t, 0, 128, pq.tok)
            fw.mm(pr.v(0, 128), kgq[:, 0, :], Sh); yield
            r_ = rp.get()
            fw.tt("dve", r_[:, :], vtok[:, c, :], pr.v(0, 128), ALU.subtract); yield
            fw.mm(pr.v(0, 128), tf[:, :], r_[:, :]); yield
            vn = vnew.get()
            fw.act(vn[:, :], pr.v(0, 128), AF.Copy, scale=colh(beta)); yield
            fw.mm(pr.v(0, 128), kgq[:, 1, :], Sh, start=True, stop=False)
            fw.mm(pr.v(0, 128), at[:, :], vn[:, :], start=False, stop=True); yield
            ob32 = osb.get()
            fw.copy("act", ob32[:, :], pr.v(0, 128)); yield
            fw.mm(pr.v(0, 128), ke_t[:, :], vn[:, :]); yield
            fw.stt("dve", Sh, Sh, colh(dect), pr.v(0, 128), ALU.mult, ALU.add); yield
            so = sso.get()
            fw.act(junk.get()[:, :], ob32[:, :], AF.Square, accum=so[:, 0:1]); yield
            fw.act(so[:, 1:2], so[:, 0:1], AF.Ln, scale=1.0 / HD, bias=eps_c); yield
            fw.act(so[:, 1:2], so[:, 1:2], AF.Exp, scale=-0.5); yield
            ob = onb.get()
            fw.act(ob[:, :], ob32[:, :], AF.Copy, scale=so[:, 1:2]); yield
            fw.tr(pbf_ps.v(128, 256), ob[:, :], identb); yield
            fw.stt("dve", yaT[:, cs], pbf_ps.v(128, 256), sm(O_GNW), gateT[:, cs], ALU.mult, ALU.mult); yield

        def gdn_head(l, h, jslot, bg=None):
            kT, qT, vT = qkvT[1], qkvT[0], qkvT[2]
            ktok, qtok, vtok = tokm
            for which, (srcT, dst, ss) in enumerate(((kT, ktok, ssk), (qT, qtok, ssq), (vT, vtok, None))):
                for c in range(NCH):
                    fw.tr(pbf_ps.v(c * 128, (c + 1) * 128), srcT[:, c * 128:(c + 1) * 128], identb)
                if ss is not None:
                    sqt = tmpB.get()
                    fw.act(sqt[:, 0:TT], pbf_ps.v(0, TT), AF.Square)
                    sq3 = V(sqt.t[:, 0:TT].rearrange("p (c d) -> p c d", d=128), [sqt.tok])
                    fw.op("dve", lambda e: e.tensor_reduce(out=ss.t[:, :], in_=sq3.ap, axis=mybir.AxisListType.X, op=ALU.add),
                          reads=sq3.toks, writes=[ss.tok])
                fw.copy("dve" if which != 1 else "act", dst.re("p c d -> p (c d)"), pbf_ps.v(0, TT))
            checkpoint("gdnA%d" % h)
            hs = lambda t: V(t.t[:, :, h], [t.tok])
            fw.act(sc["rk"][:, :], ssk[:, :], AF.Ln, bias=eps_c)
            fw.act(sc["rk"][:, :], sc["rk"][:, :], AF.Exp, scale=-0.5)
            fw.act(sc["rq"][:, :], ssq[:, :], AF.Ln, bias=eps_c)
            fw.act(sc["rq"][:, :], sc["rq"][:, :], AF.Exp, scale=-0.5)
            fw.ts("dve", sc["rq"][:, :], sc["rq"][:, :], float(HD) ** -0.5, ALU.mult)
            fw.tt("dve", sc["kg"][:, :], sc["rk"][:, :], hs(egt), ALU.mult)
            fw.tt("dve", sc["ke"][:, :], sc["rk"][:, :], hs(ket), ALU.mult)
            fw.tt("dve", sc["qg"][:, :], sc["rq"][:, :], hs(egt), ALU.mult)
            L_ = lambda c: gdn_local(h, c, lsets[c % KI], psets[c % (2 * KI)], g_banks[c % KI])
            Q_ = lambda c: gdn_seq(h, c, psets[c % (2 * KI)], q_bank)
            groups = [list(range(c0, c0 + KI)) for c0 in range(0, NCH, KI)]
            bgs = [bg, False] if bg is not None else None
            interleave([L_(c) for c in groups[0]], bgs)
            for gi in range(1, len(groups)):
                interleave([L_(c) for c in groups[gi]] + [chain(*[Q_(c) for c in groups[gi - 1]])], bgs)
            interleave([chain(*[Q_(c) for c in groups[-1]])], bgs)
            if bg is not None:
                drain(bg)
            fw.tt("pool", slots[jslot][:, :], yaT[:, :], slots[jslot][:, :], ALU.add)
            checkpoint("headdone%d" % h)

        def pool_phase_g(l, ti, g):
            win_ = 2 ** (g + 1)
            base = g * 28
            for jj in range(4):
                j = 4 * g + jj
                acc = yield from proj_g(ws_main, ("in", l, ti, base + jj))
                evac2(lambda hf, ps, sl: fw.act(slots[j][:, sl], ps, AF.Sigmoid), acc); yield
            for jj in range(4):
                j = 4 * g + jj
                acc = yield from proj_g(ws_main, ("in", l, ti, base + 4 + jj))
                U = tmpA.get()
                fw.copy("act", U[:, 1:16], c_pool[:, j, :])
                fw.memset("dve", U[:, 0:1], 0.0)
                evac2(lambda hf, ps, sl: fw.copy("act", U[:, 16 + sl.start:16 + sl.stop], ps), acc); yield
                fw.copy("act", c_pool[:, j, :], U[:, TT + 1:TT + 16])
                cur = U
                sh = 1
                e0 = 1
                while sh < win_:
                    nxt = tmpB.get()
                    fw.tt("pool" if sh > 1 else "dve", nxt[:, e0:TT + 16], cur[:, e0:TT + 16], cur[:, e0 - sh:TT + 16 - sh], ALU.add); yield
                    cur = nxt
                    sh *= 2
                    e0 = 2 * sh - 1
                fw.stt("dve", pooledT[jj][:, :], cur[:, 16:TT + 16], 1.0 / win_, U[:, 16:TT + 16], ALU.mult, ALU.subtract); yield
                if ti == 0:
                    tq = junk2.get()
                    fw.tt("dve", tq[:, 0:16], cur[:, 16:32], cf[:, C_ICNT + g * 16:C_ICNT + g * 16 + 16], ALU.mult)
                    fw.tt("dve", pooledT[jj][:, 0:16], tq[:, 0:16], U[:, 16:32], ALU.subtract); yield
            for ot in range(4):
                j = 4 * g + ot
                acc = yield from proj_g(ws_pool, ("pw", l, ti, g, ot), nk=4, rhs_tiles=pooledT)
                evac2(lambda hf, ps, sl: fw.stt("dve", slots[j][:, sl], ps, sm(O_PSC + j), slots[j][:, sl], ALU.mult, ALU.mult), acc); yield

        def head_prep_g(l, ti, h):
            g = h // 4
            base = g * 28 + 8 + (h % 4) * 5
            gateT = gateTs[h % 2]
            for which in range(3):
                cti = which * 16 + h
                acc = yield from proj_g(ws_main, ("in", l, ti, base + which))
                R = tmpA.get()
                A = tmpB.get()
                fw.copy("act", R[:, 0:3], c_qkv[:, cti, :])
                wq = lambda jtap: sm(O_CQKV + cti * 4 + jtap)
                evac2(lambda hf, ps, sl: fw.copy("act", R[:, 3 + sl.start:3 + sl.stop], ps), acc); yield
                evac2(lambda hf, ps, sl: fw.act(A[:, sl], ps, AF.Copy, scale=wq(3)), acc); yield
                fw.copy("act", c_qkv[:, cti, :], R[:, TT:TT + 3])
                fw.stt("dve", A[:, 0:TT], R[:, 2:TT + 2], wq(2), A[:, 0:TT], ALU.mult, ALU.add); yield
                fw.stt("dve", A[:, 0:TT], R[:, 1:TT + 1], wq(1), A[:, 0:TT], ALU.mult, ALU.add); yield
                fw.stt("dve", A[:, 0:TT], R[:, 0:TT], wq(0), A[:, 0:TT], ALU.mult, ALU.add); yield
                fw.act(qkvT[which][:, :], A[:, 0:TT], AF.Silu); yield
            accz = yield from proj_g(ws_main, ("in", l, ti, base + 3))
            Z = tmpB.get()
            evac2(lambda hf, ps, sl: fw.act(Z[:, sl], ps, AF.Silu), accz); yield
            accg = yield from proj_g(ws_main, ("in", l, ti, base + 4))
            evac2(lambda hf, ps, sl: fw.act(gateT[:, sl], ps, AF.Sigmoid), accg); yield
            fw.tt("pool", gateT[:, :], Z[:, 0:TT], gateT[:, :], ALU.mult); yield

        def prep_g(l, ti, h):
            if h % 4 == 0:
                yield from pool_phase_g(l, ti, h // 4)
            yield from head_prep_g(l, ti, h)

        def mixer(l, ti):
            rmsnorm_to_hT(l, 0, ti, O_NMIX)
            checkpoint("rms")
            token_scalars()
            checkpoint("tsc")
            drain(prep_g(l, ti, 0))
            checkpoint("qkv0")
            for h in range(NH):
                gdn_head(l, h, h, prep_g(l, ti, h + 1) if h + 1 < NH else None)
                checkpoint("head%d" % h)
            for j in range(16):
                acc = proj(ws_main, ("out", l, ti, j), rhs_tiles=slots)
                xb = xbuf.get()
                fw.dma("sp", xb[:, :], x_src(l, 0, ti, j), xb.tok)
                evac2(lambda hf, ps, sl: fw.tt("dve", xb[:, sl], ps, xb[:, sl], ALU.add), acc)
                fw.dma("sp", y_dst(ti, j), xb[:, :], st_rr.get())

        def ffn(l, ti):
            rmsnorm_to_hT(l, 1, ti, O_NFFN)
            for hh in range(NQ):
                for ff in range(FH):
                    f = hh * FH + ff
                    accg = proj(ws_main, ("up", l, ti, 2 * f))
                    R = tmpA.get()
                    A = tmpB.get()
                    fw.copy("act", R[:, 0:2], c_ffn[:, f, :])
                    wf = lambda jtap: sm(O_CFW + f * 3 + jtap)
                    evac2(lambda hf, ps, sl: fw.copy("act", R[:, 2 + sl.start:2 + sl.stop], ps), accg)
                    evac2(lambda hf, ps, sl: fw.act(A[:, sl], ps, AF.Identity, scale=wf(2), bias=sm(O_CFB + f)), accg)
                    fw.copy("act", c_ffn[:, f, :], R[:, TT:TT + 2])
                    fw.stt("dve", A[:, 0:TT], R[:, 1:TT + 1], wf(1), A[:, 0:TT], ALU.mult, ALU.add)
                    fw.stt("dve", A[:, 0:TT], R[:, 0:TT], wf(0), A[:, 0:TT], ALU.mult, ALU.add)
                    Gl = tmpB.get()
                    fw.act(Gl[:, 0:TT], A[:, 0:TT], AF.Gelu)
                    accu = proj(ws_main, ("up", l, ti, 2 * f + 1))
                    evac2(lambda hf, ps, sl: fw.tt("dve", slots[ff][:, sl], ps, Gl[:, sl], ALU.mult), accu)
                for j in range(16):
                    acc = proj(ws_dn, ("dn", l, ti, hh, j), nk=FH, rhs_tiles=slots)
                    xb = xbuf.get()
                    fw.dma("sp", xb[:, :], x_src(l, 1, ti, j), xb.tok)
                    evac2(lambda hf, ps, sl: fw.tt("dve", xb[:, sl], ps, xb[:, sl], ALU.add), acc)
                    fw.dma("sp", y_dst(ti, j), xb[:, :], st_rr.get())

        def final_norm(ti):
            acc = acc_rr.get()
            for kc in range(KC):
                xb = xbuf.get()
                fw.dma("sp", xb[:, :], x_src(L, 0, ti, kc), xb.tok)
                sq = sqb.get()
                fw.act(sq[:, :], xb[:, :], AF.Square)
                for hf in range(2):
                    fw.mm(acc[hf][:, :], onesb, sq[:, hf * 512:(hf + 1) * 512], start=(kc == 0), stop=(kc == KC - 1))
            for hf in range(2):
                fw.act(rstd[:, hf * 512:(hf + 1) * 512], acc[hf][:, :], AF.Ln, scale=1.0 / D, bias=eps_c)
                fw.act(rstd[:, hf * 512:(hf + 1) * 512], rstd[:, hf * 512:(hf + 1) * 512], AF.Exp, scale=-0.5)
            for kc in range(KC):
                xb = xbuf.get()
                fw.dma("sp", xb[:, :], x_src(L, 0, ti, kc), xb.tok)
                fw.stt("dve", xb[:, :], xb[:, :], cf[:, C_NFIN + kc:C_NFIN + kc + 1], rstd[:, :],
                       ALU.mult, ALU.mult)
                fw.dma("sp", y_dst(ti, kc), xb[:, :], st_rr.get())

        def dump(name, v, width):
            dd = dr("dbg_" + name, [128, width], kind="ExternalOutput")
            tk = fw.dtok("dbg_" + name)
            fw.dma("pool", V(dd.ap()[:, :], [Tok()]), v, tk)
            dbg_list.append(tk)

        build_program.dump = dump
        build_program.env = locals()
        try:
          for l in range(L):
            load_small(l)
            fw.memset("dve", Sst[:, :, :], 0.0)
            for h in range(NH):
                Stok[h].w = Sst.tok.w
            fw.memset("dve", c_qkv[:, :, :], 0.0)
            fw.memset("dve", c_pool[:, :, :], 0.0)
            fw.memset("dve", c_ffn[:, :, :], 0.0)
            for ti in range(NT):
                mixer(l, ti)
                checkpoint("mixer")
                ffn(l, ti)
                checkpoint("ffn")
          for ti in range(NT):
            final_norm(ti)
        except StopBuild:
            if build_program.on_stop:
                build_program.on_stop(locals())
        for t in fw.dtoks:
            if t.dcnt:
                fw.eng["sp"].wait_ge(t.dsem, t.dcnt)
        for e in ("pe", "act", "dve", "pool"):
            if fw.cnt[e]:
                fw.eng["sp"].wait_ge(fw.sem[e], fw.cnt[e])
        print("program: ins=%d waits=%d" % (fw.nins, fw.nwait), {e: fw.cnt[e] for e in fw.cnt})
    return nc


build_program.on_stop = None
_CACHE = {}


def run(inputs, L, S, ncores, stop_at=None, ret_all=False):
    B = inputs["x"].shape[0]
    pw = prep_weights(inputs, L)
    key = (L, S, stop_at)
    if key not in _CACHE:
        _CACHE[key] = build_program(L, S, stop_at=stop_at)
    nc = _CACHE[key]
    in_maps = []
    for c in range(ncores):
        b = c % B
        m = {"xT": np.ascontiguousarray(inputs["x"][b, :S].T)}
        m.update(pw)
        in_maps.append(m)
    res = run_bass_kernel_spmd(nc, in_maps, core_ids=list(range(ncores)))
    if ret_all:
        return res.results
    out = np.stack([np.ascontiguousarray(res.results[b]["yT"].T) for b in range(B)], axis=0)
    return out.astype(np.float32)


def kernel(**inputs):
    inputs = {k: np.asarray(v, dtype=np.float32) for k, v in inputs.items()}
    return run(inputs, 4, 4096, 8)
```

```python
import numpy as np
from contextlib import ExitStack
import concourse.bass as bass
import concourse.mybir as mybir
from concourse.bass_utils import run_bass_kernel_spmd

F32 = mybir.dt.float32
BF16 = mybir.dt.bfloat16
AF = mybir.ActivationFunctionType
ALU = mybir.AluOpType

D = 2048
NH = 16
HD = 128
DFF = 5632
NFF = DFF // 128
DIN = 4 * D + 2 * NH + D + 2 * D
EPS = 1e-6
TT = 1024
NCH = TT // 128
KC = D // 128
NQ = 4
FH = NFF // NQ


class Tok:
    __slots__ = ("w", "r", "dsem", "dcnt", "name")

    def __init__(self, name=""):
        self.w = None
        self.r = {}
        self.dsem = None
        self.dcnt = 0
        self.name = name


class V:
    __slots__ = ("ap", "toks")

    def __init__(self, ap, toks):
        self.ap = ap
        self.toks = toks


class Tile:
    def __init__(self, t, tok):
        self.t = t
        self.tok = tok

    def __getitem__(self, idx):
        return V(self.t[idx], [self.tok])

    def re(self, pat, **kw):
        return V(self.t[:].rearrange(pat, **kw), [self.tok])


class FW:
    def __init__(self, nc, es):
        self.nc = nc
        self.es = es
        self.eng = {"pe": nc.tensor, "act": nc.scalar, "dve": nc.vector, "pool": nc.gpsimd, "sp": nc.sync}
        self.sem = {e: es.enter_context(nc.semaphore("s_" + e)) for e in self.eng}
        self.cnt = {e: 0 for e in self.eng}
        self.seen = {e: {} for e in self.eng}
        self.nwait = 0
        self.nins = 0
        self.ntile = 0
        self.dtoks = []

    def sb(self, name, shape, dt, dma=False):
        t = self.es.enter_context(self.nc.sbuf_tensor("sb_" + name, list(shape), dt))
        tok = self.dtok(name) if dma else Tok(name)
        return Tile(t, tok)

    def dtok(self, name):
        t = Tok(name)
        t.dsem = self.es.enter_context(self.nc.semaphore("d_" + name))
        self.dtoks.append(t)
        return t

    def _deps(self, e, reads, writes):
        deps = []
        for t in reads:
            if t.w is not None:
                deps.append(t.w)
        for t in writes:
            if t.w is not None:
                deps.append(t.w)
            deps.extend(t.r.values())
        mysem = self.sem[e]
        seen = self.seen[e]
        for (sem, val) in deps:
            if e == "pe" and sem is mysem:
                continue
            if seen.get(sem, 0) >= val:
                continue
            self.eng[e].wait_ge(sem, val)
            seen[sem] = val
            self.nwait += 1

    def op(self, e, build, reads=(), writes=()):
        self._deps(e, reads, writes)
        ins = build(self.eng[e])
        self.cnt[e] += 1
        self.nins += 1
        ins.then_inc(self.sem[e], 1)
        me = (self.sem[e], self.cnt[e])
        s = self.sem[e]
        for t in reads:
            t.r[s] = me
        for t in writes:
            t.w = me
            t.r = {}
        return ins

    def dma(self, q, out, in_, tok):
        reads, writes = in_.toks, out.toks
        self._deps(q, reads, writes)
        if tok.dcnt > 0 and self.seen[q].get(tok.dsem, 0) < tok.dcnt:
            self.eng[q].wait_ge(tok.dsem, tok.dcnt)
            self.seen[q][tok.dsem] = tok.dcnt
            self.nwait += 1
        ins = self.eng[q].dma_start(out=out.ap, in_=in_.ap)
        tok.dcnt += 16
        ins.then_inc(tok.dsem, 16)
        self.nins += 1
        me = (tok.dsem, tok.dcnt)
        for t in reads:
            t.r[tok.dsem] = me
        for t in writes:
            t.w = me
            t.r = {}
        return ins

    def wait_tok(self, e, tok):
        self._deps(e, [tok], [])

    def mm(self, out, lhsT, rhs, start=True, stop=True):
        return self.op("pe", lambda e: e.matmul(out.ap, lhsT=lhsT.ap, rhs=rhs.ap, start=start, stop=stop),
                       reads=lhsT.toks + rhs.toks, writes=out.toks)

    def tr(self, out, in_, ident):
        return self.op("pe", lambda e: e.transpose(out.ap, in_.ap, ident.ap),
                       reads=in_.toks + ident.toks, writes=out.toks)

    def act(self, out, in_, func, bias=None, scale=None, accum=None, eng="act"):
        kw = {}
        reads = list(in_.toks)
        writes = list(out.toks)
        if bias is not None:
            if isinstance(bias, V):
                kw["bias"] = bias.ap
                reads += bias.toks
            else:
                kw["bias"] = bias
        if scale is not None:
            if isinstance(scale, V):
                kw["scale"] = scale.ap
                reads += scale.toks
            else:
                kw["scale"] = scale
        if accum is not None:
            kw["accum_out"] = accum.ap
            writes += accum.toks
        return self.op(eng, lambda e: e.activation(out=out.ap, in_=in_.ap, func=func, **kw), reads=reads, writes=writes)

    def tt(self, eng, out, in0, in1, op):
        return self.op(eng, lambda e: e.tensor_tensor(out=out.ap, in0=in0.ap, in1=in1.ap, op=op),
                       reads=in0.toks + in1.toks, writes=out.toks)

    def ts(self, eng, out, in0, s1, op0, s2=None, op1=None, accum=None):
        reads = list(in0.toks)
        writes = list(out.toks)
        a1 = s1
        if isinstance(s1, V):
            a1 = s1.ap
            reads += s1.toks
        a2 = s2
        if isinstance(s2, V):
            a2 = s2.ap
            reads += s2.toks
        kw = {}
        if op1 is not None:
            kw["op1"] = op1
        if accum is not None:
            kw["accum_out"] = accum.ap
            writes += accum.toks
        return self.op(eng, lambda e: e.tensor_scalar(out=out.ap, in0=in0.ap, scalar1=a1, scalar2=a2, op0=op0, **kw),
                       reads=reads, writes=writes)

    def stt(self, eng, out, in0, scalar, in1, op0, op1):
        reads = in0.toks + in1.toks
        a = scalar
        if isinstance(scalar, V):
            a = scalar.ap
            reads = reads + scalar.toks
        return self.op(eng, lambda e: e.scalar_tensor_tensor(out=out.ap, in0=in0.ap, scalar=a, in1=in1.ap, op0=op0, op1=op1),
                       reads=reads, writes=out.toks)

    def copy(self, eng, out, in_):
        if eng == "act":
            return self.act(out, in_, AF.Copy)
        return self.op(eng, lambda e: e.tensor_copy(out=out.ap, in_=in_.ap), reads=in_.toks, writes=out.toks)

    def memset(self, eng, out, val):
        return self.op(eng, lambda e: e.memset(out.ap, val), reads=[], writes=out.toks)


class RR:
    def __init__(self, items):
        self.items = items
        self.i = 0

    def get(self):
        it = self.items[self.i % len(self.items)]
        self.i += 1
        return it


O_NMIX = 0
O_NFFN = 16
O_PSC = 32
O_CQKV = 48
O_CFW = O_CQKV + 192
O_CFB = O_CFW + 132
O_GNW = O_CFB + 44
O_ALOG = O_GNW + 1
O_DTB = O_ALOG + 128
NSM = O_DTB + 128
C_ID = 0
C_TRIU = 128
C_SU = 256
C_SL = 384
C_ONE = 512
C_ICNT = 640
C_NFIN = 704
NCONST = 720


def _colmajor(w, ncol_tiles):
    K = w.shape[0]
    return np.ascontiguousarray(w.reshape(K // 128, 128, ncol_tiles, 128).transpose(2, 1, 0, 3)).reshape(
        ncol_tiles, 128, (K // 128) * 128)


def prep_weights(inp, L):
    out = {}
    w_in = inp["w_in"]
    order = []
    for g in range(4):
        for j in range(4 * g, 4 * g + 4):
            order.append(4 * D + 2 * NH + D + D + j * 128)
        for j in range(4 * g, 4 * g + 4):
            order.append(4 * D + 2 * NH + j * 128)
        for h in range(4 * g, 4 * g + 4):
            order.append(0 * D + h * 128)
            order.append(1 * D + h * 128)
            order.append(2 * D + h * 128)
            order.append(3 * D + h * 128)
            order.append(4 * D + 2 * NH + D + h * 128)
    cols = np.concatenate([np.arange(o, o + 128) for o in order])
    win = np.empty((L, len(order), 128, D), np.float32)
    wab = np.empty((L, 128, KC * 32), np.float32)
    wout = np.empty((L, 16, 128, D), np.float32)
    wup = np.empty((L, 2 * NFF, 128, D), np.float32)
    wdn = np.empty((L, NQ, 16, 128, FH * 128), np.float32)
    wpool = np.empty((L, 4, 4, 128, 512), np.float32)
    small = np.zeros((L, 128, NSM), np.float32)
    upcols = np.concatenate([np.concatenate([np.arange(f * 128, f * 128 + 128), np.arange(DFF + f * 128, DFF + f * 128 + 128)])
                             for f in range(NFF)])
    for l in range(L):
        win[l] = _colmajor(w_in[l][:, cols], len(order))
        ab = w_in[l][:, 4 * D:4 * D + 32]
        wab[l] = ab.reshape(KC, 128, 32).transpose(1, 0, 2).reshape(128, KC * 32)
        wout[l] = _colmajor(inp["w_out"][l], 16)
        wup[l] = _colmajor(inp["w_up"][l][:, upcols], 2 * NFF)
        wd = inp["w_down"][l]
        for hh in range(NQ):
            blk = wd[hh * FH * 128:(hh + 1) * FH * 128]
            wdn[l, hh] = _colmajor(blk, 16)
        for g in range(4):
            wpool[l, g] = _colmajor(inp["pool_w"][l, g], 4)
        sm = small[l]
        sm[:, O_NMIX:O_NMIX + 16] = inp["norm_mix_w"][l].reshape(16, 128).T
        sm[:, O_NFFN:O_NFFN + 16] = inp["norm_ffn_w"][l].reshape(16, 128).T
        sm[:, O_PSC:O_PSC + 16] = inp["pool_scale"][l].reshape(16, 128).T
        sm[:, O_CQKV:O_CQKV + 192] = inp["conv_qkv_w"][l].reshape(4, 48, 128).transpose(2, 1, 0).reshape(128, 192)
        sm[:, O_CFW:O_CFW + 132] = inp["conv_ffn_w"][l].reshape(3, NFF, 128).transpose(2, 1, 0).reshape(128, 132)
        sm[:, O_CFB:O_CFB + 44] = inp["conv_ffn_b"][l].reshape(NFF, 128).T
        sm[:, O_GNW] = inp["gdn_norm_w"][l]
        sm[:, O_ALOG:O_ALOG + 128] = np.tile(inp["a_log"][l], NCH)[None, :]
        sm[:, O_DTB:O_DTB + 128] = np.tile(inp["dt_bias"][l], NCH)[None, :]
    out.update(win=win, wab=wab, wout=wout, wup=wup, wdn=wdn, wpool=wpool, small=small)
    c = np.zeros((128, NCONST), np.float32)
    idx = np.arange(128)
    c[:, C_ID:C_ID + 128] = np.eye(128, dtype=np.float32)
    c[:, C_TRIU:C_TRIU + 128] = (idx[:, None] <= idx[None, :])
    c[:, C_SU:C_SU + 128] = (idx[None, :] > idx[:, None])
    c[:, C_SL:C_SL + 128] = (idx[:, None] > idx[None, :])
    c[:, C_ONE:C_ONE + 128] = 1.0
    for g, win_ in enumerate((2, 4, 8, 16)):
        t = np.arange(16)
        c[:, C_ICNT + g * 16:C_ICNT + g * 16 + 16] = (np.float32(1.0) / np.minimum(t + 1, win_).astype(np.float32))[None, :]
    c[:, C_NFIN:C_NFIN + 16] = inp["norm_final_w"].reshape(16, 128).T
    out["consts"] = c
    return out


class StopBuild(Exception):
    pass


def build_program(L, S, dbg=None, stop_at=None):
    assert S % TT == 0
    NT = S // TT
    nc = bass.Bass("TRN2", target_bir_lowering=False)
    dr = lambda name, shape, kind="ExternalInput": nc.dram_tensor(name, list(shape), F32, kind=kind)
    xT_d = dr("xT", [D, S])
    win_d = dr("win", [L, 112, 128, D])
    wab_d = dr("wab", [L, 128, KC * 32])
    wout_d = dr("wout", [L, 16, 128, D])
    wup_d = dr("wup", [L, 2 * NFF, 128, D])
    wdn_d = dr("wdn", [L, NQ, 16, 128, FH * 128])
    wpool_d = dr("wpool", [L, 4, 4, 128, 512])
    small_d = dr("small", [L, 128, NSM])
    consts_d = dr("consts", [128, NCONST])
    yT_d = dr("yT", [D, S], kind="ExternalOutput")
    dbg_list = []

    def checkpoint(name):
        if stop_at == name:
            raise StopBuild()

    with ExitStack() as es:
        fw = FW(nc, es)
        t_win, t_wab, t_wout, t_wup, t_wdn, t_wpool, t_small, t_consts = (Tok() for _ in range(8))
        t_xin = Tok()
        t_y = [[Tok() for _ in range(KC)] for _ in range(NT)]
        st_tok = [fw.dtok("st%d" % i) for i in range(3)]
        st_rr = RR(st_tok)

        cf = fw.sb("cf", [128, NCONST], F32, dma=True)
        cb = fw.sb("cb", [128, 640], BF16, dma=True)
        fw.dma("sp", cf[:, :], V(consts_d.ap()[:, :], [t_consts]), cf.tok)
        fw.dma("pool", cb[:, :], V(consts_d.ap()[:, 0:640], [t_consts]), cb.tok)
        identb = cb[:, C_ID:C_ID + 128]
        triub = cb[:, C_TRIU:C_TRIU + 128]
        sub_ = cb[:, C_SU:C_SU + 128]
        onesb = cb[:, C_ONE:C_ONE + 128]
        triuf = cf[:, C_TRIU:C_TRIU + 128]
        slf = cf[:, C_SL:C_SL + 128]
        onesf = cf[:, C_ONE:C_ONE + 128]
        identf = cf[:, C_ID:C_ID + 128]

        hT = [fw.sb("hT%d" % k, [128, TT], BF16) for k in range(KC)]
        slots = [fw.sb("slot%d" % k, [128, TT], BF16) for k in range(16)]
        wbuf = RR([fw.sb("wb%d" % i, [128, D], BF16, dma=True) for i in range(3)])
        wdbuf = RR([fw.sb("wd%d" % i, [128, FH * 128], BF16, dma=True) for i in range(2)])
        wpbuf = RR([fw.sb("wp%d" % i, [128, 512], BF16, dma=True) for i in range(2)])
        wabb = fw.sb("wabb", [128, KC * 32], BF16, dma=True)
        small = fw.sb("small", [128, NSM], F32, dma=True)
        xbuf = RR([fw.sb("xb%d" % i, [128, TT], F32, dma=True) for i in range(2)])
        sqb = RR([fw.sb("sq%d" % i, [128, TT], BF16) for i in range(2)])
        rstd = fw.sb("rstd", [128, TT], F32)
        tmpA = RR([fw.sb("tA%d" % i, [128, TT + 16], F32) for i in range(2)])
        tmpB = RR([fw.sb("tB%d" % i, [128, TT + 16], F32) for i in range(2)])
        qkvT = [fw.sb("qkvT%d" % i, [128, TT], BF16) for i in range(3)]
        gateTs = [fw.sb("gateT%d" % i, [128, TT], BF16) for i in range(2)]
        yaT = fw.sb("yaT", [128, TT], F32)
        pooledT = qkvT + [gateTs[0]]
        tokm = [fw.sb("tokm%d" % i, [128, NCH, 128], F32 if i < 2 else BF16) for i in range(3)]
        Sst = fw.sb("Sst", [128, NH, 128], F32)
        Stok = [Tok() for _ in range(NH)]
        c_qkv = fw.sb("c_qkv", [128, 48, 3], F32)
        c_pool = fw.sb("c_pool", [128, 16, 15], F32)
        c_ffn = fw.sb("c_ffn", [128, NFF, 2], F32)
        ab_sb = fw.sb("ab_sb", [128, NCH, 32], F32)
        beta = fw.sb("beta", [128, NCH, 16], F32)
        glog = fw.sb("glog", [128, NCH, 16], F32)
        gcs = fw.sb("gcs", [128, NCH, 16], F32)
        egt = fw.sb("egt", [128, NCH, 16], F32)
        ket = fw.sb("ket", [128, NCH, 16], F32)
        dect = fw.sb("dect", [128, NCH, 16], F32)
        negA = fw.sb("negA", [128, NCH, 16], F32)
        ts0 = fw.sb("ts0", [128, NCH, 16], F32)
        ts1 = fw.sb("ts1", [128, NCH, 16], F32)
        ssk = fw.sb("ssk", [128, NCH], F32)
        ssq = fw.sb("ssq", [128, NCH], F32)
        sc = {n: fw.sb("sc_" + n, [128, NCH], F32) for n in ("rk", "rq", "kg", "ke", "qg")}
        sso = RR([fw.sb("sso%d" % i, [128, 2], F32) for i in range(2)])
        junk = RR([fw.sb("junk%d" % i, [128, 128], F32) for i in range(2)])
        junk2 = RR([fw.sb("junkb%d" % i, [128, 16], F32) for i in range(2)])
        KI = 4
        lsets = []
        for i in range(KI):
            lsets.append(dict(
                dgy=fw.sb("dgy%d" % i, [128, 4, 128], F32),
                kq=fw.sb("kq%d" % i, [128, 2, 128], F32),
                sld=fw.sb("sld%d" % i, [128, 128], F32),
                dTs=fw.sb("dTs%d" % i, [128, 128], F32),
            ))
        psets = []
        for i in range(2 * KI):
            psets.append(dict(
                kgq=fw.sb("kgq%d" % i, [128, 2, 128], F32),
                ketok=fw.sb("ketok%d" % i, [128, 128], F32),
                attnT=fw.sb("attnT%d" % i, [128, 128], F32),
                Tfin=fw.sb("Tfin%d" % i, [128, 128], F32),
            ))
        osb = RR([fw.sb("osb%d" % i, [128, 128], F32) for i in range(2)])
        rp = RR([fw.sb("rp%d" % i, [128, 128], F32) for i in range(2)])
        vnew = RR([fw.sb("vnew%d" % i, [128, 128], F32) for i in range(2)])
        onb = RR([fw.sb("onb%d" % i, [128, 128], BF16) for i in range(2)])
        eps_t = fw.sb("eps_t", [128, 1], F32)
        fw.memset("dve", eps_t[:, :], EPS)
        eps_c = eps_t[:, 0:1]

        pbank = [es.enter_context(nc.psum_tensor("pb%d" % i, [128, 512], F32)) for i in range(7)]
        pbf = es.enter_context(nc.psum_tensor("pbf", [128, 1024], BF16))
        _acc_banks = RR([Tile(pbank[0], Tok()), Tile(pbank[1], Tok())])

        class _AccRR:
            def get(self):
                return [_acc_banks.get(), _acc_banks.get()]

        acc_rr = _AccRR()

        class PS:
            def __init__(self, t, lo, n, tok):
                self.t, self.lo, self.n, self.tok = t, lo, n, tok

            def v(self, a=0, b=None):
                b = self.n if b is None else b
                return V(self.t[:, self.lo + a:self.lo + b], [self.tok])

        g_rr = RR([PS(pbank[2], 0, 512, Tok()), PS(pbank[3], 0, 512, Tok()), PS(pbank[4], 0, 512, Tok()),
                   PS(pbank[5], 0, 512, Tok())])
        q_bank = PS(pbank[6], 0, 512, Tok())
        _wide = [Tile(pbank[0], _acc_banks.items[0].tok), Tile(pbank[1], _acc_banks.items[1].tok)] + \
                [Tile(pbank[2 + i], g_rr.items[i].tok) for i in range(4)]
        _wide_rr = RR(_wide)

        class _AccWide:
            def get(self):
                return [_wide_rr.get(), _wide_rr.get()]

        acc_wide = _AccWide()
        pair_rr = g_rr
        g_banks = g_rr.items
        sing_rr = g_rr
        pbf_ps = PS(pbf, 0, 1024, Tok())

        def sm(off, n=1):
            return small[:, off:off + n]

        def load_small(l):
            fw.dma("sp", small[:, :], V(small_d.ap()[l], [t_small]), small.tok)
            fw.dma("pool", wabb[:, :], V(wab_d.ap()[l], [t_wab]), wabb.tok)
            fw.act(ts0.re("p a b -> p (a b)"), sm(O_ALOG, 128), AF.Exp)
            fw.ts("dve", negA.re("p a b -> p (a b)"), ts0.re("p a b -> p (a b)"), -1.0, ALU.mult)

        def x_src(l, phase, ti, kc):
            cols = slice(ti * TT, (ti + 1) * TT)
            if l == 0 and phase == 0:
                return V(xT_d.ap()[kc * 128:(kc + 1) * 128, cols], [t_xin])
            return V(yT_d.ap()[kc * 128:(kc + 1) * 128, cols], [t_y[ti][kc]])

        def y_dst(ti, kc):
            cols = slice(ti * TT, (ti + 1) * TT)
            return V(yT_d.ap()[kc * 128:(kc + 1) * 128, cols], [t_y[ti][kc]])

        def rmsnorm_to_hT(l, phase, ti, woff):
            acc = acc_rr.get()
            for kc in range(KC):
                xb = xbuf.get()
                fw.dma("sp", xb[:, :], x_src(l, phase, ti, kc), xb.tok)
                sq = sqb.get()
                fw.act(sq[:, :], xb[:, :], AF.Square)
                for hf in range(2):
                    fw.mm(acc[hf][:, :], onesb, sq[:, hf * 512:(hf + 1) * 512], start=(kc == 0), stop=(kc == KC - 1))
            for hf in range(2):
                fw.act(rstd[:, hf * 512:(hf + 1) * 512], acc[hf][:, :], AF.Ln, scale=1.0 / D, bias=eps_c)
                fw.act(rstd[:, hf * 512:(hf + 1) * 512], rstd[:, hf * 512:(hf + 1) * 512], AF.Exp, scale=-0.5)
            for kc in range(KC):
                xb = xbuf.get()
                fw.dma("sp", xb[:, :], x_src(l, phase, ti, kc), xb.tok)
                fw.stt("dve", hT[kc][:, :], xb[:, :], sm(woff + kc), rstd[:, :], ALU.mult, ALU.mult)

        class WStream:
            def __init__(self, bufs, src_iter):
                self.bufs, self.it, self.issued, self.taken, self.n = bufs, src_iter, 0, 0, len(bufs)
                self.keys = []

            def _issue(self):
                try:
                    key, ap, tok = next(self.it)
                except StopIteration:
                    return False
                wb = self.bufs[self.issued % self.n]
                fw.dma("pool", wb[:, :], V(ap, [tok]), wb.tok)
                self.keys.append(key)
                self.issued += 1
                return True

            def next(self, key):
                m = self.taken
                while self.issued < m + self.n:
                    if not self._issue():
                        break
                assert self.keys[m] == key, (self.keys[m], key)
                self.taken += 1
                return self.bufs[m % self.n]

        def _src_main():
            for l_ in range(L):
                for ti_ in range(NT):
                    for ct_ in range(112):
                        yield ("in", l_, ti_, ct_), win_d.ap()[l_, ct_], t_win
                    for j_ in range(16):
                        yield ("out", l_, ti_, j_), wout_d.ap()[l_, j_], t_wout
                    for f_ in range(2 * NFF):
                        yield ("up", l_, ti_, f_), wup_d.ap()[l_, f_], t_wup

        def _src_dn():
            for l_ in range(L):
                for ti_ in range(NT):
                    for hh_ in range(NQ):
                        for j_ in range(16):
                            yield ("dn", l_, ti_, hh_, j_), wdn_d.ap()[l_, hh_, j_], t_wdn

        def _src_pool():
            for l_ in range(L):
                for ti_ in range(NT):
                    for g_ in range(4):
                        for ot_ in range(4):
                            yield ("pw", l_, ti_, g_, ot_), wpool_d.ap()[l_, g_, ot_], t_wpool

        ws_main = WStream(wbuf.items, _src_main())
        ws_dn = WStream(wdbuf.items, _src_dn())
        ws_pool = WStream(wpbuf.items, _src_pool())

        def proj_g(stream, key, nk=KC, rhs_tiles=None, wide=False):
            rhs_tiles = rhs_tiles or hT
            wb = stream.next(key)
            acc = (acc_wide if wide else acc_rr).get()
            n = 0
            for hf in range(2):
                for kc in range(nk):
                    fw.mm(acc[hf][:, :], wb[:, kc * 128:(kc + 1) * 128], rhs_tiles[kc][:, hf * 512:(hf + 1) * 512],
                          start=(kc == 0), stop=(kc == nk - 1))
                    n += 1
                    if n % 4 == 0 and not (kc == nk - 1):
                        yield
                yield
            return acc

        def drain(g_):
            try:
                while True:
                    next(g_)
            except StopIteration as e_:
                return e_.value

        def proj(stream, key, nk=KC, rhs_tiles=None, wide=False):
            return drain(proj_g(stream, key, nk=nk, rhs_tiles=rhs_tiles, wide=wide))

        def evac2(eng_fn, acc):
            for hf in range(2):
                eng_fn(hf, acc[hf][:, :], slice(hf * 512, (hf + 1) * 512))

        def token_scalars():
            flat = lambda t: t.re("p a b -> p (a b)")
            for c in range(NCH):
                ps = pair_rr.get()
                for kc in range(KC):
                    fw.mm(ps.v(0, 32), hT[kc][:, c * 128:(c + 1) * 128], wabb[:, kc * 32:(kc + 1) * 32],
                          start=(kc == 0), stop=(kc == KC - 1))
                fw.copy("act", ab_sb[:, c, :], ps.v(0, 32))
            checkpoint("tsc1")
            fw.act(beta[:, :, :], ab_sb[:, :, 0:16], AF.Sigmoid)
            fw.tt("dve", ts0[:, :, :], ab_sb[:, :, 16:32],
                  V(small.t[:, O_DTB:O_DTB + 128].rearrange("p (a b) -> p a b", b=16), [small.tok]), ALU.add)
            fw.act(ts1[:, :, :], ts0[:, :, :], AF.Exp)
            fw.act(ts0[:, :, :], ts1[:, :, :], AF.Ln, bias=1.0)
            fw.tt("dve", glog[:, :, :], ts0[:, :, :], negA[:, :, :], ALU.mult)
            checkpoint("tsc2")
            p1 = pair_rr.get()
            fw.mm(p1.v(0, 128), triuf, flat(glog))
            p2 = pair_rr.get()
            fw.mm(p2.v(0, 128), onesf, flat(glog))
            checkpoint("tsc3")
            fw.copy("act", flat(gcs), p1.v(0, 128))
            fw.act(flat(egt), p1.v(0, 128), AF.Exp)
            fw.tt("dve", flat(ts0), p2.v(0, 128), flat(gcs), ALU.subtract)
            fw.act(flat(ket), flat(ts0), AF.Exp)
            fw.act(flat(dect), p2.v(0, 128), AF.Exp)

        def interleave(gens, bg=None):
            gens = list(gens)
            while gens:
                for g_ in list(gens):
                    try:
                        next(g_)
                    except StopIteration:
                        gens.remove(g_)
                if bg is not None and not bg[1]:
                    try:
                        next(bg[0])
                    except StopIteration:
                        bg[1] = True

        def chain(*gens):
            for g_ in gens:
                yield from g_

        def gdn_local(h, c, Ls, Ps, pb):
            ktok, qtok, vtok = tokm
            col = lambda t: V(t.t[:, c:c + 1], [t.tok])
            colh = lambda t: V(t.t[:, c, h:h + 1], [t.tok])
            d = Ls["dgy"]
            kq = Ls["kq"]
            kgq = Ps["kgq"]
            fw.ts("dve", d[:, 0, :], identf, col(sc["rk"]), ALU.mult); yield
            fw.ts("dve", d[:, 1, :], identf, col(sc["kg"]), ALU.mult); yield
            fw.act(d[:, 2, :], identf, AF.Copy, scale=col(sc["rq"])); yield
            fw.act(d[:, 3, :], identf, AF.Copy, scale=col(sc["qg"])); yield
            p = PS(pb.t, 0, 256, pb.tok)
            fw.mm(p.v(0, 256), ktok[:, c, :], V(d.t[:, 0:2, :].rearrange("p a b -> p (a b)"), [d.tok])); yield
            fw.copy("act", kq[:, 0, :], p.v(0, 128)); yield
            fw.copy("act", kgq[:, 0, :], p.v(128, 256)); yield
            fw.mm(p.v(0, 256), qtok[:, c, :], V(d.t[:, 2:4, :].rearrange("p a b -> p (a b)"), [d.tok])); yield
            fw.copy("act", kq[:, 1, :], p.v(0, 128)); yield
            fw.copy("act", kgq[:, 1, :], p.v(128, 256)); yield
            if c == 0: checkpoint("La")
            ke_t = Ps["ketok"]
            fw.act(ke_t[:, :], ktok[:, c, :], AF.Copy, scale=col(sc["ke"])); yield
            sl = Ls["sld"]
            fw.ts("dve", sl[:, :], slf, colh(glog), ALU.mult); yield
            pd = PS(pb.t, 0, 128, pb.tok)
            fw.mm(pd.v(0, 128), sl[:, :], triuf); yield
            fw.act(sl[:, :], pd.v(0, 128), AF.Exp); yield
            ds_ = Ls["dTs"]
            fw.tt("dve", ds_[:, :], sl[:, :], V(cf.t[:, C_SU:C_SU + 128], [cf.tok]), ALU.mult); yield
            fw.tt("dve", sl[:, :], sl[:, :], V(cf.t[:, C_TRIU:C_TRIU + 128], [cf.tok]), ALU.mult); yield
            if c == 0: checkpoint("Lb")
            pg = PS(pb.t, 0, 256, pb.tok)
            fw.mm(pg.v(0, 256), kq[:, 0, :], kq.re("p a b -> p (a b)")); yield
            at = Ps["attnT"]
            fw.tt("dve", at[:, :], pg.v(128, 256), sl[:, :], ALU.mult); yield
            ymA = V(d.t[:, 0:2, :], [d.tok])
            ymB = V(d.t[:, 2:4, :], [d.tok])
            half = lambda ym, i: V(ym.ap[:, i, :], ym.toks)
            flat2 = lambda ym: V(ym.ap.rearrange("p a b -> p (a b)"), ym.toks)
            fw.stt("dve", half(ymA, 0), pg.v(0, 128), colh(beta), ds_[:, :], ALU.mult, ALU.mult); yield
            fw.tt("dve", half(ymB, 1), identf, half(ymA, 0), ALU.subtract); yield
            pn = PS(pb.t, 0, 128, pb.tok)
            fw.mm(pn.v(0, 128), half(ymA, 0), identf); yield
            ytA, ytB = kq[:, 0, :], kq[:, 1, :]
            fw.copy("act", ytA, pn.v(0, 128)); yield
            if c == 0: checkpoint("Lc")
            ym, ym2, yt, yt2 = ymA, ymB, ytA, ytB
            for k in range(7):
                last = (k == 6)
                first = (k == 0)
                pm = PS(pb.t, 0, 256, pb.tok)
                if first:
                    fw.mm(pm.v(0, 128), yt, half(ym, 0)); yield
                elif last:
                    fw.mm(pm.v(128, 256), yt, half(ym, 1)); yield
                else:
                    fw.mm(pm.v(0, 256), yt, flat2(ym)); yield
                if not last:
                    fw.copy("act", half(ym2, 0), pm.v(0, 128)); yield
                    if not first:
                        fw.tt("dve", half(ym2, 1), pm.v(128, 256), half(ym, 1), ALU.add); yield
                    p2_ = PS(pb.t, 0, 128, pb.tok)
                    fw.mm(p2_.v(0, 128), half(ym, 0), yt); yield
                    fw.copy("act", yt2, p2_.v(0, 128)); yield
                    ym, ym2, yt, yt2 = ym2, ym, yt2, yt
                else:
                    tf = Ps["Tfin"]
                    fw.tt("dve", tf[:, :], pm.v(128, 256), half(ym, 1), ALU.add); yield

        def gdn_state(h, c, Ps, pq):
            ktok, qtok, vtok = tokm
            colh = lambda t: V(t.t[:, c, h:h + 1], [t.tok])
            Sh = V(Sst.t[:, h, :], [Stok[h]])
            kgq, ke_t, at, tf = Ps["kgq"], Ps["ketok"], Ps["attnT"], Ps["Tfin"]
            if c == 0: checkpoint("Qa")
            pr = PS(pq.t, 0, 128, pq.tok)
            fw.mm(pr.v(0, 128), kgq[:, 0, :], Sh); yield
            r_ = rp.get()
            fw.tt("dve", r_[:, :], vtok[:, c, :], pr.v(0, 128), ALU.subtract); yield
            fw.mm(pr.v(0, 128), tf[:, :], r_[:, :]); yield
            vn = vnew.get()
            fw.act(vn[:, :], pr.v(0, 128), AF.Copy, scale=colh(beta)); yield
            fw.mm(pr.v(0, 128), kgq[:, 1, :], Sh, start=True, stop=False)
            fw.mm(pr.v(0, 128), at[:, :], vn[:, :], start=False, stop=True); yield
            ob32 = osb.get()
            fw.copy("act", ob32[:, :], pr.v(0, 128)); yield
            fw.mm(pr.v(0, 128), ke_t[:, :], vn[:, :]); yield
            fw.stt("dve", Sh, Sh, colh(dect), pr.v(0, 128), ALU.mult, ALU.add); yield
            return ob32

        def gdn_out(h, c, ob32):
            gateT = gateTs[h % 2]
            cs = slice(c * 128, (c + 1) * 128)
            so = sso.get()
            fw.act(junk.get()[:, :], ob32[:, :], AF.Square, accum=so[:, 0:1]); yield
            fw.act(so[:, 1:2], so[:, 0:1], AF.Ln, scale=1.0 / HD, bias=eps_c); yield
            fw.act(so[:, 1:2], so[:, 1:2], AF.Exp, scale=-0.5); yield
            ob = onb.get()
            fw.act(ob[:, :], ob32[:, :], AF.Copy, scale=so[:, 1:2]); yield
            fw.tr(pbf_ps.v(128, 256), ob[:, :], identb); yield
            fw.stt("dve", yaT[:, cs], pbf_ps.v(128, 256), sm(O_GNW), gateT[:, cs], ALU.mult, ALU.mult); yield

        def q_chain(h, cs_):
            go = None
            for c in cs_:
                gs = gdn_state(h, c, psets[c % (2 * KI)], q_bank)
                ob32 = None
                gs_done = False
                while not gs_done or go is not None:
                    if not gs_done:
                        try:
                            next(gs)
                        except StopIteration as e_:
                            ob32 = e_.value
                            gs_done = True
                    if go is not None:
                        try:
                            next(go)
                        except StopIteration:
                            go = None
                    yield
                go = gdn_out(h, c, ob32)
            yield from go

        def gdn_head(l, h, jslot, bg=None):
            kT, qT, vT = qkvT[1], qkvT[0], qkvT[2]
            ktok, qtok, vtok = tokm
            for which, (srcT, dst, ss) in enumerate(((kT, ktok, ssk), (qT, qtok, ssq), (vT, vtok, None))):
                for c in range(NCH):
                    fw.tr(pbf_ps.v(c * 128, (c + 1) * 128), srcT[:, c * 128:(c + 1) * 128], identb)
                if ss is not None:
                    sqt = tmpB.get()
                    fw.act(sqt[:, 0:TT], pbf_ps.v(0, TT), AF.Square)
                    sq3 = V(sqt.t[:, 0:TT].rearrange("p (c d) -> p c d", d=128), [sqt.tok])
                    fw.op("dve", lambda e: e.tensor_reduce(out=ss.t[:, :], in_=sq3.ap, axis=mybir.AxisListType.X, op=ALU.add),
                          reads=sq3.toks, writes=[ss.tok])
                fw.copy("dve" if which != 1 else "act", dst.re("p c d -> p (c d)"), pbf_ps.v(0, TT))
            checkpoint("gdnA%d" % h)
            hs = lambda t: V(t.t[:, :, h], [t.tok])
            fw.act(sc["rk"][:, :], ssk[:, :], AF.Ln, bias=eps_c)
            fw.act(sc["rk"][:, :], sc["rk"][:, :], AF.Exp, scale=-0.5)
            fw.act(sc["rq"][:, :], ssq[:, :], AF.Ln, bias=eps_c)
            fw.act(sc["rq"][:, :], sc["rq"][:, :], AF.Exp, scale=-0.5)
            fw.ts("dve", sc["rq"][:, :], sc["rq"][:, :], float(HD) ** -0.5, ALU.mult)
            fw.tt("dve", sc["kg"][:, :], sc["rk"][:, :], hs(egt), ALU.mult)
            fw.tt("dve", sc["ke"][:, :], sc["rk"][:, :], hs(ket), ALU.mult)
            fw.tt("dve", sc["qg"][:, :], sc["rq"][:, :], hs(egt), ALU.mult)
            L_ = lambda c: gdn_local(h, c, lsets[c % KI], psets[c % (2 * KI)], g_banks[c % KI])
            groups = [list(range(c0, c0 + KI)) for c0 in range(0, NCH, KI)]
            bgs = [bg, False] if bg is not None else None
            interleave([L_(c) for c in groups[0]], bgs)
            for gi in range(1, len(groups)):
                interleave([L_(c) for c in groups[gi]] + [q_chain(h, groups[gi - 1])], bgs)
            interleave([q_chain(h, groups[-1])], bgs)
            if bg is not None:
                drain(bg)
            fw.tt("pool", slots[jslot][:, :], yaT[:, :], slots[jslot][:, :], ALU.add)
            checkpoint("headdone%d" % h)

        def pool_phase_g(l, ti, g):
            win_ = 2 ** (g + 1)
            base = g * 28
            for jj in range(4):
                j = 4 * g + jj
                acc = yield from proj_g(ws_main, ("in", l, ti, base + jj))
                evac2(lambda hf, ps, sl: fw.act(slots[j][:, sl], ps, AF.Sigmoid), acc); yield
            for jj in range(4):
                j = 4 * g + jj
                acc = yield from proj_g(ws_main, ("in", l, ti, base + 4 + jj))
                U = tmpA.get()
                fw.copy("act", U[:, 1:16], c_pool[:, j, :])
                fw.memset("dve", U[:, 0:1], 0.0)
                evac2(lambda hf, ps, sl: fw.copy("act", U[:, 16 + sl.start:16 + sl.stop], ps), acc); yield
                fw.copy("act", c_pool[:, j, :], U[:, TT + 1:TT + 16])
                cur = U
                sh = 1
                e0 = 1
                while sh < win_:
                    nxt = tmpB.get()
                    fw.tt("pool" if sh > 1 else "dve", nxt[:, e0:TT + 16], cur[:, e0:TT + 16], cur[:, e0 - sh:TT + 16 - sh], ALU.add); yield
                    cur = nxt
                    sh *= 2
                    e0 = 2 * sh - 1
                fw.stt("dve", pooledT[jj][:, :], cur[:, 16:TT + 16], 1.0 / win_, U[:, 16:TT + 16], ALU.mult, ALU.subtract); yield
                if ti == 0:
                    tq = junk2.get()
                    fw.tt("dve", tq[:, 0:16], cur[:, 16:32], cf[:, C_ICNT + g * 16:C_ICNT + g * 16 + 16], ALU.mult)
                    fw.tt("dve", pooledT[jj][:, 0:16], tq[:, 0:16], U[:, 16:32], ALU.subtract); yield
            for ot in range(4):
                j = 4 * g + ot
                acc = yield from proj_g(ws_pool, ("pw", l, ti, g, ot), nk=4, rhs_tiles=pooledT)
                evac2(lambda hf, ps, sl: fw.stt("dve", slots[j][:, sl], ps, sm(O_PSC + j), slots[j][:, sl], ALU.mult, ALU.mult), acc); yield

        def head_prep_g(l, ti, h):
            g = h // 4
            base = g * 28 + 8 + (h % 4) * 5
            gateT = gateTs[h % 2]
            for which in range(3):
                cti = which * 16 + h
                acc = yield from proj_g(ws_main, ("in", l, ti, base + which))
                R = tmpA.get()
                A = tmpB.get()
                fw.copy("act", R[:, 0:3], c_qkv[:, cti, :])
                wq = lambda jtap: sm(O_CQKV + cti * 4 + jtap)
                evac2(lambda hf, ps, sl: fw.copy("act", R[:, 3 + sl.start:3 + sl.stop], ps), acc); yield
                fw.act(A[:, 0:TT], R[:, 3:TT + 3], AF.Copy, scale=wq(3)); yield
                fw.copy("act", c_qkv[:, cti, :], R[:, TT:TT + 3])
                fw.stt("dve", A[:, 0:TT], R[:, 2:TT + 2], wq(2), A[:, 0:TT], ALU.mult, ALU.add); yield
                fw.stt("dve", A[:, 0:TT], R[:, 1:TT + 1], wq(1), A[:, 0:TT], ALU.mult, ALU.add); yield
                fw.stt("dve", A[:, 0:TT], R[:, 0:TT], wq(0), A[:, 0:TT], ALU.mult, ALU.add); yield
                fw.act(qkvT[which][:, :], A[:, 0:TT], AF.Silu); yield
            accz = yield from proj_g(ws_main, ("in", l, ti, base + 3))
            Z = tmpB.get()
            evac2(lambda hf, ps, sl: fw.act(Z[:, sl], ps, AF.Silu), accz); yield
            accg = yield from proj_g(ws_main, ("in", l, ti, base + 4))
            evac2(lambda hf, ps, sl: fw.act(gateT[:, sl], ps, AF.Sigmoid), accg); yield
            fw.tt("pool", gateT[:, :], Z[:, 0:TT], gateT[:, :], ALU.mult); yield

        def prep_g(l, ti, h):
            if h % 4 == 0:
                yield from pool_phase_g(l, ti, h // 4)
            yield from head_prep_g(l, ti, h)

        def mixer(l, ti):
            rmsnorm_to_hT(l, 0, ti, O_NMIX)
            checkpoint("rms")
            token_scalars()
            checkpoint("tsc")
            drain(prep_g(l, ti, 0))
            checkpoint("qkv0")
            for h in range(NH):
                gdn_head(l, h, h, prep_g(l, ti, h + 1) if h + 1 < NH else None)
                checkpoint("head%d" % h)
            for j in range(16):
                acc = proj(ws_main, ("out", l, ti, j), rhs_tiles=slots, wide=True)
                xb = xbuf.get()
                fw.dma("sp", xb[:, :], x_src(l, 0, ti, j), xb.tok)
                evac2(lambda hf, ps, sl: fw.tt("dve", xb[:, sl], ps, xb[:, sl], ALU.add), acc)
                fw.dma("sp", y_dst(ti, j), xb[:, :], st_rr.get())

        def ffn(l, ti):
            rmsnorm_to_hT(l, 1, ti, O_NFFN)
            for hh in range(NQ):
                for ff in range(FH):
                    f = hh * FH + ff
                    accg = proj(ws_main, ("up", l, ti, 2 * f), wide=True)
                    R = tmpA.get()
                    A = tmpB.get()
                    fw.copy("act", R[:, 0:2], c_ffn[:, f, :])
                    wf = lambda jtap: sm(O_CFW + f * 3 + jtap)
                    evac2(lambda hf, ps, sl: fw.copy("act", R[:, 2 + sl.start:2 + sl.stop], ps), accg)
                    evac2(lambda hf, ps, sl: fw.act(A[:, sl], ps, AF.Identity, scale=wf(2), bias=sm(O_CFB + f)), accg)
                    fw.copy("act", c_ffn[:, f, :], R[:, TT:TT + 2])
                    fw.stt("dve", A[:, 0:TT], R[:, 1:TT + 1], wf(1), A[:, 0:TT], ALU.mult, ALU.add)
                    fw.stt("dve", A[:, 0:TT], R[:, 0:TT], wf(0), A[:, 0:TT], ALU.mult, ALU.add)
                    Gl = tmpB.get()
                    fw.act(Gl[:, 0:TT], A[:, 0:TT], AF.Gelu)
                    accu = proj(ws_main, ("up", l, ti, 2 * f + 1), wide=True)
                    evac2(lambda hf, ps, sl: fw.tt("dve", slots[ff][:, sl], ps, Gl[:, sl], ALU.mult), accu)
                for j in range(16):
                    acc = proj(ws_dn, ("dn", l, ti, hh, j), nk=FH, rhs_tiles=slots, wide=True)
                    xb = xbuf.get()
                    fw.dma("sp", xb[:, :], x_src(l, 1, ti, j), xb.tok)
                    evac2(lambda hf, ps, sl: fw.tt("dve", xb[:, sl], ps, xb[:, sl], ALU.add), acc)
                    fw.dma("sp", y_dst(ti, j), xb[:, :], st_rr.get())

        def final_norm(ti):
            acc = acc_rr.get()
            for kc in range(KC):
                xb = xbuf.get()
                fw.dma("sp", xb[:, :], x_src(L, 0, ti, kc), xb.tok)
                sq = sqb.get()
                fw.act(sq[:, :], xb[:, :], AF.Square)
                for hf in range(2):
                    fw.mm(acc[hf][:, :], onesb, sq[:, hf * 512:(hf + 1) * 512], start=(kc == 0), stop=(kc == KC - 1))
            for hf in range(2):
                fw.act(rstd[:, hf * 512:(hf + 1) * 512], acc[hf][:, :], AF.Ln, scale=1.0 / D, bias=eps_c)
                fw.act(rstd[:, hf * 512:(hf + 1) * 512], rstd[:, hf * 512:(hf + 1) * 512], AF.Exp, scale=-0.5)
            for kc in range(KC):
                xb = xbuf.get()
                fw.dma("sp", xb[:, :], x_src(L, 0, ti, kc), xb.tok)
                fw.stt("dve", xb[:, :], xb[:, :], cf[:, C_NFIN + kc:C_NFIN + kc + 1], rstd[:, :],
                       ALU.mult, ALU.mult)
                fw.dma("sp", y_dst(ti, kc), xb[:, :], st_rr.get())

        def dump(name, v, width):
            dd = dr("dbg_" + name, [128, width], kind="ExternalOutput")
            tk = fw.dtok("dbg_" + name)
            fw.dma("pool", V(dd.ap()[:, :], [Tok()]), v, tk)
            dbg_list.append(tk)

        build_program.dump = dump
        build_program.env = locals()
        try:
          for l in range(L):
            load_small(l)
            fw.memset("dve", Sst[:, :, :], 0.0)
            for h in range(NH):
                Stok[h].w = Sst.tok.w
            fw.memset("dve", c_qkv[:, :, :], 0.0)
            fw.memset("dve", c_pool[:, :, :], 0.0)
            fw.memset("dve", c_ffn[:, :, :], 0.0)
            for ti in range(NT):
                mixer(l, ti)
                checkpoint("mixer")
                ffn(l, ti)
                checkpoint("ffn")
          for ti in range(NT):
            final_norm(ti)
        except StopBuild:
            if build_program.on_stop:
                build_program.on_stop(locals())
        for t in fw.dtoks:
            if t.dcnt:
                fw.eng["sp"].wait_ge(t.dsem, t.dcnt)
        for e in ("pe", "act", "dve", "pool"):
            if fw.cnt[e]:
                fw.eng["sp"].wait_ge(fw.sem[e], fw.cnt[e])
        print("program: ins=%d waits=%d" % (fw.nins, fw.nwait), {e: fw.cnt[e] for e in fw.cnt})
    return nc


build_program.on_stop = None
_CACHE = {}


def run(inputs, L, S, ncores, stop_at=None, ret_all=False):
    B = inputs["x"].shape[0]
    pw = prep_weights(inputs, L)
    key = (L, S, stop_at)
    if key not in _CACHE:
        _CACHE[key] = build_program(L, S, stop_at=stop_at)
    nc = _CACHE[key]
    in_maps = []
    for c in range(ncores):
        b = c % B
        m = {"xT": np.ascontiguousarray(inputs["x"][b, :S].T)}
        m.update(pw)
        in_maps.append(m)
    res = run_bass_kernel_spmd(nc, in_maps, core_ids=list(range(ncores)))
    if ret_all:
        return res.results
    out = np.stack([np.ascontiguousarray(res.results[b]["yT"].T) for b in range(B)], axis=0)
    return out.astype(np.float32)


def kernel(**inputs):
    inputs = {k: np.asarray(v, dtype=np.float32) for k, v in inputs.items()}
    return run(inputs, 4, 4096, 8)
```

```python
import numpy as np
from contextlib import ExitStack
import concourse.bass as bass
import concourse.mybir as mybir
from concourse.bass_utils import run_bass_kernel_spmd

F32 = mybir.dt.float32
BF16 = mybir.dt.bfloat16
AF = mybir.ActivationFunctionType
ALU = mybir.AluOpType

D = 2048
NH = 16
HD = 128
DFF = 5632
NFF = DFF // 128
DIN = 4 * D + 2 * NH + D + 2 * D
EPS = 1e-6
TT = 1024
NCH = TT // 128
KC = D // 128
NQ = 4
FH = NFF // NQ


class Tok:
    __slots__ = ("w", "r", "dsem", "dcnt", "name")

    def __init__(self, name=""):
        self.w = None
        self.r = {}
        self.dsem = None
        self.dcnt = 0
        self.name = name


class V:
    __slots__ = ("ap", "toks")

    def __init__(self, ap, toks):
        self.ap = ap
        self.toks = toks


class Tile:
    def __init__(self, t, tok):
        self.t = t
        self.tok = tok

    def __getitem__(self, idx):
        return V(self.t[idx], [self.tok])

    def re(self, pat, **kw):
        return V(self.t[:].rearrange(pat, **kw), [self.tok])


class FW:
    def __init__(self, nc, es):
        self.nc = nc
        self.es = es
        self.eng = {"pe": nc.tensor, "act": nc.scalar, "dve": nc.vector, "pool": nc.gpsimd, "sp": nc.sync}
        self.sem = {e: es.enter_context(nc.semaphore("s_" + e)) for e in self.eng}
        self.cnt = {e: 0 for e in self.eng}
        self.seen = {e: {} for e in self.eng}
        self.nwait = 0
        self.nins = 0
        self.ntile = 0
        self.dtoks = []

    def sb(self, name, shape, dt, dma=False):
        t = self.es.enter_context(self.nc.sbuf_tensor("sb_" + name, list(shape), dt))
        tok = self.dtok(name) if dma else Tok(name)
        return Tile(t, tok)

    def dtok(self, name):
        t = Tok(name)
        t.dsem = self.es.enter_context(self.nc.semaphore("d_" + name))
        self.dtoks.append(t)
        return t

    def _deps(self, e, reads, writes):
        deps = []
        for t in reads:
            if t.w is not None:
                deps.append(t.w)
        for t in writes:
            if t.w is not None:
                deps.append(t.w)
            deps.extend(t.r.values())
        mysem = self.sem[e]
        seen = self.seen[e]
        for (sem, val) in deps:
            if e == "pe" and sem is mysem:
                continue
            if seen.get(sem, 0) >= val:
                continue
            self.eng[e].wait_ge(sem, val)
            seen[sem] = val
            self.nwait += 1

    def op(self, e, build, reads=(), writes=()):
        self._deps(e, reads, writes)
        ins = build(self.eng[e])
        self.cnt[e] += 1
        self.nins += 1
        ins.then_inc(self.sem[e], 1)
        me = (self.sem[e], self.cnt[e])
        s = self.sem[e]
        for t in reads:
            t.r[s] = me
        for t in writes:
            t.w = me
            t.r = {}
        return ins

    def dma(self, q, out, in_, tok):
        reads, writes = in_.toks, out.toks
        self._deps(q, reads, writes)
        if tok.dcnt > 0 and self.seen[q].get(tok.dsem, 0) < tok.dcnt:
            self.eng[q].wait_ge(tok.dsem, tok.dcnt)
            self.seen[q][tok.dsem] = tok.dcnt
            self.nwait += 1
        ins = self.eng[q].dma_start(out=out.ap, in_=in_.ap)
        tok.dcnt += 16
        ins.then_inc(tok.dsem, 16)
        self.nins += 1
        me = (tok.dsem, tok.dcnt)
        for t in reads:
            t.r[tok.dsem] = me
        for t in writes:
            t.w = me
            t.r = {}
        return ins

    def wait_tok(self, e, tok):
        self._deps(e, [tok], [])

    def mm(self, out, lhsT, rhs, start=True, stop=True):
        return self.op("pe", lambda e: e.matmul(out.ap, lhsT=lhsT.ap, rhs=rhs.ap, start=start, stop=stop),
                       reads=lhsT.toks + rhs.toks, writes=out.toks)

    def tr(self, out, in_, ident):
        return self.op("pe", lambda e: e.transpose(out.ap, in_.ap, ident.ap),
                       reads=in_.toks + ident.toks, writes=out.toks)

    def act(self, out, in_, func, bias=None, scale=None, accum=None, eng="act"):
        kw = {}
        reads = list(in_.toks)
        writes = list(out.toks)
        if bias is not None:
            if isinstance(bias, V):
                kw["bias"] = bias.ap
                reads += bias.toks
            else:
                kw["bias"] = bias
        if scale is not None:
            if isinstance(scale, V):
                kw["scale"] = scale.ap
                reads += scale.toks
            else:
                kw["scale"] = scale
        if accum is not None:
            kw["accum_out"] = accum.ap
            writes += accum.toks
        return self.op(eng, lambda e: e.activation(out=out.ap, in_=in_.ap, func=func, **kw), reads=reads, writes=writes)

    def tt(self, eng, out, in0, in1, op):
        return self.op(eng, lambda e: e.tensor_tensor(out=out.ap, in0=in0.ap, in1=in1.ap, op=op),
                       reads=in0.toks + in1.toks, writes=out.toks)

    def ts(self, eng, out, in0, s1, op0, s2=None, op1=None, accum=None):
        reads = list(in0.toks)
        writes = list(out.toks)
        a1 = s1
        if isinstance(s1, V):
            a1 = s1.ap
            reads += s1.toks
        a2 = s2
        if isinstance(s2, V):
            a2 = s2.ap
            reads += s2.toks
        kw = {}
        if op1 is not None:
            kw["op1"] = op1
        if accum is not None:
            kw["accum_out"] = accum.ap
            writes += accum.toks
        return self.op(eng, lambda e: e.tensor_scalar(out=out.ap, in0=in0.ap, scalar1=a1, scalar2=a2, op0=op0, **kw),
                       reads=reads, writes=writes)

    def stt(self, eng, out, in0, scalar, in1, op0, op1):
        reads = in0.toks + in1.toks
        a = scalar
        if isinstance(scalar, V):
            a = scalar.ap
            reads = reads + scalar.toks
        return self.op(eng, lambda e: e.scalar_tensor_tensor(out=out.ap, in0=in0.ap, scalar=a, in1=in1.ap, op0=op0, op1=op1),
                       reads=reads, writes=out.toks)

    def copy(self, eng, out, in_):
        if eng == "act":
            return self.act(out, in_, AF.Copy)
        return self.op(eng, lambda e: e.tensor_copy(out=out.ap, in_=in_.ap), reads=in_.toks, writes=out.toks)

    def memset(self, eng, out, val):
        return self.op(eng, lambda e: e.memset(out.ap, val), reads=[], writes=out.toks)


class RR:
    def __init__(self, items):
        self.items = items
        self.i = 0

    def get(self):
        it = self.items[self.i % len(self.items)]
        self.i += 1
        return it


O_NMIX = 0
O_NFFN = 16
O_PSC = 32
O_CQKV = 48
O_CFW = O_CQKV + 192
O_CFB = O_CFW + 132
O_GNW = O_CFB + 44
O_ALOG = O_GNW + 1
O_DTB = O_ALOG + 128
NSM = O_DTB + 128
C_ID = 0
C_TRIU = 128
C_SU = 256
C_SL = 384
C_ONE = 512
C_ICNT = 640
C_NFIN = 704
NCONST = 720


def _colmajor(w, ncol_tiles):
    K = w.shape[0]
    return np.ascontiguousarray(w.reshape(K // 128, 128, ncol_tiles, 128).transpose(2, 1, 0, 3)).reshape(
        ncol_tiles, 128, (K // 128) * 128)


def prep_weights(inp, L):
    out = {}
    w_in = inp["w_in"]
    order = []
    for g in range(4):
        for j in range(4 * g, 4 * g + 4):
            order.append(4 * D + 2 * NH + D + D + j * 128)
        for j in range(4 * g, 4 * g + 4):
            order.append(4 * D + 2 * NH + j * 128)
        for h in range(4 * g, 4 * g + 4):
            order.append(0 * D + h * 128)
            order.append(1 * D + h * 128)
            order.append(2 * D + h * 128)
            order.append(3 * D + h * 128)
            order.append(4 * D + 2 * NH + D + h * 128)
    cols = np.concatenate([np.arange(o, o + 128) for o in order])
    win = np.empty((L, len(order), 128, D), np.float32)
    wab = np.empty((L, 128, KC * 32), np.float32)
    wout = np.empty((L, 16, 128, D), np.float32)
    wup = np.empty((L, 2 * NFF, 128, D), np.float32)
    wdn = np.empty((L, NQ, 16, 128, FH * 128), np.float32)
    wpool = np.empty((L, 4, 4, 128, 512), np.float32)
    small = np.zeros((L, 128, NSM), np.float32)
    upcols = np.concatenate([np.concatenate([np.arange(f * 128, f * 128 + 128), np.arange(DFF + f * 128, DFF + f * 128 + 128)])
                             for f in range(NFF)])
    for l in range(L):
        win[l] = _colmajor(w_in[l][:, cols], len(order))
        ab = w_in[l][:, 4 * D:4 * D + 32]
        wab[l] = ab.reshape(KC, 128, 32).transpose(1, 0, 2).reshape(128, KC * 32)
        wout[l] = _colmajor(inp["w_out"][l], 16)
        wup[l] = _colmajor(inp["w_up"][l][:, upcols], 2 * NFF)
        wd = inp["w_down"][l]
        for hh in range(NQ):
            blk = wd[hh * FH * 128:(hh + 1) * FH * 128]
            wdn[l, hh] = _colmajor(blk, 16)
        for g in range(4):
            wpool[l, g] = _colmajor(inp["pool_w"][l, g], 4)
        sm = small[l]
        sm[:, O_NMIX:O_NMIX + 16] = inp["norm_mix_w"][l].reshape(16, 128).T
        sm[:, O_NFFN:O_NFFN + 16] = inp["norm_ffn_w"][l].reshape(16, 128).T
        sm[:, O_PSC:O_PSC + 16] = inp["pool_scale"][l].reshape(16, 128).T
        sm[:, O_CQKV:O_CQKV + 192] = inp["conv_qkv_w"][l].reshape(4, 48, 128).transpose(2, 1, 0).reshape(128, 192)
        sm[:, O_CFW:O_CFW + 132] = inp["conv_ffn_w"][l].reshape(3, NFF, 128).transpose(2, 1, 0).reshape(128, 132)
        sm[:, O_CFB:O_CFB + 44] = inp["conv_ffn_b"][l].reshape(NFF, 128).T
        sm[:, O_GNW] = inp["gdn_norm_w"][l]
        sm[:, O_ALOG:O_ALOG + 128] = np.tile(inp["a_log"][l], NCH)[None, :]
        sm[:, O_DTB:O_DTB + 128] = np.tile(inp["dt_bias"][l], NCH)[None, :]
    out.update(win=win, wab=wab, wout=wout, wup=wup, wdn=wdn, wpool=wpool, small=small)
    c = np.zeros((128, NCONST), np.float32)
    idx = np.arange(128)
    c[:, C_ID:C_ID + 128] = np.eye(128, dtype=np.float32)
    c[:, C_TRIU:C_TRIU + 128] = (idx[:, None] <= idx[None, :])
    c[:, C_SU:C_SU + 128] = (idx[None, :] > idx[:, None])
    c[:, C_SL:C_SL + 128] = (idx[:, None] > idx[None, :])
    c[:, C_ONE:C_ONE + 128] = 1.0
    for g, win_ in enumerate((2, 4, 8, 16)):
        t = np.arange(16)
        c[:, C_ICNT + g * 16:C_ICNT + g * 16 + 16] = (np.float32(1.0) / np.minimum(t + 1, win_).astype(np.float32))[None, :]
    c[:, C_NFIN:C_NFIN + 16] = inp["norm_final_w"].reshape(16, 128).T
    out["consts"] = c
    return out


class StopBuild(Exception):
    pass


def build_program(L, S, dbg=None, stop_at=None):
    assert S % TT == 0
    NT = S // TT
    nc = bass.Bass("TRN2", target_bir_lowering=False)
    dr = lambda name, shape, kind="ExternalInput": nc.dram_tensor(name, list(shape), F32, kind=kind)
    xT_d = dr("xT", [D, S])
    win_d = dr("win", [L, 112, 128, D])
    wab_d = dr("wab", [L, 128, KC * 32])
    wout_d = dr("wout", [L, 16, 128, D])
    wup_d = dr("wup", [L, 2 * NFF, 128, D])
    wdn_d = dr("wdn", [L, NQ, 16, 128, FH * 128])
    wpool_d = dr("wpool", [L, 4, 4, 128, 512])
    small_d = dr("small", [L, 128, NSM])
    consts_d = dr("consts", [128, NCONST])
    yT_d = dr("yT", [D, S], kind="ExternalOutput")
    dbg_list = []

    def checkpoint(name):
        if stop_at == name:
            raise StopBuild()

    with ExitStack() as es:
        fw = FW(nc, es)
        t_win, t_wab, t_wout, t_wup, t_wdn, t_wpool, t_small, t_consts = (Tok() for _ in range(8))
        t_xin = Tok()
        t_y = [[Tok() for _ in range(KC)] for _ in range(NT)]
        st_tok = [fw.dtok("st%d" % i) for i in range(3)]
        st_rr = RR(st_tok)

        cf = fw.sb("cf", [128, NCONST], F32, dma=True)
        cb = fw.sb("cb", [128, 640], BF16, dma=True)
        fw.dma("sp", cf[:, :], V(consts_d.ap()[:, :], [t_consts]), cf.tok)
        fw.dma("pool", cb[:, :], V(consts_d.ap()[:, 0:640], [t_consts]), cb.tok)
        identb = cb[:, C_ID:C_ID + 128]
        triub = cb[:, C_TRIU:C_TRIU + 128]
        sub_ = cb[:, C_SU:C_SU + 128]
        onesb = cb[:, C_ONE:C_ONE + 128]
        triuf = cf[:, C_TRIU:C_TRIU + 128]
        slf = cf[:, C_SL:C_SL + 128]
        onesf = cf[:, C_ONE:C_ONE + 128]
        identf = cf[:, C_ID:C_ID + 128]

        hT = [fw.sb("hT%d" % k, [128, TT], BF16) for k in range(KC)]
        slots = [fw.sb("slot%d" % k, [128, TT], BF16) for k in range(16)]
        wbuf = RR([fw.sb("wb%d" % i, [128, D], BF16, dma=True) for i in range(3)])
        wdbuf = RR([fw.sb("wd%d" % i, [128, FH * 128], BF16, dma=True) for i in range(2)])
        wpbuf = RR([fw.sb("wp%d" % i, [128, 512], BF16, dma=True) for i in range(2)])
        wabb = fw.sb("wabb", [128, KC * 32], BF16, dma=True)
        small = fw.sb("small", [128, NSM], F32, dma=True)
        xbuf = RR([fw.sb("xb%d" % i, [128, TT], F32, dma=True) for i in range(2)])
        sqb = RR([fw.sb("sq%d" % i, [128, TT], BF16) for i in range(2)])
        rstd = fw.sb("rstd", [128, TT], F32)
        tmpA = RR([fw.sb("tA%d" % i, [128, TT + 16], F32) for i in range(2)])
        tmpB = RR([fw.sb("tB%d" % i, [128, TT + 16], F32) for i in range(2)])
        qkvT = [fw.sb("qkvT%d" % i, [128, TT], BF16) for i in range(3)]
        gateTs = [fw.sb("gateT%d" % i, [128, TT], BF16) for i in range(2)]
        yaT = fw.sb("yaT", [128, TT], F32)
        pooledT = qkvT + [gateTs[0]]
        tokm = [fw.sb("tokm%d" % i, [128, NCH, 128], BF16) for i in range(3)]
        Sst = fw.sb("Sst", [128, NH, 128], F32)
        Stok = [Tok() for _ in range(NH)]
        c_qkv = fw.sb("c_qkv", [128, 48, 3], F32)
        c_pool = fw.sb("c_pool", [128, 16, 15], F32)
        c_ffn = fw.sb("c_ffn", [128, NFF, 2], F32)
        ab_sb = fw.sb("ab_sb", [128, NCH, 32], F32)
        beta = fw.sb("beta", [128, NCH, 16], F32)
        glog = fw.sb("glog", [128, NCH, 16], F32)
        gcs = fw.sb("gcs", [128, NCH, 16], F32)
        egt = fw.sb("egt", [128, NCH, 16], F32)
        ket = fw.sb("ket", [128, NCH, 16], F32)
        dect = fw.sb("dect", [128, NCH, 16], F32)
        negA = fw.sb("negA", [128, NCH, 16], F32)
        ts0 = fw.sb("ts0", [128, NCH, 16], F32)
        ts1 = fw.sb("ts1", [128, NCH, 16], F32)
        ssk = fw.sb("ssk", [128, NCH], F32)
        ssq = fw.sb("ssq", [128, NCH], F32)
        sc = {n: fw.sb("sc_" + n, [128, NCH], F32) for n in ("rk", "rq", "kg", "ke", "qg")}
        sso = RR([fw.sb("sso%d" % i, [128, 2], F32) for i in range(2)])
        junk = RR([fw.sb("junk%d" % i, [128, 128], F32) for i in range(2)])
        junk2 = RR([fw.sb("junkb%d" % i, [128, 16], F32) for i in range(2)])
        KI = 4
        lsets = []
        for i in range(KI):
            lsets.append(dict(
                dgy=fw.sb("dgy%d" % i, [128, 4, 128], F32),
                kq=fw.sb("kq%d" % i, [128, 2, 128], F32),
                dgb=fw.sb("dgb%d" % i, [128, 4, 128], BF16),
                sld=fw.sb("sld%d" % i, [128, 128], F32),
                dTs=fw.sb("dTs%d" % i, [128, 128], F32),
            ))
        psets = []
        for i in range(2 * KI):
            psets.append(dict(
                kgq=fw.sb("kgq%d" % i, [128, 2, 128], F32),
                ketok=fw.sb("ketok%d" % i, [128, 128], F32),
                attnT=fw.sb("attnT%d" % i, [128, 128], F32),
                Tfin=fw.sb("Tfin%d" % i, [128, 128], F32),
            ))
        osb = RR([fw.sb("osb%d" % i, [128, 128], F32) for i in range(2)])
        rp = RR([fw.sb("rp%d" % i, [128, 128], F32) for i in range(2)])
        vnew = RR([fw.sb("vnew%d" % i, [128, 128], F32) for i in range(2)])
        onb = RR([fw.sb("onb%d" % i, [128, 128], BF16) for i in range(2)])
        eps_t = fw.sb("eps_t", [128, 1], F32)
        fw.memset("dve", eps_t[:, :], EPS)
        eps_c = eps_t[:, 0:1]

        pbank = [es.enter_context(nc.psum_tensor("pb%d" % i, [128, 512], F32)) for i in range(7)]
        pbf = es.enter_context(nc.psum_tensor("pbf", [128, 1024], BF16))
        _acc_banks = RR([Tile(pbank[0], Tok()), Tile(pbank[1], Tok())])

        class _AccRR:
            def get(self):
                return [_acc_banks.get(), _acc_banks.get()]

        acc_rr = _AccRR()

        class PS:
            def __init__(self, t, lo, n, tok):
                self.t, self.lo, self.n, self.tok = t, lo, n, tok

            def v(self, a=0, b=None):
                b = self.n if b is None else b
                return V(self.t[:, self.lo + a:self.lo + b], [self.tok])

        g_rr = RR([PS(pbank[2], 0, 512, Tok()), PS(pbank[3], 0, 512, Tok()), PS(pbank[4], 0, 512, Tok()),
                   PS(pbank[5], 0, 512, Tok())])
        q_bank = PS(pbank[6], 0, 512, Tok())
        _wide = [Tile(pbank[0], _acc_banks.items[0].tok), Tile(pbank[1], _acc_banks.items[1].tok)] + \
                [Tile(pbank[2 + i], g_rr.items[i].tok) for i in range(4)]
        _wide_rr = RR(_wide)

        class _AccWide:
            def get(self):
                return [_wide_rr.get(), _wide_rr.get()]

        acc_wide = _AccWide()
        pair_rr = g_rr
        g_banks = g_rr.items
        sing_rr = g_rr
        pbf_ps = PS(pbf, 0, 1024, Tok())

        def sm(off, n=1):
            return small[:, off:off + n]

        def load_small(l):
            fw.dma("sp", small[:, :], V(small_d.ap()[l], [t_small]), small.tok)
            fw.dma("pool", wabb[:, :], V(wab_d.ap()[l], [t_wab]), wabb.tok)
            fw.act(ts0.re("p a b -> p (a b)"), sm(O_ALOG, 128), AF.Exp)
            fw.ts("dve", negA.re("p a b -> p (a b)"), ts0.re("p a b -> p (a b)"), -1.0, ALU.mult)

        def x_src(l, phase, ti, kc):
            cols = slice(ti * TT, (ti + 1) * TT)
            if l == 0 and phase == 0:
                return V(xT_d.ap()[kc * 128:(kc + 1) * 128, cols], [t_xin])
            return V(yT_d.ap()[kc * 128:(kc + 1) * 128, cols], [t_y[ti][kc]])

        def y_dst(ti, kc):
            cols = slice(ti * TT, (ti + 1) * TT)
            return V(yT_d.ap()[kc * 128:(kc + 1) * 128, cols], [t_y[ti][kc]])

        def rmsnorm_to_hT(l, phase, ti, woff):
            acc = acc_rr.get()
            for kc in range(KC):
                xb = xbuf.get()
                fw.dma("sp", xb[:, :], x_src(l, phase, ti, kc), xb.tok)
                sq = sqb.get()
                fw.act(sq[:, :], xb[:, :], AF.Square)
                for hf in range(2):
                    fw.mm(acc[hf][:, :], onesb, sq[:, hf * 512:(hf + 1) * 512], start=(kc == 0), stop=(kc == KC - 1))
            for hf in range(2):
                fw.act(rstd[:, hf * 512:(hf + 1) * 512], acc[hf][:, :], AF.Ln, scale=1.0 / D, bias=eps_c)
                fw.act(rstd[:, hf * 512:(hf + 1) * 512], rstd[:, hf * 512:(hf + 1) * 512], AF.Exp, scale=-0.5)
            for kc in range(KC):
                xb = xbuf.get()
                fw.dma("sp", xb[:, :], x_src(l, phase, ti, kc), xb.tok)
                fw.stt("dve", hT[kc][:, :], xb[:, :], sm(woff + kc), rstd[:, :], ALU.mult, ALU.mult)

        class WStream:
            def __init__(self, bufs, src_iter):
                self.bufs, self.it, self.issued, self.taken, self.n = bufs, src_iter, 0, 0, len(bufs)
                self.keys = []

            def _issue(self):
                try:
                    key, ap, tok = next(self.it)
                except StopIteration:
                    return False
                wb = self.bufs[self.issued % self.n]
                fw.dma("pool", wb[:, :], V(ap, [tok]), wb.tok)
                self.keys.append(key)
                self.issued += 1
                return True

            def next(self, key):
                m = self.taken
                while self.issued < m + self.n:
                    if not self._issue():
                        break
                assert self.keys[m] == key, (self.keys[m], key)
                self.taken += 1
                return self.bufs[m % self.n]

        def _src_main():
            for l_ in range(L):
                for ti_ in range(NT):
                    for ct_ in range(112):
                        yield ("in", l_, ti_, ct_), win_d.ap()[l_, ct_], t_win
                    for j_ in range(16):
                        yield ("out", l_, ti_, j_), wout_d.ap()[l_, j_], t_wout
                    for f_ in range(2 * NFF):
                        yield ("up", l_, ti_, f_), wup_d.ap()[l_, f_], t_wup

        def _src_dn():
            for l_ in range(L):
                for ti_ in range(NT):
                    for hh_ in range(NQ):
                        for j_ in range(16):
                            yield ("dn", l_, ti_, hh_, j_), wdn_d.ap()[l_, hh_, j_], t_wdn

        def _src_pool():
            for l_ in range(L):
                for ti_ in range(NT):
                    for g_ in range(4):
                        for ot_ in range(4):
                            yield ("pw", l_, ti_, g_, ot_), wpool_d.ap()[l_, g_, ot_], t_wpool

        ws_main = WStream(wbuf.items, _src_main())
        ws_dn = WStream(wdbuf.items, _src_dn())
        ws_pool = WStream(wpbuf.items, _src_pool())

        def proj_g(stream, key, nk=KC, rhs_tiles=None, wide=False):
            rhs_tiles = rhs_tiles or hT
            wb = stream.next(key)
            acc = (acc_wide if wide else acc_rr).get()
            n = 0
            for hf in range(2):
                for kc in range(nk):
                    fw.mm(acc[hf][:, :], wb[:, kc * 128:(kc + 1) * 128], rhs_tiles[kc][:, hf * 512:(hf + 1) * 512],
                          start=(kc == 0), stop=(kc == nk - 1))
                    n += 1
                    if n % 4 == 0 and not (kc == nk - 1):
                        yield
                yield
            return acc

        def drain(g_):
            try:
                while True:
                    next(g_)
            except StopIteration as e_:
                return e_.value

        def proj(stream, key, nk=KC, rhs_tiles=None, wide=False):
            return drain(proj_g(stream, key, nk=nk, rhs_tiles=rhs_tiles, wide=wide))

        def evac2(eng_fn, acc):
            for hf in range(2):
                eng_fn(hf, acc[hf][:, :], slice(hf * 512, (hf + 1) * 512))

        def token_scalars():
            flat = lambda t: t.re("p a b -> p (a b)")
            for c in range(NCH):
                ps = pair_rr.get()
                for kc in range(KC):
                    fw.mm(ps.v(0, 32), hT[kc][:, c * 128:(c + 1) * 128], wabb[:, kc * 32:(kc + 1) * 32],
                          start=(kc == 0), stop=(kc == KC - 1))
                fw.copy("act", ab_sb[:, c, :], ps.v(0, 32))
            checkpoint("tsc1")
            fw.act(beta[:, :, :], ab_sb[:, :, 0:16], AF.Sigmoid)
            fw.tt("dve", ts0[:, :, :], ab_sb[:, :, 16:32],
                  V(small.t[:, O_DTB:O_DTB + 128].rearrange("p (a b) -> p a b", b=16), [small.tok]), ALU.add)
            fw.act(ts1[:, :, :], ts0[:, :, :], AF.Exp)
            fw.act(ts0[:, :, :], ts1[:, :, :], AF.Ln, bias=1.0)
            fw.tt("dve", glog[:, :, :], ts0[:, :, :], negA[:, :, :], ALU.mult)
            checkpoint("tsc2")
            p1 = pair_rr.get()
            fw.mm(p1.v(0, 128), triuf, flat(glog))
            p2 = pair_rr.get()
            fw.mm(p2.v(0, 128), onesf, flat(glog))
            checkpoint("tsc3")
            fw.copy("act", flat(gcs), p1.v(0, 128))
            fw.act(flat(egt), p1.v(0, 128), AF.Exp)
            fw.tt("dve", flat(ts0), p2.v(0, 128), flat(gcs), ALU.subtract)
            fw.act(flat(ket), flat(ts0), AF.Exp)
            fw.act(flat(dect), p2.v(0, 128), AF.Exp)

        def interleave(gens, bg=None):
            gens = list(gens)
            while gens:
                for g_ in list(gens):
                    try:
                        next(g_)
                    except StopIteration:
                        gens.remove(g_)
                if bg is not None and not bg[1]:
                    try:
                        next(bg[0])
                    except StopIteration:
                        bg[1] = True

        def chain(*gens):
            for g_ in gens:
                yield from g_

        def gdn_local(h, c, Ls, Ps, pb):
            ktok, qtok, vtok = tokm
            col = lambda t: V(t.t[:, c:c + 1], [t.tok])
            colh = lambda t: V(t.t[:, c, h:h + 1], [t.tok])
            d = Ls["dgy"]
            kq = Ls["kq"]
            kgq = Ps["kgq"]
            db = Ls["dgb"]
            fw.ts("dve", db[:, 0, :], identf, col(sc["rk"]), ALU.mult); yield
            fw.ts("dve", db[:, 1, :], identf, col(sc["kg"]), ALU.mult); yield
            fw.act(db[:, 2, :], identf, AF.Copy, scale=col(sc["rq"])); yield
            fw.act(db[:, 3, :], identf, AF.Copy, scale=col(sc["qg"])); yield
            p = PS(pb.t, 0, 256, pb.tok)
            fw.mm(p.v(0, 256), ktok[:, c, :], V(db.t[:, 0:2, :].rearrange("p a b -> p (a b)"), [db.tok])); yield
            fw.copy("act", kq[:, 0, :], p.v(0, 128)); yield
            fw.copy("act", kgq[:, 0, :], p.v(128, 256)); yield
            fw.mm(p.v(0, 256), qtok[:, c, :], V(db.t[:, 2:4, :].rearrange("p a b -> p (a b)"), [db.tok])); yield
            fw.copy("act", kq[:, 1, :], p.v(0, 128)); yield
            fw.copy("act", kgq[:, 1, :], p.v(128, 256)); yield
            if c == 0: checkpoint("La")
            ke_t = Ps["ketok"]
            fw.act(ke_t[:, :], ktok[:, c, :], AF.Copy, scale=col(sc["ke"])); yield
            sl = Ls["sld"]
            fw.ts("dve", sl[:, :], slf, colh(glog), ALU.mult); yield
            pd = PS(pb.t, 0, 128, pb.tok)
            fw.mm(pd.v(0, 128), sl[:, :], triuf); yield
            fw.act(sl[:, :], pd.v(0, 128), AF.Exp); yield
            ds_ = Ls["dTs"]
            fw.tt("dve", ds_[:, :], sl[:, :], V(cf.t[:, C_SU:C_SU + 128], [cf.tok]), ALU.mult); yield
            fw.tt("dve", sl[:, :], sl[:, :], V(cf.t[:, C_TRIU:C_TRIU + 128], [cf.tok]), ALU.mult); yield
            if c == 0: checkpoint("Lb")
            pg = PS(pb.t, 0, 256, pb.tok)
            fw.mm(pg.v(0, 256), kq[:, 0, :], kq.re("p a b -> p (a b)")); yield
            at = Ps["attnT"]
            fw.tt("dve", at[:, :], pg.v(128, 256), sl[:, :], ALU.mult); yield
            ymA = V(d.t[:, 0:2, :], [d.tok])
            ymB = V(d.t[:, 2:4, :], [d.tok])
            half = lambda ym, i: V(ym.ap[:, i, :], ym.toks)
            flat2 = lambda ym: V(ym.ap.rearrange("p a b -> p (a b)"), ym.toks)
            fw.stt("dve", half(ymA, 0), pg.v(0, 128), colh(beta), ds_[:, :], ALU.mult, ALU.mult); yield
            fw.tt("dve", half(ymB, 1), identf, half(ymA, 0), ALU.subtract); yield
            pn = PS(pb.t, 0, 128, pb.tok)
            fw.mm(pn.v(0, 128), half(ymA, 0), identf); yield
            ytA, ytB = kq[:, 0, :], kq[:, 1, :]
            fw.copy("act", ytA, pn.v(0, 128)); yield
            if c == 0: checkpoint("Lc")
            ym, ym2, yt, yt2 = ymA, ymB, ytA, ytB
            for k in range(7):
                last = (k == 6)
                first = (k == 0)
                pm = PS(pb.t, 0, 256, pb.tok)
                if first:
                    fw.mm(pm.v(0, 128), yt, half(ym, 0)); yield
                elif last:
                    fw.mm(pm.v(128, 256), yt, half(ym, 1)); yield
                else:
                    fw.mm(pm.v(0, 256), yt, flat2(ym)); yield
                if not last:
                    fw.copy("act", half(ym2, 0), pm.v(0, 128)); yield
                    if not first:
                        fw.tt("dve", half(ym2, 1), pm.v(128, 256), half(ym, 1), ALU.add); yield
                    p2_ = PS(pb.t, 0, 128, pb.tok)
                    fw.mm(p2_.v(0, 128), half(ym, 0), yt); yield
                    fw.copy("act", yt2, p2_.v(0, 128)); yield
                    ym, ym2, yt, yt2 = ym2, ym, yt2, yt
                else:
                    tf = Ps["Tfin"]
                    fw.tt("dve", tf[:, :], pm.v(128, 256), half(ym, 1), ALU.add); yield

        def gdn_state(h, c, Ps, pq):
            ktok, qtok, vtok = tokm
            colh = lambda t: V(t.t[:, c, h:h + 1], [t.tok])
            Sh = V(Sst.t[:, h, :], [Stok[h]])
            kgq, ke_t, at, tf = Ps["kgq"], Ps["ketok"], Ps["attnT"], Ps["Tfin"]
            if c == 0: checkpoint("Qa")
            pr = PS(pq.t, 0, 128, pq.tok)
            fw.mm(pr.v(0, 128), kgq[:, 0, :], Sh); yield
            r_ = rp.get()
            fw.tt("dve", r_[:, :], vtok[:, c, :], pr.v(0, 128), ALU.subtract); yield
            fw.mm(pr.v(0, 128), tf[:, :], r_[:, :]); yield
            vn = vnew.get()
            fw.act(vn[:, :], pr.v(0, 128), AF.Copy, scale=colh(beta)); yield
            fw.mm(pr.v(0, 128), kgq[:, 1, :], Sh, start=True, stop=False)
            fw.mm(pr.v(0, 128), at[:, :], vn[:, :], start=False, stop=True); yield
            ob32 = osb.get()
            fw.copy("act", ob32[:, :], pr.v(0, 128)); yield
            fw.mm(pr.v(0, 128), ke_t[:, :], vn[:, :]); yield
            fw.stt("dve", Sh, Sh, colh(dect), pr.v(0, 128), ALU.mult, ALU.add); yield
            return ob32

        def gdn_out(h, c, ob32):
            gateT = gateTs[h % 2]
            cs = slice(c * 128, (c + 1) * 128)
            so = sso.get()
            fw.act(junk.get()[:, :], ob32[:, :], AF.Square, accum=so[:, 0:1]); yield
            fw.act(so[:, 1:2], so[:, 0:1], AF.Ln, scale=1.0 / HD, bias=eps_c); yield
            fw.act(so[:, 1:2], so[:, 1:2], AF.Exp, scale=-0.5); yield
            ob = onb.get()
            fw.act(ob[:, :], ob32[:, :], AF.Copy, scale=so[:, 1:2]); yield
            fw.tr(pbf_ps.v(128, 256), ob[:, :], identb); yield
            fw.stt("dve", yaT[:, cs], pbf_ps.v(128, 256), sm(O_GNW), gateT[:, cs], ALU.mult, ALU.mult); yield

        def q_chain(h, cs_):
            go = None
            for c in cs_:
                gs = gdn_state(h, c, psets[c % (2 * KI)], q_bank)
                ob32 = None
                gs_done = False
                while not gs_done or go is not None:
                    if not gs_done:
                        try:
                            next(gs)
                        except StopIteration as e_:
                            ob32 = e_.value
                            gs_done = True
                    if go is not None:
                        try:
                            next(go)
                        except StopIteration:
                            go = None
                    yield
                go = gdn_out(h, c, ob32)
            yield from go

        def gdn_head(l, h, jslot, bg=None):
            kT, qT, vT = qkvT[1], qkvT[0], qkvT[2]
            ktok, qtok, vtok = tokm
            for which, (srcT, dst, ss) in enumerate(((kT, ktok, ssk), (qT, qtok, ssq), (vT, vtok, None))):
                for c in range(NCH):
                    fw.tr(pbf_ps.v(c * 128, (c + 1) * 128), srcT[:, c * 128:(c + 1) * 128], identb)
                if ss is not None:
                    sqt = tmpB.get()
                    fw.act(sqt[:, 0:TT], pbf_ps.v(0, TT), AF.Square)
                    sq3 = V(sqt.t[:, 0:TT].rearrange("p (c d) -> p c d", d=128), [sqt.tok])
                    fw.op("dve", lambda e: e.tensor_reduce(out=ss.t[:, :], in_=sq3.ap, axis=mybir.AxisListType.X, op=ALU.add),
                          reads=sq3.toks, writes=[ss.tok])
                fw.copy("dve" if which != 1 else "act", dst.re("p c d -> p (c d)"), pbf_ps.v(0, TT))
            checkpoint("gdnA%d" % h)
            hs = lambda t: V(t.t[:, :, h], [t.tok])
            fw.act(sc["rk"][:, :], ssk[:, :], AF.Ln, bias=eps_c)
            fw.act(sc["rk"][:, :], sc["rk"][:, :], AF.Exp, scale=-0.5)
            fw.act(sc["rq"][:, :], ssq[:, :], AF.Ln, bias=eps_c)
            fw.act(sc["rq"][:, :], sc["rq"][:, :], AF.Exp, scale=-0.5)
            fw.ts("dve", sc["rq"][:, :], sc["rq"][:, :], float(HD) ** -0.5, ALU.mult)
            fw.tt("dve", sc["kg"][:, :], sc["rk"][:, :], hs(egt), ALU.mult)
            fw.tt("dve", sc["ke"][:, :], sc["rk"][:, :], hs(ket), ALU.mult)
            fw.tt("dve", sc["qg"][:, :], sc["rq"][:, :], hs(egt), ALU.mult)
            L_ = lambda c: gdn_local(h, c, lsets[c % KI], psets[c % (2 * KI)], g_banks[c % KI])
            groups = [list(range(c0, c0 + KI)) for c0 in range(0, NCH, KI)]
            bgs = [bg, False] if bg is not None else None
            interleave([L_(c) for c in groups[0]], bgs)
            for gi in range(1, len(groups)):
                interleave([L_(c) for c in groups[gi]] + [q_chain(h, groups[gi - 1])], bgs)
            interleave([q_chain(h, groups[-1])], bgs)
            if bg is not None:
                drain(bg)
            fw.tt("pool", slots[jslot][:, :], yaT[:, :], slots[jslot][:, :], ALU.add)
            checkpoint("headdone%d" % h)

        def pool_phase_g(l, ti, g):
            win_ = 2 ** (g + 1)
            base = g * 28
            for jj in range(4):
                j = 4 * g + jj
                acc = yield from proj_g(ws_main, ("in", l, ti, base + jj))
                evac2(lambda hf, ps, sl: fw.act(slots[j][:, sl], ps, AF.Sigmoid), acc); yield
            for jj in range(4):
                j = 4 * g + jj
                acc = yield from proj_g(ws_main, ("in", l, ti, base + 4 + jj))
                U = tmpA.get()
                fw.copy("act", U[:, 1:16], c_pool[:, j, :])
                fw.memset("dve", U[:, 0:1], 0.0)
                evac2(lambda hf, ps, sl: fw.copy("act", U[:, 16 + sl.start:16 + sl.stop], ps), acc); yield
                fw.copy("act", c_pool[:, j, :], U[:, TT + 1:TT + 16])
                cur = U
                sh = 1
                e0 = 1
                while sh < win_:
                    nxt = tmpB.get()
                    fw.tt("pool" if sh > 1 else "dve", nxt[:, e0:TT + 16], cur[:, e0:TT + 16], cur[:, e0 - sh:TT + 16 - sh], ALU.add); yield
                    cur = nxt
                    sh *= 2
                    e0 = 2 * sh - 1
                fw.stt("dve", pooledT[jj][:, :], cur[:, 16:TT + 16], 1.0 / win_, U[:, 16:TT + 16], ALU.mult, ALU.subtract); yield
                if ti == 0:
                    tq = junk2.get()
                    fw.tt("dve", tq[:, 0:16], cur[:, 16:32], cf[:, C_ICNT + g * 16:C_ICNT + g * 16 + 16], ALU.mult)
                    fw.tt("dve", pooledT[jj][:, 0:16], tq[:, 0:16], U[:, 16:32], ALU.subtract); yield
            for ot in range(4):
                j = 4 * g + ot
                acc = yield from proj_g(ws_pool, ("pw", l, ti, g, ot), nk=4, rhs_tiles=pooledT)
                evac2(lambda hf, ps, sl: fw.stt("dve", slots[j][:, sl], ps, sm(O_PSC + j), slots[j][:, sl], ALU.mult, ALU.mult), acc); yield

        def head_prep_g(l, ti, h):
            g = h // 4
            base = g * 28 + 8 + (h % 4) * 5
            gateT = gateTs[h % 2]
            for which in range(3):
                cti = which * 16 + h
                acc = yield from proj_g(ws_main, ("in", l, ti, base + which))
                R = tmpA.get()
                A = tmpB.get()
                fw.copy("act", R[:, 0:3], c_qkv[:, cti, :])
                wq = lambda jtap: sm(O_CQKV + cti * 4 + jtap)
                evac2(lambda hf, ps, sl: fw.copy("act", R[:, 3 + sl.start:3 + sl.stop], ps), acc); yield
                fw.act(A[:, 0:TT], R[:, 3:TT + 3], AF.Copy, scale=wq(3)); yield
                fw.copy("act", c_qkv[:, cti, :], R[:, TT:TT + 3])
                fw.stt("dve", A[:, 0:TT], R[:, 2:TT + 2], wq(2), A[:, 0:TT], ALU.mult, ALU.add); yield
                fw.stt("dve", A[:, 0:TT], R[:, 1:TT + 1], wq(1), A[:, 0:TT], ALU.mult, ALU.add); yield
                fw.stt("dve", A[:, 0:TT], R[:, 0:TT], wq(0), A[:, 0:TT], ALU.mult, ALU.add); yield
                fw.act(qkvT[which][:, :], A[:, 0:TT], AF.Silu); yield
            accz = yield from proj_g(ws_main, ("in", l, ti, base + 3))
            Z = tmpB.get()
            evac2(lambda hf, ps, sl: fw.act(Z[:, sl], ps, AF.Silu), accz); yield
            accg = yield from proj_g(ws_main, ("in", l, ti, base + 4))
            evac2(lambda hf, ps, sl: fw.act(gateT[:, sl], ps, AF.Sigmoid), accg); yield
            fw.tt("pool", gateT[:, :], Z[:, 0:TT], gateT[:, :], ALU.mult); yield

        def prep_g(l, ti, h):
            if h % 4 == 0:
                yield from pool_phase_g(l, ti, h // 4)
            yield from head_prep_g(l, ti, h)

        def mixer(l, ti):
            rmsnorm_to_hT(l, 0, ti, O_NMIX)
            checkpoint("rms")
            token_scalars()
            checkpoint("tsc")
            drain(prep_g(l, ti, 0))
            checkpoint("qkv0")
            for h in range(NH):
                gdn_head(l, h, h, prep_g(l, ti, h + 1) if h + 1 < NH else None)
                checkpoint("head%d" % h)
            for j in range(16):
                acc = proj(ws_main, ("out", l, ti, j), rhs_tiles=slots, wide=True)
                xb = xbuf.get()
                fw.dma("sp", xb[:, :], x_src(l, 0, ti, j), xb.tok)
                evac2(lambda hf, ps, sl: fw.tt("dve", xb[:, sl], ps, xb[:, sl], ALU.add), acc)
                fw.dma("sp", y_dst(ti, j), xb[:, :], st_rr.get())

        def ffn(l, ti):
            rmsnorm_to_hT(l, 1, ti, O_NFFN)
            for hh in range(NQ):
                for ff in range(FH):
                    f = hh * FH + ff
                    accg = proj(ws_main, ("up", l, ti, 2 * f), wide=True)
                    R = tmpA.get()
                    A = tmpB.get()
                    fw.copy("act", R[:, 0:2], c_ffn[:, f, :])
                    wf = lambda jtap: sm(O_CFW + f * 3 + jtap)
                    evac2(lambda hf, ps, sl: fw.copy("act", R[:, 2 + sl.start:2 + sl.stop], ps), accg)
                    evac2(lambda hf, ps, sl: fw.act(A[:, sl], ps, AF.Identity, scale=wf(2), bias=sm(O_CFB + f)), accg)
                    fw.copy("act", c_ffn[:, f, :], R[:, TT:TT + 2])
                    fw.stt("dve", A[:, 0:TT], R[:, 1:TT + 1], wf(1), A[:, 0:TT], ALU.mult, ALU.add)
                    fw.stt("dve", A[:, 0:TT], R[:, 0:TT], wf(0), A[:, 0:TT], ALU.mult, ALU.add)
                    Gl = tmpB.get()
                    fw.act(Gl[:, 0:TT], A[:, 0:TT], AF.Gelu)
                    accu = proj(ws_main, ("up", l, ti, 2 * f + 1), wide=True)
                    evac2(lambda hf, ps, sl: fw.tt("dve", slots[ff][:, sl], ps, Gl[:, sl], ALU.mult), accu)
                for j in range(16):
                    acc = proj(ws_dn, ("dn", l, ti, hh, j), nk=FH, rhs_tiles=slots, wide=True)
                    xb = xbuf.get()
                    fw.dma("sp", xb[:, :], x_src(l, 1, ti, j), xb.tok)
                    evac2(lambda hf, ps, sl: fw.tt("dve", xb[:, sl], ps, xb[:, sl], ALU.add), acc)
                    fw.dma("sp", y_dst(ti, j), xb[:, :], st_rr.get())

        def final_norm(ti):
            acc = acc_rr.get()
            for kc in range(KC):
                xb = xbuf.get()
                fw.dma("sp", xb[:, :], x_src(L, 0, ti, kc), xb.tok)
                sq = sqb.get()
                fw.act(sq[:, :], xb[:, :], AF.Square)
                for hf in range(2):
                    fw.mm(acc[hf][:, :], onesb, sq[:, hf * 512:(hf + 1) * 512], start=(kc == 0), stop=(kc == KC - 1))
            for hf in range(2):
                fw.act(rstd[:, hf * 512:(hf + 1) * 512], acc[hf][:, :], AF.Ln, scale=1.0 / D, bias=eps_c)
                fw.act(rstd[:, hf * 512:(hf + 1) * 512], rstd[:, hf * 512:(hf + 1) * 512], AF.Exp, scale=-0.5)
            for kc in range(KC):
                xb = xbuf.get()
                fw.dma("sp", xb[:, :], x_src(L, 0, ti, kc), xb.tok)
                fw.stt("dve", xb[:, :], xb[:, :], cf[:, C_NFIN + kc:C_NFIN + kc + 1], rstd[:, :],
                       ALU.mult, ALU.mult)
                fw.dma("sp", y_dst(ti, kc), xb[:, :], st_rr.get())

        def dump(name, v, width):
            dd = dr("dbg_" + name, [128, width], kind="ExternalOutput")
            tk = fw.dtok("dbg_" + name)
            fw.dma("pool", V(dd.ap()[:, :], [Tok()]), v, tk)
            dbg_list.append(tk)

        build_program.dump = dump
        build_program.env = locals()
        try:
          for l in range(L):
            load_small(l)
            fw.memset("dve", Sst[:, :, :], 0.0)
            for h in range(NH):
                Stok[h].w = Sst.tok.w
            fw.memset("dve", c_qkv[:, :, :], 0.0)
            fw.memset("dve", c_pool[:, :, :], 0.0)
            fw.memset("dve", c_ffn[:, :, :], 0.0)
            for ti in range(NT):
                mixer(l, ti)
                checkpoint("mixer")
                ffn(l, ti)
                checkpoint("ffn")
          for ti in range(NT):
            final_norm(ti)
        except StopBuild:
            if build_program.on_stop:
                build_program.on_stop(locals())
        for t in fw.dtoks:
            if t.dcnt:
                fw.eng["sp"].wait_ge(t.dsem, t.dcnt)
        for e in ("pe", "act", "dve", "pool"):
            if fw.cnt[e]:
                fw.eng["sp"].wait_ge(fw.sem[e], fw.cnt[e])
        print("program: ins=%d waits=%d" % (fw.nins, fw.nwait), {e: fw.cnt[e] for e in fw.cnt})
    return nc


build_program.on_stop = None
_CACHE = {}


def run(inputs, L, S, ncores, stop_at=None, ret_all=False):
    B = inputs["x"].shape[0]
    pw = prep_weights(inputs, L)
    key = (L, S, stop_at)
    if key not in _CACHE:
        _CACHE[key] = build_program(L, S, stop_at=stop_at)
    nc = _CACHE[key]
    in_maps = []
    for c in range(ncores):
        b = c % B
        m = {"xT": np.ascontiguousarray(inputs["x"][b, :S].T)}
        m.update(pw)
        in_maps.append(m)
    res = run_bass_kernel_spmd(nc, in_maps, core_ids=list(range(ncores)))
    if ret_all:
        return res.results
    out = np.stack([np.ascontiguousarray(res.results[b]["yT"].T) for b in range(B)], axis=0)
    return out.astype(np.float32)


def kernel(**inputs):
    inputs = {k: np.asarray(v, dtype=np.float32) for k, v in inputs.items()}
    return run(inputs, 4, 4096, 8)
```
